# Optimizing a Trainium2 kernel written in Bass

```python
import jax, jax.numpy as jnp
from jax import lax
import numpy as np

D_MODEL = 1024
BATCH = 8
SEQ = 2048
DEPTH = 2
DEC_BATCH = 128
DEC_SEQ = 1
PAST_LEN = 8192
PAGE_SIZE = 128

D_POOL = D_MODEL // 4
POOL_WINDOWS = (2, 4, 8, 16)
N_POOL_GROUPS = 4
POOL_GC = D_POOL // N_POOL_GROUPS
POOL_BUF = 15
D_SGU = D_MODEL // 4
CHUNK = 128
N_SGU_GROUPS = 4
SGU_GC = D_SGU // N_SGU_GROUPS
HEAD_DIM = 64
N_HEADS = (D_MODEL // 2) // HEAD_DIM
N_KV_HEADS = 2
Q_PER_KV = N_HEADS // N_KV_HEADS
D_ATTN = N_HEADS * HEAD_DIM
D_KV = N_KV_HEADS * HEAD_DIM
WINDOW = 128
BLOCK = 128
ROPE_THETA = 10000.0
N_BRANCHES = 3
SPLIT_SIZES = (D_POOL, D_POOL, D_SGU, D_SGU, D_SGU, D_ATTN, D_KV, D_KV, D_ATTN, N_BRANCHES * D_MODEL)
D_IN = 2 * D_POOL + 3 * D_SGU + 2 * D_ATTN + 2 * D_KV + N_BRANCHES * D_MODEL
V_OFF = 2 * D_POOL + 3 * D_SGU + D_ATTN + D_KV
ALPHA = (2.0 * DEPTH) ** 0.25
BETA = (8.0 * DEPTH) ** -0.25
LN_EPS = 1e-5
NEG_INF = -1e30

kernel_name = "hybrid_pool_sgu_swa_decoder_step"


def layer_norm(x, g, b):
    xf = x.astype(jnp.float32)
    mu = xf.mean(-1, keepdims=True)
    var = jnp.square(xf - mu).mean(-1, keepdims=True)
    y = (xf - mu) * lax.rsqrt(var + LN_EPS)
    return (y * g.astype(jnp.float32) + b.astype(jnp.float32)).astype(x.dtype)


def split_in(h):
    offs = np.cumsum(SPLIT_SIZES)[:-1].tolist()
    return jnp.split(h, offs, axis=-1)


def rope(x, pos):
    half = HEAD_DIM // 2
    inv = ROPE_THETA ** (-jnp.arange(half, dtype=jnp.float32) / half)
    ang = pos.astype(jnp.float32)[:, None] * inv[None, :]
    cos = jnp.cos(ang)[None, :, None, :]
    sin = jnp.sin(ang)[None, :, None, :]
    xf = x.astype(jnp.float32)
    x1, x2 = xf[..., :half], xf[..., half:]
    return jnp.concatenate([x1 * cos - x2 * sin, x2 * cos + x1 * sin], axis=-1).astype(x.dtype)


def pool_mix(xa_ext, n_hist, pos, pool_w, pool_scale):
    B, L, _ = xa_ext.shape
    T = L - n_hist
    xg = xa_ext.astype(jnp.float32).reshape(B, L, N_POOL_GROUPS, POOL_GC)
    cs = jnp.cumsum(xg, axis=1)
    outs = []
    for g, w in enumerate(POOL_WINDOWS):
        csg = cs[:, :, g]
        shifted = jnp.pad(csg, ((0, 0), (w, 0), (0, 0)))[:, :L]
        win_sum = (csg - shifted)[:, n_hist:]
        cnt = jnp.minimum(pos + 1, w).astype(jnp.float32)
        outs.append(win_sum / cnt[None, :, None] - xg[:, n_hist:, g])
    pooled = jnp.stack(outs, axis=2)
    mixed = jnp.einsum('btgc,gcd->btgd', pooled, pool_w.astype(jnp.float32)).reshape(B, T, D_POOL)
    return (mixed * pool_scale.astype(jnp.float32)).astype(xa_ext.dtype)


def sgu_spatial(v_chunks, sgu_w, sgu_b):
    tc = v_chunks.shape[2]
    mask = jnp.tril(jnp.ones((tc, tc), dtype=bool))
    w = jnp.where(mask[None], sgu_w[:, :tc, :tc], 0.0).astype(v_chunks.dtype)
    s = jnp.einsum('gts,bnsgc->bntgc', w, v_chunks)
    return s + jnp.transpose(sgu_b[:, :tc])[None, None, :, :, None].astype(v_chunks.dtype)


def sink_softmax(scores, allowed, sinks):
    scores = jnp.where(allowed, scores, NEG_INF)
    sink = sinks.astype(jnp.float32).reshape(N_KV_HEADS, Q_PER_KV, 1, 1)
    m = jnp.maximum(scores.max(-1, keepdims=True), sink)
    p = jnp.exp(scores - m)
    return p / (p.sum(-1, keepdims=True) + jnp.exp(sink - m))


def banded_window_attention(q, k, v, sinks):
    B, L = q.shape[:2]
    NB = L // BLOCK
    qb = q.reshape(B, NB, BLOCK, N_KV_HEADS, Q_PER_KV, HEAD_DIM)
    kb = k.reshape(B, NB, BLOCK, N_KV_HEADS, HEAD_DIM)
    vb = v.reshape(B, NB, BLOCK, N_KV_HEADS, HEAD_DIM)
    pad = ((0, 0), (1, 0), (0, 0), (0, 0), (0, 0))
    keys = jnp.concatenate([jnp.pad(kb, pad)[:, :NB], kb], axis=2)
    vals = jnp.concatenate([jnp.pad(vb, pad)[:, :NB], vb], axis=2)
    scores = jnp.einsum('bnqkgd,bnskd->bnkgqs', qb, keys).astype(jnp.float32) * (HEAD_DIM ** -0.5)
    blk = jnp.arange(NB, dtype=jnp.int32)[:, None] * BLOCK
    q_pos = blk + jnp.arange(BLOCK, dtype=jnp.int32)[None, :]
    k_pos = blk - BLOCK + jnp.arange(2 * BLOCK, dtype=jnp.int32)[None, :]
    diff = q_pos[:, :, None] - k_pos[:, None, :]
    allowed = (diff >= 0) & (diff <= WINDOW) & (k_pos[:, None, :] >= 0)
    probs = sink_softmax(scores, allowed[None, :, None, None], sinks)
    out = jnp.einsum('bnkgqs,bnskd->bnqkgd', probs.astype(vals.dtype), vals)
    return out.reshape(B, L, D_ATTN)


def window_decode_attention(q, keys, vals, pos, sinks):
    Bd, T = q.shape[:2]
    qg = q.reshape(Bd, T, N_KV_HEADS, Q_PER_KV, HEAD_DIM)
    scores = jnp.einsum('btkgd,bskd->bkgts', qg, keys).astype(jnp.float32) * (HEAD_DIM ** -0.5)
    k_pos = jnp.concatenate([PAST_LEN - WINDOW + jnp.arange(WINDOW, dtype=jnp.int32), pos])
    diff = pos[:, None] - k_pos[None, :]
    allowed = (diff >= 0) & (diff <= WINDOW)
    probs = sink_softmax(scores, allowed, sinks)
    out = jnp.einsum('bkgts,bskd->btkgd', probs.astype(vals.dtype), vals)
    return out.reshape(Bd, T, D_ATTN)


def merge_and_norm(x, ya, za, yb, zb, yc, zc, gates, b_gate, w_pa, w_pb, w_pc, w_out, ln_g, ln_b):
    lead = gates.shape[:-1]
    g = jax.nn.sigmoid((gates.reshape(lead + (N_BRANCHES, D_MODEL)) + b_gate).astype(jnp.float32)).astype(x.dtype)
    oa = jnp.einsum('blc,cd->bld', ya * jax.nn.silu(za), w_pa)
    ob = jnp.einsum('blc,cd->bld', yb * jax.nn.silu(zb), w_pb)
    oc = jnp.einsum('blc,cd->bld', yc * jax.nn.silu(zc), w_pc)
    merged = g[..., 0, :] * oa + g[..., 1, :] * ob + g[..., 2, :] * oc
    out = jnp.einsum('bld,de->ble', merged, w_out)
    return layer_norm(ALPHA * x + out, ln_g, ln_b)


def layer_prompt(x, w_in, b_gate, pool_w, pool_scale, sgu_ln_g, sgu_ln_b, sgu_w, sgu_b, attn_sinks,
                 w_pa, w_pb, w_pc, w_out, ln_g, ln_b):
    B, L, _ = x.shape
    pos = jnp.arange(L, dtype=jnp.int32)
    xa, za, u, v, zb, q, k, vv, zc, gates = split_in(jnp.einsum('bld,de->ble', x, w_in))
    ya = pool_mix(xa, 0, pos, pool_w, pool_scale)
    vn = layer_norm(v, sgu_ln_g, sgu_ln_b)
    yb = u * sgu_spatial(vn.reshape(B, L // CHUNK, CHUNK, N_SGU_GROUPS, SGU_GC), sgu_w, sgu_b).reshape(B, L, D_SGU)
    qr = rope(q.reshape(B, L, N_HEADS, HEAD_DIM), pos)
    kr = rope(k.reshape(B, L, N_KV_HEADS, HEAD_DIM), pos)
    vr = vv.reshape(B, L, N_KV_HEADS, HEAD_DIM)
    yc = banded_window_attention(qr, kr, vr, attn_sinks)
    y = merge_and_norm(x, ya, za, yb, zb, yc, zc, gates, b_gate, w_pa, w_pb, w_pc, w_out, ln_g, ln_b)
    return y, xa[:, L - POOL_BUF:], kr[:, L - WINDOW:], vr[:, L - WINDOW:]


def layer_sample(x, pool_buf, k_buf, v_buf, w_in, b_gate, pool_w, pool_scale, sgu_ln_g, sgu_ln_b, sgu_w, sgu_b,
                 attn_sinks, w_pa, w_pb, w_pc, w_out, ln_g, ln_b):
    Bd, T, _ = x.shape
    pos = PAST_LEN + jnp.arange(T, dtype=jnp.int32)
    xa, za, u, v, zb, q, k, vv, zc, gates = split_in(jnp.einsum('bld,de->ble', x, w_in))
    xa_ext = jnp.concatenate([pool_buf.astype(xa.dtype), xa], axis=1)
    ya = pool_mix(xa_ext, POOL_BUF, pos, pool_w, pool_scale)
    vn = layer_norm(v, sgu_ln_g, sgu_ln_b)
    yb = u * sgu_spatial(vn.reshape(Bd, 1, T, N_SGU_GROUPS, SGU_GC), sgu_w, sgu_b).reshape(Bd, T, D_SGU)
    qr = rope(q.reshape(Bd, T, N_HEADS, HEAD_DIM), pos)
    kr = rope(k.reshape(Bd, T, N_KV_HEADS, HEAD_DIM), pos)
    vr = vv.reshape(Bd, T, N_KV_HEADS, HEAD_DIM)
    keys = jnp.concatenate([k_buf.astype(kr.dtype), kr], axis=1)
    vals = jnp.concatenate([v_buf.astype(vr.dtype), vr], axis=1)
    yc = window_decode_attention(qr, keys, vals, pos, attn_sinks)
    y = merge_and_norm(x, ya, za, yb, zb, yc, zc, gates, b_gate, w_pa, w_pb, w_pc, w_out, ln_g, ln_b)
    return y, xa_ext[:, T:], keys[:, T:], vals[:, T:], vn


def setup_inputs(seed: int = 0) -> dict:
    key = jax.random.key(seed)
    ks = jax.random.split(key, 20)
    f32 = jnp.float32
    nrm = lambda k, s: jax.random.normal(k, s, dtype=f32)
    w_in = nrm(ks[5], (DEPTH, D_MODEL, D_IN)) * D_MODEL ** -0.5
    w_in = w_in.at[:, :, V_OFF:V_OFF + D_KV].multiply(BETA)
    return {
        "x_prompt": nrm(ks[0], (BATCH, SEQ, D_MODEL)),
        "x_sample": nrm(ks[1], (DEC_BATCH, DEC_SEQ, D_MODEL)),
        "state_pool": nrm(ks[2], (DEPTH, DEC_BATCH, POOL_BUF, D_POOL)),
        "cache_k_win": nrm(ks[3], (DEPTH, DEC_BATCH, WINDOW, N_KV_HEADS, HEAD_DIM)),
        "cache_v_win": nrm(ks[4], (DEPTH, DEC_BATCH, WINDOW, N_KV_HEADS, HEAD_DIM)),
        "w_in": w_in,
        "b_gate": 0.02 * nrm(ks[6], (DEPTH, N_BRANCHES, D_MODEL)),
        "pool_w": nrm(ks[7], (DEPTH, N_POOL_GROUPS, POOL_GC, POOL_GC)) * POOL_GC ** -0.5,
        "pool_scale": 1.0 + 0.1 * nrm(ks[8], (DEPTH, D_POOL)),
        "sgu_ln_g": 1.0 + 0.1 * nrm(ks[9], (DEPTH, D_SGU)),
        "sgu_ln_b": 0.02 * nrm(ks[10], (DEPTH, D_SGU)),
        "sgu_w": nrm(ks[11], (DEPTH, N_SGU_GROUPS, CHUNK, CHUNK)) * CHUNK ** -0.5,
        "sgu_b": 1.0 + 0.1 * nrm(ks[12], (DEPTH, N_SGU_GROUPS, CHUNK)),
        "attn_sinks": 0.5 * nrm(ks[13], (DEPTH, N_HEADS)),
        "w_proj_a": nrm(ks[14], (DEPTH, D_POOL, D_MODEL)) * D_POOL ** -0.5 * BETA,
        "w_proj_b": nrm(ks[15], (DEPTH, D_SGU, D_MODEL)) * D_SGU ** -0.5 * BETA,
        "w_proj_c": nrm(ks[16], (DEPTH, D_ATTN, D_MODEL)) * D_ATTN ** -0.5 * BETA,
        "w_out": nrm(ks[17], (DEPTH, D_MODEL, D_MODEL)) * D_MODEL ** -0.5 * BETA,
        "ln_g": 1.0 + 0.1 * nrm(ks[18], (DEPTH, D_MODEL)),
        "ln_b": 0.02 * nrm(ks[19], (DEPTH, D_MODEL)),
    }


def reference(x_prompt, x_sample, state_pool, cache_k_win, cache_v_win, w_in, b_gate, pool_w, pool_scale,
              sgu_ln_g, sgu_ln_b, sgu_w, sgu_b, attn_sinks, w_proj_a, w_proj_b, w_proj_c, w_out, ln_g, ln_b):
    y_p, y_s = x_prompt, x_sample
    pool_p, kp, vp, pool_s, ksm, vsm, chunk_v = [], [], [], [], [], [], []
    for l in range(DEPTH):
        lw = (w_in[l], b_gate[l], pool_w[l], pool_scale[l], sgu_ln_g[l], sgu_ln_b[l], sgu_w[l], sgu_b[l],
              attn_sinks[l], w_proj_a[l], w_proj_b[l], w_proj_c[l], w_out[l], ln_g[l], ln_b[l])
        y_p, sp, kpl, vpl = layer_prompt(y_p, *lw)
        y_s, ss, ksl, vsl, cvl = layer_sample(y_s, state_pool[l], cache_k_win[l], cache_v_win[l], *lw)
        pool_p.append(sp); kp.append(kpl); vp.append(vpl)
        pool_s.append(ss); ksm.append(ksl); vsm.append(vsl); chunk_v.append(cvl)
    new_state_pool_prompt = jnp.stack(pool_p)
    new_cache_k_win_prompt = jnp.stack(kp)
    new_cache_v_win_prompt = jnp.stack(vp)
    new_state_pool_sample = jnp.stack(pool_s)
    new_cache_k_win_sample = jnp.stack(ksm)
    new_cache_v_win_sample = jnp.stack(vsm)
    new_state_chunk_v_sample = jnp.stack(chunk_v)
    return (y_p, y_s, new_state_pool_prompt, new_cache_k_win_prompt, new_cache_v_win_prompt,
            new_state_pool_sample, new_cache_k_win_sample, new_cache_v_win_sample, new_state_chunk_v_sample)
```

```python
from contextlib import ExitStack
import numpy as np
import concourse.bass as bass
import concourse.mybir as mybir
from concourse.bass_utils import run_bass_kernel_spmd

F32 = mybir.dt.float32
BF16 = mybir.dt.bfloat16
AF = mybir.ActivationFunctionType
ALU = mybir.AluOpType
AX = mybir.AxisListType

NCORES = 8
D = 1024
SEQ = 2048
NT = SEQ // 128
G = 8
NG = NT // G
NS = 16
DEPTH = 2
D_IN = 5632
ALPHA = (2.0 * DEPTH) ** 0.25
LN_EPS = 1e-5
PAST_LEN = 8192
POOL_WINDOWS = (2, 4, 8, 16)


class Buf:
    __slots__ = ("name", "last_w", "readers", "dsem", "dcount", "exclusive")

    def __init__(self, name, exclusive=False):
        self.name = name
        self.exclusive = exclusive
        self.last_w = None
        self.readers = []
        self.dsem = None
        self.dcount = 0


class Sched:
    ENGS = ("pe", "act", "dve", "pool", "sp")

    def __init__(self, nc, stack):
        self.nc = nc
        self.stack = stack
        self.sems = {}
        self.ops = {e: [] for e in self.ENGS}
        self.count = {e: 0 for e in self.ENGS}
        self.waited = {e: {} for e in self.ENGS}
        self.nsem = 0
        for e in ("pe", "act", "dve", "pool"):
            self.sems[e] = self._sem("eng_" + e)
        self.final_events = []
        self.dma_keys = []
        self.nops = 0
        self.maxops = None
        self.trace = []

    def _sem(self, name):
        self.nsem += 1
        return self.stack.enter_context(self.nc.semaphore(name))

    def buf(self, name):
        return Buf(name)

    def _deps(self, eng, reads, writes):
        waits = {}

        def need(ev):
            if ev is None:
                return
            sid, val, weng = ev
            if weng == eng and eng == "pe":
                return
            if self.waited[eng].get(sid, 0) >= val:
                return
            if waits.get(sid, (0,))[0] < val:
                waits[sid] = (val,)

        for b in reads:
            need(b.last_w)
            if b.exclusive:
                for ev in b.readers:
                    if ev[2] != eng:
                        need(ev)
        for b in writes:
            need(b.last_w)
            for ev in b.readers:
                need(ev)
        out = []
        for sid, (val,) in waits.items():
            self.waited[eng][sid] = val
            out.append((sid, val))
        return out

    def _commit(self, ev, reads, writes):
        for b in writes:
            b.last_w = ev
            b.readers = []
        for b in reads:
            if b not in writes:
                b.readers.append(ev)

    def _skip(self):
        self.nops += 1
        if self.maxops is not None and self.nops > self.maxops:
            return True
        if self.maxops is not None:
            import inspect
            fr = inspect.stack()
            self.trace.append((self.nops, [f.lineno for f in fr[2:5]]))
        return False

    def op(self, eng, fn, reads=(), writes=()):
        if self._skip():
            return None
        waits = self._deps(eng, reads, writes)
        self.count[eng] += 1
        ev = (eng, self.count[eng], eng)
        self.waited[eng][eng] = max(self.waited[eng].get(eng, 0), 0)
        self.ops[eng].append((waits, fn, (eng, 1)))
        self._commit(ev, reads, writes)
        return ev

    def dma(self, q, out_ap, in_ap, key, reads=(), writes=(), final=False, slow=False):
        if self._skip():
            return None
        if key.dsem is None:
            key.dsem = "dma_" + key.name
            self.dma_keys.append(key)
            self.sems[key.dsem] = self._sem(key.dsem)
        waits = self._deps(q, reads, writes)
        key.dcount += 16
        ev = (key.dsem, key.dcount, "dma")

        def fn(e, out_ap=out_ap, in_ap=in_ap, slow=slow):
            if slow:
                return e.dma_start(out=out_ap, in_=in_ap, allow_slow_non_contiguous=True)
            return e.dma_start(out=out_ap, in_=in_ap)

        self.ops[q].append((waits, fn, (key.dsem, 16)))
        self._commit(ev, reads, writes)
        if final:
            self.final_events.append(ev)
        return ev

    def emit(self):
        nc = self.nc
        fin = {}
        for key in self.dma_keys:
            fin[key.dsem] = key.dcount
        handles = {"pe": "tensor", "act": "scalar", "dve": "vector", "pool": "gpsimd", "sp": "sync"}
        with nc.Block() as block:
            for eng in self.ENGS:
                ops = self.ops[eng]
                if not ops and eng != "sp":
                    continue

                def body(e, ops=ops, eng=eng):
                    for waits, fn, inc in ops:
                        for sid, val in waits:
                            e.wait_ge(self.sems[sid], val)
                        ins = fn(e)
                        ins.then_inc(self.sems[inc[0]], inc[1])
                    if eng == "sp":
                        for sid, val in fin.items():
                            e.wait_ge(self.sems[sid], val)

                getattr(block, handles[eng])(body)


def build_program(dbg=None):
    dbg = dbg or {}
    nc = bass.Bass("TRN2", target_bir_lowering=False)
    stack = ExitStack()
    S = Sched(nc, stack)
    S.maxops = dbg.get("maxops")

    def din(name, shape):
        return nc.dram_tensor(name, list(shape), F32, kind="ExternalInput").ap()

    def dout(name, shape):
        return nc.dram_tensor(name, list(shape), F32, kind="ExternalOutput").ap()

    xp = din("xp", [SEQ, D])
    xs = din("xs", [NS, D])
    spool = din("spool", [DEPTH, NS * 15, 256])
    ck = din("ck", [DEPTH, NS, 128, 128])
    cv = din("cv", [DEPTH, NS, 128, 128])
    w_in = din("w_in", [DEPTH, D, D_IN])
    b_gate = din("b_gate", [DEPTH, 3 * D])
    pool_w = din("pool_w", [DEPTH, 4, 64, 64])
    pool_scale = din("pool_scale", [DEPTH, 256])
    sgu_ln_g = din("sgu_ln_g", [DEPTH, 256])
    sgu_ln_b = din("sgu_ln_b", [DEPTH, 256])
    sgu_w = din("sgu_w", [DEPTH, 4, 128, 128])
    sgu_b = din("sgu_b", [DEPTH, 4, 128])
    sinks = din("sinks", [DEPTH, 8])
    w_pa = din("w_pa", [DEPTH, 256, D])
    w_pb = din("w_pb", [DEPTH, 256, D])
    w_pc = din("w_pc", [DEPTH, 512, D])
    w_out = din("w_out", [DEPTH, D, D])
    ln_g = din("ln_g", [DEPTH, D])
    ln_b = din("ln_b", [DEPTH, D])
    c_cs = din("c_cs", [SEQ, 64])
    c_cs_s = din("c_cs_s", [NS, 64])
    c_poolP = din("c_poolP", [3, 4, 128, 128])
    c_mask = din("c_mask", [2, 128, 128])
    c_tril = din("c_tril", [128, 128])
    c_ident = din("c_ident", [128, 128])
    c_sel = din("c_sel", [4, 2, 120, 16])
    c_dmask = din("c_dmask", [NS, NS * 8])

    y_p = dout("y_p", [SEQ, D])
    y_s = dout("y_s", [NS, D])
    o_pool_p = dout("o_pool_p", [DEPTH, 15, 256])
    o_k_p = dout("o_k_p", [DEPTH, 128, 128])
    o_v_p = dout("o_v_p", [DEPTH, 128, 128])
    o_pool_s = dout("o_pool_s", [DEPTH, NS, 15, 256])
    o_k_s = dout("o_k_s", [DEPTH, NS, 128, 128])
    o_v_s = dout("o_v_s", [DEPTH, NS, 128, 128])
    o_cv_s = dout("o_cv_s", [DEPTH, NS, 256])

    def sb(name, shape, dt=F32):
        return stack.enter_context(nc.sbuf_tensor(name, list(shape), dt))

    Wbuf = sb("Wbuf", [128, 8, 5120], BF16)
    W1 = Wbuf[:, :, 0:2560]
    Wg = Wbuf[:, :, 0:3072]
    Wp = Wbuf[:, :, 3072:4096]
    Wo = Wbuf[:, :, 4096:5120]
    Bg = sb("Bg", [128, G + 1, 8, 128], BF16)
    x_res = sb("x_res", [128, G, D])
    xs_res = sb("xs_res", [NS, D])
    identf = sb("identf", [128, 128])
    identb = sb("identb", [128, 128], BF16)
    poolP = sb("poolP", [128, 12, 128], BF16)
    maskb = sb("maskb", [128, 2, 128], BF16)
    cs_t = sb("cs_t", [128, 64])
    cs_s = sb("cs_s", [NS, 64])
    ones2 = sb("ones2", [2, 128], BF16)
    onesb = sb("onesb", [128, 128], BF16)
    chalf = sb("chalf", [128, 1])
    selb = sb("selb", [120, 8, 16], BF16)
    dmask = sb("dmask", [NS, NS * 8])
    sguWT = sb("sguWT", [128, 4, 128], BF16)
    sgub = sb("sgub", [128, 256])
    slng = sb("slng", [128, 256])
    slnb = sb("slnb", [128, 256])
    bdw = sb("bdw", [128, 2, 128], BF16)
    bg24 = sb("bg24", [24, 128])
    bgh = sb("bgh", [128, 24])
    lng = sb("lng", [128, D])
    lnb = sb("lnb", [128, D])
    esink = sb("esink", [128, 8])
    esink_h = sb("esink_h", [128, 8])
    xT2 = sb("xT2", [128, 2, 8, 128], BF16)
    xT = xT2[:, 0, :, :]
    sz = sb("sz", [128, 1024])
    pooled_bf = sb("pooled_bf", [128, 256], BF16)
    pooledT = sb("pooledT", [128, 2, 128], BF16)
    B_tm = sb("B_tm", [128, 1024], BF16)
    st6 = sb("st6", [128, 12])
    mv = sb("mv", [128, 8])
    vn_bf = sb("vn_bf", [128, 256], BF16)
    q_bf = sb("q_bf", [128, 512], BF16)
    kdup = sb("kdup", [128, 2, 128], BF16)
    qT = sb("qT", [128, 4, 128], BF16)
    den = sb("den", [128, 8])
    rden = sb("rden", [128, 8])
    gsb = sb("gsb", [128, 1024])
    acc = sb("acc", [128, 1024])
    tmp = sb("tmp", [128, 1024])
    PTm = sb("PTm", [128, 2, 1024], BF16)
    PT = PTm
    m_bf = PTm[:, 0, :]
    mT = PTm[:, 1, :].rearrange("p (k t) -> p k t", k=8)
    qk = acc[:, 0:640]
    vtmp = acc[:, 640:896]
    rb = gsb[:, 0:640]
    uz = gsb[:, 640:896]
    yc = tmp[:, 512:1024]
    xa_f = tmp[:, 0:256]
    vr_f = tmp[:, 256:384]
    sguW = sb("sguW_s", [128, 4, 128])
    tril = sb("tril_s", [128, 128])
    sgb4 = sb("sgb4_s", [4, 128])
    bdwf = sb("bdwf_s", [128, 2, 128])
    pscale = sb("pscale_s", [128, 256])

    KTs = sb("KTs", [128, NS, 128], BF16)
    Vda = sb("Vda", [128, NS // 2, 2, 128], BF16)
    Vdb = Bg[:, 0:2, :, :].rearrange("p a c f -> p (a c) f").rearrange("p (b kv) f -> p b kv f", kv=2)
    Ks = PTm[:, :, :].rearrange("p a (b f) -> p (a b) f", f=128)
    hist = sb("hist", [120, 2, 256], BF16)
    Qexp = sb("Qexp", [NS, 8, 128], BF16)
    Qblk = sb("Qblk", [128, NS, 8], BF16)
    PTs = sb("PTs", [128, 128], BF16)
    Pself = sb("Pself", [NS, 128], BF16)
    Pself_f = tmp[0:NS, 256:384]
    vdn = sb("vdn", [NS, 2, 128], BF16)
    kTn = sb("kTn", [128, NS], BF16)
    sw00 = sb("sw00", [NS, 4])
    sb0 = sb("sb0", [NS, 4])
    rden_s = tmp[:, 0:128]
    Rn = tmp[:, 128:256]

    ps = stack.enter_context(nc.psum_tensor("ps", [128, 8, 512], F32))

    def psb(bank):
        return ps[:, bank, :].bitcast(BF16)

    B = {}

    def bf(name):
        if name not in B:
            B[name] = S.buf(name)
        return B[name]

    PSB = [bf("ps%d" % i) for i in range(8)]
    for _b in PSB:
        _b.exclusive = True

    S.dma("sp", identf[:], c_ident[:, :], bf("identf"), writes=[bf("identf")])
    S.op("dve", lambda e: e.tensor_copy(out=identb[:], in_=identf[:]), reads=[bf("identf")], writes=[bf("identb")])
    S.dma("pool", poolP[:], c_poolP.rearrange("v g s t -> s (v g) t"), bf("poolP"), writes=[bf("poolP")])
    S.dma("pool", maskb[:], c_mask.rearrange("v s t -> s v t"), bf("maskb"), writes=[bf("maskb")])
    S.dma("sp", cs_s[:], c_cs_s[:, :], bf("cs_s"), writes=[bf("cs_s")])
    S.dma("sp", tril[:], c_tril[:, :], bf("stg_tril"), writes=[bf("stg_tril")])
    S.dma("pool", selb[:], c_sel.rearrange("g c r b -> r (g c) b"), bf("selb"), writes=[bf("selb")])
    S.dma("sp", dmask[:], c_dmask[:, :], bf("dmask"), writes=[bf("dmask")])
    S.op("dve", lambda e: e.memset(ones2[:], 1.0), writes=[bf("ones2")])
    S.op("dve", lambda e: e.memset(onesb[:], 1.0), writes=[bf("onesb")])
    S.op("dve", lambda e: e.memset(chalf[:], -0.5), writes=[bf("chalf")])
    S.op("dve", lambda e: e.memset(Qexp[:], 0.0), writes=[bf("Qexp")])

    def WB(*idx):
        return [bf("WB%d" % i) for i in idx]

    def w1_block(l, b):
        wv = w_in[l].rearrange("(k p) n -> p k n", p=128)
        if b == 0:
            S.dma("pool", Wbuf[:, :, 0:512], wv[:, :, 1280:1792], bf("WB0"), writes=WB(0))
        elif b == 1:
            S.dma("pool", Wbuf[:, :, 512:768], wv[:, :, 1792:2048], bf("WB1"), writes=WB(1))
            S.dma("pool", Wbuf[:, :, 768:1024], wv[:, :, 1024:1280], bf("WB1"), writes=WB(1))
        elif b == 2:
            S.dma("pool", Wbuf[:, :, 1024:1536], wv[:, :, 0:512], bf("WB2"), writes=WB(2))
        elif b == 3:
            S.dma("pool", Wbuf[:, :, 1536:2048], wv[:, :, 512:1024], bf("WB3"), writes=WB(3))
        else:
            S.dma("pool", Wbuf[:, :, 2048:2560], wv[:, :, 2048:2560], bf("WB4"), writes=WB(4))

    def w2_block(l, i):
        wv = w_in[l].rearrange("(k p) n -> p k n", p=128)
        S.dma("pool", Wbuf[:, :, i * 512:(i + 1) * 512], wv[:, :, 2560 + i * 512:2560 + (i + 1) * 512],
              bf("WB%d" % i), writes=WB(i))

    def load_w1(l):
        for b in range(5):
            w1_block(l, b)
        return
        wv = w_in[l].rearrange("(k p) n -> p k n", p=128)
        S.dma("pool", Wbuf[:, :, 0:512], wv[:, :, 1280:1792], bf("WB0"), writes=WB(0))
        S.dma("pool", Wbuf[:, :, 512:768], wv[:, :, 1792:2048], bf("WB1"), writes=WB(1))
        S.dma("pool", Wbuf[:, :, 768:1024], wv[:, :, 1024:1280], bf("WB1"), writes=WB(1))
        S.dma("pool", Wbuf[:, :, 1024:1536], wv[:, :, 0:512], bf("WB2"), writes=WB(2))
        S.dma("pool", Wbuf[:, :, 1536:2048], wv[:, :, 512:1024], bf("WB3"), writes=WB(3))
        S.dma("pool", Wbuf[:, :, 2048:2560], wv[:, :, 2048:2560], bf("WB4"), writes=WB(4))

    def w2_upper(l, i):
        wv = w_in[l].rearrange("(k p) n -> p k n", p=128)
        if i == 0:
            S.dma("pool", Wbuf[:, :, 2560:3072], wv[:, :, 2560 + 2560:2560 + 3072], bf("WB5"), writes=WB(5))
        elif i == 1:
            S.dma("pool", Wp[:, 0:2, :], w_pa[l].rearrange("(k p) n -> p k n", p=128), bf("WB6"), writes=WB(6, 7))
        elif i == 2:
            S.dma("pool", Wp[:, 2:4, :], w_pb[l].rearrange("(k p) n -> p k n", p=128), bf("WB6"), writes=WB(6, 7))
        elif i == 3:
            S.dma("pool", Wp[:, 4:8, :], w_pc[l].rearrange("(k p) n -> p k n", p=128), bf("WB6"), writes=WB(6, 7))
        else:
            S.dma("pool", Wo, w_out[l].rearrange("(k p) n -> p k n", p=128), bf("WB8"), writes=WB(8, 9))

    def load_w2(l, part):
        wv = w_in[l].rearrange("(k p) n -> p k n", p=128)
        if part == 0:
            S.dma("pool", Wbuf[:, :, 2560:3072], wv[:, :, 2560 + 2560:2560 + 3072], bf("WB5"), writes=WB(5))
            S.dma("pool", Wp[:, 0:2, :], w_pa[l].rearrange("(k p) n -> p k n", p=128), bf("WB6"), writes=WB(6, 7))
            S.dma("pool", Wp[:, 2:4, :], w_pb[l].rearrange("(k p) n -> p k n", p=128), bf("WB6"), writes=WB(6, 7))
            S.dma("pool", Wp[:, 4:8, :], w_pc[l].rearrange("(k p) n -> p k n", p=128), bf("WB6"), writes=WB(6, 7))
            S.dma("pool", Wo, w_out[l].rearrange("(k p) n -> p k n", p=128), bf("WB8"), writes=WB(8, 9))
            return
        for i in range(5):
            S.dma("pool", Wbuf[:, :, i * 512:(i + 1) * 512], wv[:, :, 2560 + i * 512:2560 + (i + 1) * 512],
                  bf("WB%d" % i), writes=WB(i))
        return
        for i in range(6):
            S.dma("pool", Wbuf[:, :, i * 512:(i + 1) * 512], wv[:, :, 2560 + i * 512:2560 + (i + 1) * 512],
                  bf("WB%d" % i), writes=WB(i))
            if i == 1:
                S.dma("pool", Wp[:, 0:2, :], w_pa[l].rearrange("(k p) n -> p k n", p=128), bf("WB6"), writes=WB(6, 7))
            elif i == 3:
                S.dma("pool", Wp[:, 2:4, :], w_pb[l].rearrange("(k p) n -> p k n", p=128), bf("WB6"), writes=WB(6, 7))
            elif i == 5:
                S.dma("pool", Wp[:, 4:8, :], w_pc[l].rearrange("(k p) n -> p k n", p=128), bf("WB6"), writes=WB(6, 7))
        S.dma("pool", Wo, w_out[l].rearrange("(k p) n -> p k n", p=128), bf("WB8"), writes=WB(8, 9))

    SW, ST, SP_, SG = bf("stg_sguW"), bf("stg_tril"), bf("stg_pscale"), bf("stg_sgb4")
    SBD = [bf("stg_bdw%d" % g) for g in range(4)]

    def consts_prefetch(l):
        S.dma("sp", sguW[:], sgu_w[l].rearrange("g t s -> t g s"), SW, writes=[SW])
        S.dma("sp", sgb4[:], sgu_b[l], SG, writes=[SG])
        S.dma("sp", pscale[:], pool_scale[l].partition_broadcast(128), SP_, writes=[SP_])
        S.op("dve", lambda e: e.memset(bdwf[:], 0.0), writes=SBD)
        for g in range(4):
            c, j = g // 2, g % 2
            S.dma("sp", bdwf[j * 64:(j + 1) * 64, c, j * 64:(j + 1) * 64], pool_w[l, g], SBD[g], writes=[SBD[g]])
        S.dma("sp", slng[:], sgu_ln_g[l].partition_broadcast(128), bf("slng"), writes=[bf("slng")])
        S.dma("sp", slnb[:], sgu_ln_b[l].partition_broadcast(128), bf("slnb"), writes=[bf("slnb")])
        S.dma("sp", esink_h[:], sinks[l].partition_broadcast(128), bf("esink_h"), writes=[bf("esink_h")])
        S.dma("sp", bg24[:], b_gate[l].rearrange("(n p) -> n p", p=128), bf("bg24"), writes=[bf("bg24")])

    def consts_compute_p1(l):
        S.op("dve", lambda e: e.tensor_tensor(out=sguW[:], in0=sguW[:],
                                              in1=tril[:].unsqueeze(1).broadcast_to([128, 4, 128]), op=ALU.mult),
             reads=[ST, SW], writes=[SW])

        def tr(e):
            for g in range(4):
                ins = e.transpose(ps[:, 7, g * 128:(g + 1) * 128], sguW[:, g, :], identf[:])
            return ins
        S.op("pe", tr, reads=[SW, bf("identf")], writes=[PSB[7]])
        S.op("act", lambda e: e.activation(out=sguWT[:].rearrange("p g t -> p (g t)"), in_=ps[:, 7, :], func=AF.Copy),
             reads=[PSB[7]], writes=[bf("sguWT")])
        S.op("pe", lambda e: e.transpose(ps[:, 6, 0:4], sgb4[:], identf[0:4, 0:4]), reads=[SG, bf("identf")],
             writes=[PSB[6]])
        S.op("dve", lambda e: e.tensor_copy(out=sgub[:].rearrange("p (g c) -> p g c", g=4),
                                            in_=ps[:, 6, 0:4].unsqueeze(2).broadcast_to([128, 4, 64])),
             reads=[PSB[6]], writes=[bf("sgub")])
        S.op("dve", lambda e: e.tensor_tensor(out=bdw[:].rearrange("p c d -> p (c d)"),
                                              in0=bdwf[:].rearrange("p c d -> p (c d)"), in1=pscale[:], op=ALU.mult),
             reads=SBD + [SP_], writes=[bf("bdw")])
        S.op("act", lambda e: e.activation(out=esink_h[:], in_=esink_h[:], func=AF.Exp), reads=[bf("esink_h")],
             writes=[bf("esink_h")])
        S.op("dve", lambda e: e.tensor_copy(out=esink[:].rearrange("p (j kv ci) -> p j kv ci", kv=2, j=2),
                                            in_=esink_h[:].rearrange("p (kv ci j) -> p j kv ci", kv=2, ci=2)),
             reads=[bf("esink_h")], writes=[bf("esink")])

    def consts_p2(l):
        S.op("pe", lambda e: e.transpose(ps[:, 5, 0:24], bg24[:], identf[0:24, 0:24]), reads=[bf("bg24"), bf("identf")],
             writes=[PSB[5]])
        S.op("dve", lambda e: e.tensor_scalar(out=bgh[:], in0=ps[:, 5, 0:24], scalar1=0.5, scalar2=None, op0=ALU.mult),
             reads=[PSB[5]], writes=[bf("bgh")])

    def consts_late(l):
        S.dma("sp", lng[:], ln_g[l].partition_broadcast(128), bf("lng"), writes=[bf("lng")])
        S.dma("sp", lnb[:], ln_b[l].partition_broadcast(128), bf("lnb"), writes=[bf("lnb")])

    def ACT(out, in_, func, R, W, **kw):
        return S.op("act", lambda e: e.activation(out=out, in_=in_, func=func, **kw), R, W)

    def TT(out, in0, in1, op, R, W, eng="dve"):
        return S.op(eng, lambda e: e.tensor_tensor(out=out, in0=in0, in1=in1, op=op), R, W)

    def TS(out, in0, s1, s2, op0, op1, R, W):
        if op1 is None:
            return S.op("dve", lambda e: e.tensor_scalar(out=out, in0=in0, scalar1=s1, scalar2=None, op0=op0), R, W)
        return S.op("dve", lambda e: e.tensor_scalar(out=out, in0=in0, scalar1=s1, scalar2=s2, op0=op0, op1=op1), R, W)

    def STT(out, in0, scalar, in1, op0, op1, R, W):
        return S.op("dve", lambda e: e.scalar_tensor_tensor(out=out, in0=in0, scalar=scalar, in1=in1, op0=op0, op1=op1), R, W)

    def CP(out, in_, R, W):
        return S.op("dve", lambda e: e.tensor_copy(out=out, in_=in_), R, W)

    EVENG = dict(cast="act", xT="act", kdup="dve", V="act", xa="act", pooled="dve", pooledT="act", qT="act", kT="act", BT="act")
    EVENG.update(dbg.get("eveng") or {})

    def EV(key, out, in_, R, W, scale=None):
        if EVENG[key] == "act":
            if scale is None:
                return ACT(out, in_, AF.Copy, R, W)
            return ACT(out, in_, AF.Copy, R, W, scale=scale)
        if scale is None:
            return CP(out, in_, R, W)
        return TS(out, in_, scale, None, ALU.mult, None, R, W)

    def MM(lst, R, W):
        def fn(e):
            for (o, a, b, st, sp) in lst:
                ins = e.matmul(o, lhsT=a, rhs=b, start=st, stop=sp)
            return ins
        return S.op("pe", fn, R, W)

    def TR(lst, R, W):
        def fn(e):
            for (o, a, idn) in lst:
                ins = e.transpose(o, a, idn)
            return ins
        return S.op("pe", fn, R, W)

    def flat(ap3):
        return ap3.rearrange("p a b -> p (a b)")

    def layer_norm_stats(src_chunks, R):
        n = len(src_chunks)
        p = src_chunks[0].shape[0]
        for i, c in enumerate(src_chunks):
            S.op("dve", lambda e, c=c, i=i: e.bn_stats(st6[0:p, i * 6:(i + 1) * 6], c), R, [bf("st6")])
        S.op("dve", lambda e: e.bn_aggr(mv[0:p, 0:2], st6[0:p, 0:6 * n]), [bf("st6")], [bf("mv")])
        TS(mv[0:p, 2:3], mv[0:p, 1:2], LN_EPS, None, ALU.add, None, [bf("mv")], [bf("mv")])
        TT(mv[0:p, 3:4], mv[0:p, 2:3], chalf[0:p, :], ALU.pow, [bf("mv"), bf("chalf")], [bf("mv")], eng="pool")
        STT(mv[0:p, 4:5], mv[0:p, 0:1], -1.0, mv[0:p, 3:4], ALU.mult, ALU.mult, [bf("mv")], [bf("mv")])

    xa_ring = sb("xa_ring", [128, 4, 256], BF16)
    kT_ring = sb("kT_ring", [128, 4, 2, 128], BF16)
    V_ring = sb("V_ring", [128, 6, 2, 65], BF16)
    S.op("dve", lambda e: e.memset(V_ring[:, :, :, 64:65], 1.0), writes=[bf("V%d" % i) for i in range(6)])

    def p1_steps(l, gi, t):
        ti = gi * G + t
        slot = l * 2 + ti % 2
        pslot = l * 2 + 1 - ti % 2
        vslot = l * 3 + ti % 3
        vpslot = l * 3 + (ti - 1) % 3
        has_prev = ti > 0
        last = ti == NT - 1
        XR = bf("xres%d" % t)
        IDB = bf("identb")
        XA, XAP = bf("xa%d" % slot), bf("xa%d" % pslot)
        VS, VP = bf("V%d" % vslot), bf("V%d" % vpslot)
        KS, KP = bf("kT%d" % slot), bf("kT%d" % pslot)
        BGB = bf("Bg%d" % t)
        xstage = flat(Bg[:, t, :, :])
        H, T = [], []

        HB = dbg.get("hb") or [1, 2, 3, 1, 2]
        PB_ = dbg.get("poolbank", 0)
        SB_ = dbg.get("sgubank", 6)

        def mm_group(bank, j):
            MM([(ps[:, bank, :], xT[:, k, :], W1[:, k, j * 512:(j + 1) * 512], k == 0, k == 7) for k in range(8)],
               [bf("xTa"), bf("xTb"), bf("WB%d" % j)], [PSB[bank]])

        def h0a():
            EV("cast", xstage, x_res[:, t, :], [XR], [BGB])

        def h0():
            TR([(psb(0)[:, k * 128:(k + 1) * 128], xstage[:, k * 128:(k + 1) * 128], identb[:]) for k in range(8)],
               [BGB, IDB], [PSB[0]])
            EV("xT", flat(xT[:]), psb(0)[:, :], [PSB[0]], [bf("xTa"), bf("xTb")])

        cos_q = cs_t[:, 0:32].unsqueeze(1).unsqueeze(1).broadcast_to([128, 8, 2, 32])
        sin_q = cs_t[:, 32:64].unsqueeze(1).broadcast_to([128, 8, 32])
        cos_k = cs_t[:, 0:32].unsqueeze(1).unsqueeze(1).broadcast_to([128, 2, 2, 32])
        sin_k = cs_t[:, 32:64].unsqueeze(1).broadcast_to([128, 2, 32])

        def h_q():
            S.dma("sp", cs_t[:], c_cs[ti * 128:(ti + 1) * 128, :], bf("cs_t"), writes=[bf("cs_t")])
            mm_group(HB[0], 0)
            q4 = ps[:, HB[0], :].rearrange("p (h two f) -> p h two f", h=8, two=2)
            a4 = rb[:, 0:512].rearrange("p (h two f) -> p h two f", h=8, two=2)
            qb4 = q_bf[:].rearrange("p (h two f) -> p h two f", h=8, two=2)
            t1 = tmp[:, 0:256].rearrange("p (h f) -> p h f", h=8)
            t2 = tmp[:, 256:512].rearrange("p (h f) -> p h f", h=8)
            TT(a4, q4, cos_q, ALU.mult, [PSB[HB[0]], bf("cs_t")], [bf("rb")])
            TT(t1, q4[:, :, 1, :], sin_q, ALU.mult, [PSB[HB[0]], bf("cs_t")], [bf("tmp")])
            TT(t2, q4[:, :, 0, :], sin_q, ALU.mult, [PSB[HB[0]], bf("cs_t")], [bf("tmp")])
            TT(qb4[:, :, 0, :], a4[:, :, 0, :], t1, ALU.subtract, [bf("rb"), bf("tmp")], [bf("q_bf")])
            TT(qb4[:, :, 1, :], a4[:, :, 1, :], t2, ALU.add, [bf("rb"), bf("tmp")], [bf("q_bf")])

        def h_kvz():
            mm_group(HB[1], 1)
            k4 = ps[:, HB[1], 0:128].rearrange("p (h two f) -> p h two f", h=2, two=2)
            kr4 = rb[:, 512:640].rearrange("p (h two f) -> p h two f", h=2, two=2)
            u1 = acc[:, 0:64].rearrange("p (h f) -> p h f", h=2)
            u2 = acc[:, 64:128].rearrange("p (h f) -> p h f", h=2)
            TT(kr4, k4, cos_k, ALU.mult, [PSB[HB[1]], bf("cs_t")], [bf("rbk")])
            TT(u1, k4[:, :, 1, :], sin_k, ALU.mult, [PSB[HB[1]], bf("cs_t")], [bf("acc")])
            TT(u2, k4[:, :, 0, :], sin_k, ALU.mult, [PSB[HB[1]], bf("cs_t")], [bf("acc")])
            TT(kr4[:, :, 0, :], kr4[:, :, 0, :], u1, ALU.subtract, [bf("rbk"), bf("acc")], [bf("rbk")])
            TT(kr4[:, :, 1, :], kr4[:, :, 1, :], u2, ALU.add, [bf("rbk"), bf("acc")], [bf("rbk")])
            EV("kdup", kdup[:].rearrange("p kv (u d) -> p kv u d", u=2),
               rb[:, 512:640].rearrange("p (kv d) -> p kv d", kv=2).unsqueeze(2).broadcast_to([128, 2, 2, 64]),
               [bf("rbk")], [bf("kdup")])
            if last:
                S.dma("sp", o_k_p[l], rb[:, 512:640], bf("rbout"), reads=[bf("rbk")], final=True)
            EV("V", V_ring[:, vslot, :, 0:64], ps[:, HB[1], 128:256].rearrange("p (kv d) -> p kv d", kv=2), [PSB[HB[1]]], [VS])
            if last:
                ACT(vr_f[:], ps[:, HB[1], 128:256], AF.Copy, [PSB[HB[1]]], [bf("tmp")])
                S.dma("sp", o_v_p[l], vr_f[:], bf("vr_f"), reads=[bf("tmp")], final=True)
            ACT(sz[:, 256:512], ps[:, HB[1], 256:512], AF.Tanh, [PSB[HB[1]]], [bf("szb")], scale=0.5)
            STT(sz[:, 256:512], sz[:, 256:512], 1.0, ps[:, HB[1], 256:512], ALU.add, ALU.mult, [bf("szb"), PSB[HB[1]]], [bf("szb")])

        def h_xaza():
            mm_group(HB[2], 2)
            EV("xa", xa_ring[:, slot, :], ps[:, HB[2], 0:256], [PSB[HB[2]]], [XA])
            if last:
                ACT(xa_f[:], ps[:, HB[2], 0:256], AF.Copy, [PSB[HB[2]]], [bf("tmp")])
                S.dma("sp", o_pool_p[l], xa_f[113:128, :], bf("xa_f"), reads=[bf("tmp")], final=True)
            ACT(sz[:, 0:256], ps[:, HB[2], 256:512], AF.Tanh, [PSB[HB[2]]], [bf("sza")], scale=0.5)
            STT(sz[:, 0:256], sz[:, 0:256], 1.0, ps[:, HB[2], 256:512], ALU.add, ALU.mult, [bf("sza"), PSB[HB[2]]], [bf("sza")])

        def h_uv():
            mm_group(HB[3], 3)
            layer_norm_stats([ps[:, HB[3], 256:512]], [PSB[HB[3]]])
            TS(vtmp[:], ps[:, HB[3], 256:512], mv[:, 0:1], mv[:, 3:4], ALU.subtract, ALU.mult, [PSB[HB[3]], bf("mv")], [bf("vt")])
            TT(vtmp[:], vtmp[:], slng[:], ALU.mult, [bf("vt"), bf("slng")], [bf("vt")])
            TT(vn_bf[:], vtmp[:], slnb[:], ALU.add, [bf("vt"), bf("slnb")], [bf("vn_bf")])
            TT(uz[:], ps[:, HB[3], 0:256], sz[:, 256:512], ALU.mult, [PSB[HB[3]], bf("szb")], [bf("uz")])

        def h_zc():
            mm_group(HB[4], 4)
            ACT(sz[:, 512:1024], ps[:, HB[4], :], AF.Tanh, [PSB[HB[4]]], [bf("szc")], scale=0.5)
            STT(sz[:, 512:1024], sz[:, 512:1024], 1.0, ps[:, HB[4], :], ALU.add, ALU.mult, [bf("szc"), PSB[HB[4]]], [bf("szc")])

        H.extend([h0, h_q, h_kvz, h_xaza, h_uv, h_zc, h0a])

        def t_pool1():
            lst = []
            for g in range(4):
                o = ps[:, PB_, g * 64:(g + 1) * 64]
                lst.append((o, poolP[:, (0 if ti == 0 else 4) + g, :], xa_ring[:, slot, g * 64:(g + 1) * 64], True, not has_prev))
                if has_prev:
                    lst.append((o, poolP[:, 8 + g, :], xa_ring[:, pslot, g * 64:(g + 1) * 64], False, True))
            MM(lst, [XA, bf("poolP")] + ([XAP] if has_prev else []), [PSB[PB_]])
            EV("pooled", pooled_bf[:], ps[:, PB_, 0:256], [PSB[PB_]], [bf("pooled_bf")])

        def t_pool2():
            TR([(psb(5)[:, c * 128:(c + 1) * 128], pooled_bf[:, c * 128:(c + 1) * 128], identb[:]) for c in range(2)],
               [bf("pooled_bf"), IDB], [PSB[5]])
            EV("pooledT", flat(pooledT[:]), psb(5)[:, 0:256], [PSB[5]], [bf("pooledT")])

        def t_pool3():
            MM([(ps[:, PB_, 256 + c * 128:256 + (c + 1) * 128], pooledT[:, c, :], bdw[:, c, :], True, True) for c in range(2)],
               [bf("pooledT"), bf("bdw")], [PSB[PB_]])
            TT(B_tm[:, 0:256], ps[:, PB_, 256:512], sz[:, 0:256], ALU.mult, [PSB[PB_], bf("sza")], [bf("Btm_a")])

        def t_sgu():
            MM([(ps[:, SB_, g * 64:(g + 1) * 64], sguWT[:, g, :], vn_bf[:, g * 64:(g + 1) * 64], True, True) for g in range(4)],
               [bf("sguWT"), bf("vn_bf")], [PSB[SB_]])
            TT(vtmp[:], ps[:, SB_, 0:256], sgub[:], ALU.add, [PSB[SB_], bf("sgub")], [bf("vt")])
            TT(B_tm[:, 256:512], vtmp[:], uz[:], ALU.mult, [bf("vt"), bf("uz")], [bf("Btm_b")])

        def t_qkT():
            TR([(psb(7)[:, c * 128:(c + 1) * 128], q_bf[:, c * 128:(c + 1) * 128], identb[:]) for c in range(4)],
               [bf("q_bf"), IDB], [PSB[7]])
            TR([(psb(5)[:, 256 + kv * 128:256 + (kv + 1) * 128], kdup[:, kv, :], identb[:]) for kv in range(2)],
               [bf("kdup"), IDB], [PSB[5]])
            EV("qT", flat(qT[:]), psb(7)[:, 0:512], [PSB[7]], [bf("qT")])
            EV("kT", flat(kT_ring[:, slot, :, :]), psb(5)[:, 256:512], [PSB[5]], [KS])

        positions = [(0, slot, vslot, KS, VS, 6)] + ([(1, pslot, vpslot, KP, VP, 4)] if has_prev else [])

        def t_scores(pos):
            pi, sl, vs_, KB, VB, base = pos

            def fn():
                lst = []
                for kv in range(2):
                    for j in range(2):
                        lst.append((ps[:, base + j, kv * 256:(kv + 1) * 256], kT_ring[j * 64:(j + 1) * 64, sl, kv, :],
                                    qT[j * 64:(j + 1) * 64, 2 * kv:2 * kv + 2, :], True, True))
                MM(lst, [KB, bf("qT")], [PSB[base], PSB[base + 1]])
                ACT(PT[:, pi, :], ps[:, base:base + 2, :].rearrange("p b c -> p (b c)"), AF.Exp,
                    [PSB[base], PSB[base + 1]], [bf("PT%d" % pi)], scale=0.125)
                TT(PT[:, pi, :].rearrange("p (r t) -> p r t", r=8), PT[:, pi, :].rearrange("p (r t) -> p r t", r=8),
                   maskb[:, pi, :].unsqueeze(1).broadcast_to([128, 8, 128]), ALU.mult,
                   [bf("PT%d" % pi), bf("maskb")], [bf("PT%d" % pi)])
            return fn

        def t_pv():
            lst = []
            npos = len(positions)
            for r in range(8):
                kv = (r // 2) % 2
                o = ps[:, 6 + r // 4, (r % 4) * 65:(r % 4) * 65 + 65]
                for (pi, sl, vs_, KB, VB, base) in positions:
                    lst.append((o, PT[:, pi, r * 128:(r + 1) * 128], V_ring[:, vs_, kv, :], pi == 0, pi == npos - 1))
            MM(lst, [bf("PT0")] + ([bf("PT1")] if has_prev else []) + [p_[4] for p_ in positions], [PSB[6], PSB[7]])
            for b in range(2):
                pv = ps[:, 6 + b, 0:260].rearrange("p (r c) -> p r c", c=65)
                TT(den[:, b * 4:(b + 1) * 4], pv[:, :, 64], esink[:, b * 4:(b + 1) * 4], ALU.add,
                   [PSB[6 + b], bf("esink")], [bf("den")])
            S.op("dve", lambda e: e.reciprocal(out=rden[:], in_=den[:]), [bf("den")], [bf("rden")])
            for b in range(2):
                pv = ps[:, 6 + b, 0:260].rearrange("p (r c) -> p r c", c=65)
                TT(yc.rearrange("p (kv ci j d) -> p j kv ci d", kv=2, ci=2, j=2)[:, b],
                   pv[:, :, 0:64].rearrange("p (kv ci) d -> p kv ci d", kv=2),
                   rden[:, b * 4:(b + 1) * 4].rearrange("p (kv ci) -> p kv ci", kv=2).unsqueeze(3).broadcast_to([128, 2, 2, 64]),
                   ALU.mult, [PSB[6 + b], bf("rden")], [bf("tmp")])
            TT(B_tm[:, 512:1024], yc, sz[:, 512:1024], ALU.mult, [bf("tmp"), bf("szc")], [bf("Btm_c")])

        def t_BT():
            TR([(psb(4)[:, c * 128:(c + 1) * 128], B_tm[:, c * 128:(c + 1) * 128], identb[:]) for c in range(8)],
               [bf("Btm_a"), bf("Btm_b"), bf("Btm_c"), IDB], [PSB[4]])
            EV("BT", flat(Bg[:, t, :, :]), psb(4)[:, :], [PSB[4]], [BGB], scale=0.5)

        T.extend([t_pool1, t_pool2, t_pool3, t_sgu, t_qkT] + [t_scores(p_) for p_ in positions] + [t_pv, t_BT])
        return H, T

    bar_t = sb("bar_t", [128, 1])
    ALIASED = ["acc", "gsb", "tmp", "vt", "uz", "rb", "rbk", "th0", "th1", "ac0", "ac1",
               "sza", "szb", "szc", "stgA", "stgB", "stgC", "stgD"]

    def phase_barrier():
        bl = [bf(n) for n in ALIASED]
        S.op("dve", lambda e: e.memset(bar_t[:], 0.0), bl, bl + [bf("bar_t")])

    P1_ORDER = dbg.get("p1order") or ["qkT", "sgu", "pool1", "h_q", "pool2", "sc0", "sc1", "pool3", "cast", "h_kvz", "h_xaza",
                                      "pv", "h_uv", "h_zc", "xT", "BT"]

    def p1_phase(l, gi, ntiles, hooks=None, upper=None):
        Hs, Ts = zip(*[p1_steps(l, gi, t) for t in range(ntiles)])
        def hooked(h_steps):
            h0, h_q, h_kvz, h_xaza, h_uv, h_zc, h0a = h_steps
            if hooks is None:
                return h_steps
            def h_zc_h():
                h_zc()
                for b in range(5):
                    hooks[b]()
            return (h0, h_q, h_kvz, h_xaza, h_uv, h_zc_h, h0a)

        Hs = list(Hs)
        Hs[ntiles - 1] = hooked(Hs[ntiles - 1])
        if upper is not None:
            for i in range(min(5, ntiles)):
                h = list(Hs[i])
                h[4] = (lambda f=h[4], i=i: (f(), upper(i)))
                Hs[i] = tuple(h)
            for i in range(ntiles, 5):
                pass
        Hs[0][6]()
        for f in Hs[0][:6]:
            f()
        if ntiles > 1:
            Hs[1][6]()
            Hs[1][0]()
        for t in range(ntiles):
            T = list(Ts[t])
            if t + 1 < ntiles:
                h0, h_q, h_kvz, h_xaza, h_uv, h_zc, h0a = Hs[t + 1]
                cast_next = Hs[t + 2][6] if t + 2 < ntiles else (lambda: None)
                xT_next = Hs[t + 2][0] if t + 2 < ntiles else (lambda: None)
                names = ["pool1", "pool2", "pool3", "sgu", "qkT"] + ["sc%d" % i for i in range(len(T) - 7)] + ["pv", "BT"]
                tm = dict(zip(names, T))
                tm.update(h_q=h_q, h_kvz=h_kvz, h_xaza=h_xaza, h_uv=h_uv, h_zc=h_zc, cast=cast_next, xT=xT_next)
                for nme in P1_ORDER:
                    if nme in tm:
                        tm[nme]()
            else:
                names = ["pool1", "pool2", "pool3", "sgu", "qkT"] + ["sc%d" % i for i in range(len(T) - 7)] + ["pv", "BT"]
                tm = dict(zip(names, T))
                order = [tm["qkT"], tm["sc0"]] + ([tm["sc1"]] if "sc1" in tm else [])
                order += [tm["pool1"], tm["pool2"], tm["pool3"], tm["sgu"], tm["pv"], tm["BT"]]
                for f in order:
                    f()

    def resid_ln(l, p, xr, XR, ob):
        STT(xr, xr, ALPHA, ps[0:p, ob:ob + 2, :].rearrange("p b c -> p (b c)"), ALU.mult, ALU.add,
            [XR, PSB[ob], PSB[ob + 1]], [XR])
        layer_norm_stats([xr[:, 0:512], xr[:, 512:1024]], [XR])
        ACT(xr, xr, AF.Identity, [XR, bf("mv")], [XR], scale=mv[0:p, 3:4], bias=mv[0:p, 4:5])
        TT(xr, xr, lng[0:p, :], ALU.mult, [XR, bf("lng")], [XR])
        TT(xr, xr, lnb[0:p, :], ALU.add, [XR, bf("lnb")], [XR])

    pair_ctr = [0]

    def pair():
        b = 2 * (pair_ctr[0] % 4)
        pair_ctr[0] += 1
        return b

    CH = {0: [0, 1], 1: [2, 3], 2: [4, 5, 6, 7]}

    def p2_phase(l, gi, ntiles, with_sample, hooks=None, post_ln=None):
        szb16 = sz[:].bitcast(BF16)
        tmpb16 = tmp[:].bitcast(BF16)
        xa = xT2[:].rearrange("p a k t -> p (a k t)").rearrange("p (k t) -> p k t", k=4)
        xb = PTm[:].rearrange("p a c -> p (a c)").rearrange("p (k t) -> p k t", k=4)
        XA = [bf("xTa"), bf("xTb"), bf("xTc")]
        XB = [bf("PT0"), bf("PT1")]
        BT3 = [bf("Btm_a"), bf("Btm_b"), bf("Btm_c")]
        TH = [bf("th0"), bf("th1")]
        AC = [bf("ac0"), bf("ac1")]
        SZ3 = [bf("sza"), bf("szb"), bf("szc")]

        def make_batch(kind, tiles):
            if kind == "t":
                nt = len(tiles)
                N = 128 * nt
                t0 = tiles[0]
                d = dict(N=N, tiles=tiles,
                         xk=lambda k: (xa if k < 4 else xb)[:, k % 4, 0:N], XBUFS=XA + XB,
                         brhs=lambda f: Bg[:, t0:t0 + nt, f, :], BBUFS=[bf("Bg%d" % t) for t in tiles],
                         mch=lambda c: (szb16 if c < 4 else tmpb16)[:, (c % 4) * 512:(c % 4) * 512 + N],
                         MB=lambda c: (SZ3 if c < 4 else [bf("tmp")]))
            else:
                N = NS
                d = dict(N=N, tiles=None,
                         xk=lambda k: xT2[:, 0, k, 0:N], XBUFS=[bf("xTa"), bf("xTb")],
                         brhs=lambda f: Bg[:, G, f, 0:N], BBUFS=[bf("Bg%d" % G)],
                         mch=lambda c: mT[:, c, 0:N], MB=lambda c: [bf("PT1")])
            return d

        def gen_xT(bt, dead=None):
            if bt["tiles"] is not None and dead is not None and len(dead) >= len(bt["tiles"]) and dbg.get("deadstage", 1):
                tl_ = bt["tiles"]
                if dead == "first":
                    stg = [(szb16[:, 0:1024], bf("stgA")), (szb16[:, 1024:2048], bf("stgB")),
                           (tmpb16[:, 0:1024], bf("stgC")), (tmpb16[:, 1024:2048], bf("stgD"))][:len(tl_)]
                else:
                    stg = [(flat(Bg[:, dead[j], :, :]), bf("Bg%d" % dead[j])) for j in range(len(tl_))]
                for j, t in enumerate(tl_):
                    ACT(stg[j][0], x_res[:, t, :], AF.Copy, [bf("xres%d" % t)], [stg[j][1]])
                bs = []
                for j, t in enumerate(tl_):
                    b = pair()
                    bs.append(b)
                    TR([(psb(b + k // 4)[:, (k % 4) * 128:(k % 4 + 1) * 128], stg[j][0][:, k * 128:(k + 1) * 128], identb[:])
                        for k in range(8)], [stg[j][1], bf("identb")], [PSB[b], PSB[b + 1]])
                    ACT(xa[:, :, j * 128:(j + 1) * 128], psb(b)[:, 0:512].rearrange("q (k t) -> q k t", k=4), AF.Copy, [PSB[b]], XA)
                    CP(xb[:, :, j * 128:(j + 1) * 128], psb(b + 1)[:, 0:512].rearrange("q (k t) -> q k t", k=4), [PSB[b + 1]], XB)
                return
            if bt["tiles"] is None:
                p = NS
                ACT(B_tm[0:p, :], xs_res[0:p, :], AF.Copy, [bf("xs_res")], BT3)
                b = pair()
                TR([(psb(b)[:, k * p:(k + 1) * p], B_tm[0:p, k * 128:(k + 1) * 128], identb[0:p, 0:p]) for k in range(8)],
                   BT3 + [bf("identb")], [PSB[b]])
                ACT(xT2[:, 0, :, 0:p], psb(b)[:, 0:8 * p].rearrange("q (k t) -> q k t", k=8), AF.Copy, [PSB[b]],
                    [bf("xTa"), bf("xTb")])
                return
            for j, t in enumerate(bt["tiles"]):
                XR = bf("xres%d" % t)
                ACT(B_tm[:, :], x_res[:, t, :], AF.Copy, [XR], BT3)
                b = pair()
                TR([(psb(b + k // 4)[:, (k % 4) * 128:(k % 4 + 1) * 128], B_tm[:, k * 128:(k + 1) * 128], identb[:])
                    for k in range(8)], BT3 + [bf("identb")], [PSB[b], PSB[b + 1]])
                ACT(xa[:, :, j * 128:(j + 1) * 128], psb(b)[:, 0:512].rearrange("q (k t) -> q k t", k=4), AF.Copy, [PSB[b]], XA)
                CP(xb[:, :, j * 128:(j + 1) * 128], psb(b + 1)[:, 0:512].rearrange("q (k t) -> q k t", k=4), [PSB[b + 1]], XB)

        cnt = [0]

        def c_loop(bt):
            N = bt["N"]
            for c in range(8):
                for br in range(3):
                    i = cnt[0]
                    cnt[0] += 1
                    gb = pair()
                    ob = gb + 1
                    col = br * 1024 + c * 128
                    MM([(ps[:, gb, 0:N], Wg[:, k, col:col + 128], bt["xk"](k), k == 0, k == 7) for k in range(8)],
                       bt["XBUFS"] + [bf("WB%d" % (col // 512))], [PSB[gb]])
                    cl = CH[br]
                    MM([(ps[:, ob, 0:N], Wp[:, f, c * 128:(c + 1) * 128], bt["brhs"](f), f == cl[0], f == cl[-1]) for f in cl],
                       bt["BBUFS"] + [bf("WB6"), bf("WB7")], [PSB[ob]])
                    thv = gsb[:, (i % 2) * 512:(i % 2) * 512 + N]
                    ACT(thv, ps[:, gb, 0:N], AF.Tanh, [PSB[gb], bf("bgh")], [TH[i % 2]], scale=0.5,
                        bias=bgh[:, br * 8 + c:br * 8 + c + 1])
                    a0, a1 = acc[:, 0:N], acc[:, 512:512 + N]
                    if br == 0:
                        STT(a0, thv, 1.0, ps[:, ob, 0:N], ALU.add, ALU.mult, [TH[i % 2], PSB[ob]], [AC[0]])
                    else:
                        STT(a1, thv, 1.0, ps[:, ob, 0:N], ALU.add, ALU.mult, [TH[i % 2], PSB[ob]], [AC[1]])
                        TT(a0, a0, a1, ALU.add, AC, [AC[0]])
                        if br == 2:
                            ACT(bt["mch"](c), a0, AF.Copy, [AC[0]], bt["MB"](c), scale=0.5)

        def out_ln(bt, post):
            if bt["tiles"] is None:
                rows = [(NS, 0, xs_res[0:NS, :], bf("xs_res"), None)]
            else:
                rows = [(128, j, x_res[:, t, :], bf("xres%d" % t), t) for j, t in enumerate(bt["tiles"])]
            for (p, j, xr, XR, t) in rows:
                ob = pair()
                for half in range(2):
                    MM([(ps[0:p, ob + half, :], bt["mch"](k)[:, j * 128:j * 128 + p], Wo[:, k, half * 512:(half + 1) * 512],
                         k == 0, k == 7) for k in range(8)],
                       [b_ for k in range(8) for b_ in bt["MB"](k)] + [bf("WB8"), bf("WB9")], [PSB[ob + half]])
                resid_ln(l, p, xr, XR, ob)
                if l == DEPTH - 1:
                    if t is not None:
                        ti = gi * G + t
                        S.dma("sp", y_p[ti * 128:(ti + 1) * 128, :], xr, XR, reads=[XR], final=True)
                        if gi + 1 < NG:
                            tn = ti + G
                            S.dma("sp", xr, xp[tn * 128:(tn + 1) * 128, :], XR, writes=[XR])
                    else:
                        S.dma("sp", y_s[:, :], xr, XR, reads=[XR], final=True)
                post(t)

        tl = list(range(ntiles))
        batches = [make_batch("t", tl[i:i + 4]) for i in range(0, ntiles, 4)]
        if with_sample:
            batches.append(make_batch("s", None))
        nb = len(batches)
        fired = [False, False]

        def post(bi):
            def f(t):
                if t is not None:
                    for g_ in (post_ln or {}).get(t, []):
                        g_()
                if hooks is not None and bi == nb - 1 and not fired[1]:
                    fired[1] = True
                    for b in (2, 3, 4):
                        hooks[b]()
            return f

        gen_xT(batches[0], "first" if dbg.get("firststage", 1) else None)
        for bi, bt in enumerate(batches):
            c_loop(bt)
            if hooks is not None and bi == nb - 1:
                hooks[0]()
                hooks[1]()
            if bi + 1 < nb:
                gen_xT(batches[bi + 1], bt["tiles"])
            out_ln(bt, post(bi))

    spool_v = spool.rearrange("l (b r) f -> l b r f", r=15)

    def sample_loads_v_steps(l):
        steps = [lambda: S.dma("pool", hist[:], spool[l].rearrange("(c r) f -> r c f", c=2), bf("hist"), writes=[bf("hist")])]
        for j in range(2):
            for kv in range(2):
                i = j * 2 + kv
                steps.append(lambda i=i, j=j, kv=kv: S.dma(
                    "pool", Vda[:, :, kv, j * 64:(j + 1) * 64],
                    cv[l, 0:8, :, kv * 64:(kv + 1) * 64].rearrange("b s d -> s b d"), bf("Vda%d" % i),
                    writes=[bf("Vda%d" % i)] + ([bf("Vda")] if i == 0 else [])))
                steps.append(lambda i=i, j=j, kv=kv: S.dma(
                    "pool", Vdb[:, :, kv, j * 64:(j + 1) * 64],
                    cv[l, 8:16, :, kv * 64:(kv + 1) * 64].rearrange("b s d -> s b d"), bf("Vdb%d" % i),
                    writes=[bf("Vdb%d" % i)] + ([bf("Bg0"), bf("Bg1")] if i == 0 else [])))
        return steps

    def sample_loads_k(l):
        S.dma("sp", o_pool_s[l, :, 0:14, :], spool_v[l, :, 1:15, :], bf("sh_pool"), final=True)
        S.dma("sp", o_k_s[l, :, 0:127, :], ck[l, :, 1:128, :], bf("sh_k"), final=True)
        S.dma("sp", o_v_s[l, :, 0:127, :], cv[l, :, 1:128, :], bf("sh_v"), final=True)
        S.dma("pool", Ks, ck[l].rearrange("b s f -> s b f"), bf("PT0"), writes=[bf("PT0"), bf("PT1")])
        S.dma("sp", sw00[:], sgu_w[l, :, 0, 0].partition_broadcast(NS), bf("sw00"), writes=[bf("sw00")], slow=True)
        S.dma("sp", sb0[:], sgu_b[l, :, 0].partition_broadcast(NS), bf("sb0"), writes=[bf("sb0")], slow=True)

    def sample_p1(l):
        p = NS
        XR = bf("xs_res")
        IDF, IDB = bf("identf"), bf("identb")
        TR([(ps[:, 0, k * p:(k + 1) * p], xs_res[0:p, k * 128:(k + 1) * 128], identf[0:p, 0:p]) for k in range(8)],
           [XR, IDF], [PSB[0]])
        ACT(xT[:, :, 0:p], ps[:, 0, 0:8 * p].rearrange("q (k t) -> q k t", k=8), AF.Copy, [PSB[0]], [bf("xTa"), bf("xTb")])
        for pb, j in [(4, 0), (5, 1), (2, 2), (3, 3), (6, 4)]:
            MM([(ps[0:p, pb, :], xT[:, k, 0:p], W1[:, k, j * 512:(j + 1) * 512], k == 0, k == 7) for k in range(8)],
               [bf("xTa"), bf("xTb"), bf("WB%d" % j)], [PSB[pb]])
        xa_s = tmp[0:p, 0:256]
        ACT(xa_s, ps[0:p, 2, 0:256], AF.Copy, [PSB[2]], [bf("tmp")])
        S.dma("sp", o_pool_s[l, :, 14, :], xa_s, bf("o_xa_s"), reads=[bf("tmp")], final=True)
        ACT(sz[0:p, 0:256], ps[0:p, 2, 256:512], AF.Tanh, [PSB[2]], [bf("sza")], scale=0.5)
        STT(sz[0:p, 0:256], sz[0:p, 0:256], 1.0, ps[0:p, 2, 256:512], ALU.add, ALU.mult, [bf("sza"), PSB[2]], [bf("sza")])
        layer_norm_stats([ps[0:p, 3, 256:512]], [PSB[3]])
        TS(vtmp[0:p, :], ps[0:p, 3, 256:512], mv[0:p, 0:1], mv[0:p, 3:4], ALU.subtract, ALU.mult, [PSB[3], bf("mv")], [bf("acc")])
        TT(vtmp[0:p, :], vtmp[0:p, :], slng[0:p, :], ALU.mult, [bf("acc"), bf("slng")], [bf("acc")])
        TT(vtmp[0:p, :], vtmp[0:p, :], slnb[0:p, :], ALU.add, [bf("acc"), bf("slnb")], [bf("acc")])
        S.dma("sp", o_cv_s[l], vtmp[0:p, :], bf("o_cv_s"), reads=[bf("acc")], final=True)
        ACT(sz[0:p, 256:512], ps[0:p, 5, 256:512], AF.Tanh, [PSB[5]], [bf("szb")], scale=0.5)
        STT(sz[0:p, 256:512], sz[0:p, 256:512], 1.0, ps[0:p, 5, 256:512], ALU.add, ALU.mult, [bf("szb"), PSB[5]], [bf("szb")])
        TT(uz[0:p, :], ps[0:p, 3, 0:256], sz[0:p, 256:512], ALU.mult, [PSB[3], bf("szb")], [bf("gsb")])
        ACT(qk[0:p, 0:512], ps[0:p, 4, :], AF.Copy, [PSB[4]], [bf("acc")])
        ACT(qk[0:p, 512:640], ps[0:p, 5, 0:128], AF.Copy, [PSB[5]], [bf("acc")])
        v_new = tmp[0:p, 256:384]
        ACT(v_new, ps[0:p, 5, 128:256], AF.Copy, [PSB[5]], [bf("tmp")])
        S.dma("sp", o_v_s[l, :, 127, :], v_new, bf("o_vn_s"), reads=[bf("tmp")], final=True)
        CP(vdn[:].rearrange("p kv (j d) -> p kv j d", j=2),
           v_new.rearrange("p (kv d) -> p kv d", kv=2).unsqueeze(2).broadcast_to([p, 2, 2, 64]), [bf("tmp")], [bf("vdn")])
        ACT(sz[0:p, 512:1024], ps[0:p, 6, :], AF.Tanh, [PSB[6]], [bf("szc")], scale=0.5)
        STT(sz[0:p, 512:1024], sz[0:p, 512:1024], 1.0, ps[0:p, 6, :], ALU.add, ALU.mult, [bf("szc"), PSB[6]], [bf("szc")])
        qk4 = qk[0:p, :].rearrange("p (h two f) -> p h two f", h=10, two=2)
        rb4 = rb[0:p, :].rearrange("p (h two f) -> p h two f", h=10, two=2)
        cos_b = cs_s[:, 0:32].unsqueeze(1).unsqueeze(1).broadcast_to([p, 10, 2, 32])
        sin_b = cs_s[:, 32:64].unsqueeze(1).broadcast_to([p, 10, 32])
        t1 = tmp[0:p, 384:704].rearrange("p (h f) -> p h f", h=10)
        t2 = tmp[0:p, 704:1024].rearrange("p (h f) -> p h f", h=10)
        TT(rb4, qk4, cos_b, ALU.mult, [bf("acc"), bf("cs_s")], [bf("gsb")])
        TT(t1, qk4[:, :, 1, :], sin_b, ALU.mult, [bf("acc"), bf("cs_s")], [bf("tmp")])
        TT(t2, qk4[:, :, 0, :], sin_b, ALU.mult, [bf("acc"), bf("cs_s")], [bf("tmp")])
        TT(rb4[:, :, 0, :], rb4[:, :, 0, :], t1, ALU.subtract, [bf("gsb"), bf("tmp")], [bf("gsb")])
        TT(rb4[:, :, 1, :], rb4[:, :, 1, :], t2, ALU.add, [bf("gsb"), bf("tmp")], [bf("gsb")])
        S.dma("sp", o_k_s[l, :, 127, :], rb[0:p, 512:640], bf("o_kn_s"), reads=[bf("gsb")], final=True)
        lst = []
        for g in range(4):
            for c in range(2):
                lst.append((ps[0:p, 0, g * 64:(g + 1) * 64], selb[:, g * 2 + c, :], hist[:, c, g * 64:(g + 1) * 64], c == 0, c == 1))
        MM(lst, [bf("selb"), bf("hist")], [PSB[0]])
        for g, w in enumerate(POOL_WINDOWS):
            STT(pooled_bf[0:p, g * 64:(g + 1) * 64], xa_s[:, g * 64:(g + 1) * 64], 1.0 / w - 1.0,
                ps[0:p, 0, g * 64:(g + 1) * 64], ALU.mult, ALU.add, [bf("tmp"), PSB[0]], [bf("pooled_bf")])
        TR([(psb(1)[:, c * p:(c + 1) * p], pooled_bf[0:p, c * 128:(c + 1) * 128], identb[0:p, 0:p]) for c in range(2)],
           [bf("pooled_bf"), IDB], [PSB[1]])
        ACT(pooledT[:, :, 0:p], psb(1)[:, 0:2 * p].rearrange("q (c t) -> q c t", c=2), AF.Copy, [PSB[1]], [bf("pooledT")])
        MM([(ps[0:p, 0, 256 + c * 128:256 + (c + 1) * 128], pooledT[:, c, 0:p], bdw[:, c, :], True, True) for c in range(2)],
           [bf("pooledT"), bf("bdw")], [PSB[0]])
        TT(B_tm[0:p, 0:256], ps[0:p, 0, 256:512], sz[0:p, 0:256], ALU.mult, [PSB[0], bf("sza")], [bf("Btm_a")])
        vt3 = vtmp[0:p, :].rearrange("p (g c) -> p g c", g=4)
        t3 = tmp[0:p, 384:640].rearrange("p (g c) -> p g c", g=4)
        TT(t3, vt3, sw00[:].unsqueeze(2).broadcast_to([p, 4, 64]), ALU.mult, [bf("acc"), bf("sw00")], [bf("tmp")])
        TT(t3, t3, sb0[:].unsqueeze(2).broadcast_to([p, 4, 64]), ALU.add, [bf("tmp"), bf("sb0")], [bf("tmp")])
        TT(B_tm[0:p, 256:512], tmp[0:p, 384:640], uz[0:p, :], ALU.mult, [bf("tmp"), bf("gsb")], [bf("Btm_b")])
        for hb in range(2):
            TR([(psb(2 + hb)[:, i * 128:(i + 1) * 128], Ks[:, hb * 8 + i, :], identb[:]) for i in range(8)],
               [bf("PT0"), bf("PT1"), IDB], [PSB[2 + hb]])
            if hb == 0:
                ACT(flat(KTs[:, 0:8, :]), psb(2)[:, :], AF.Copy, [PSB[2]], [bf("KTs")])
            else:
                CP(flat(KTs[:, 8:16, :]), psb(3)[:, :], [PSB[3]], [bf("KTs")])
        for kv in range(2):
            CP(Qexp[:, kv * 4:(kv + 1) * 4, kv * 64:(kv + 1) * 64],
               rb[0:p, kv * 256:(kv + 1) * 256].rearrange("p (h d) -> p h d", h=4), [bf("gsb")], [bf("Qexp")])
        CP(kdup[0:p, 0, :], rb[0:p, 512:640], [bf("gsb")], [bf("kdup")])
        TR([(psb(4)[:, h * p:(h + 1) * p], Qexp[:, h, :], identb[0:p, 0:p]) for h in range(8)]
           + [(psb(4)[:, 8 * p:9 * p], kdup[0:p, 0, :], identb[0:p, 0:p])],
           [bf("Qexp"), bf("kdup"), IDB], [PSB[4]])
        ACT(Qblk[:].rearrange("q b h -> q h b"), psb(4)[:, 0:8 * p].rearrange("q (h b) -> q h b", h=8), AF.Copy,
            [PSB[4]], [bf("Qblk")])
        CP(kTn[:], psb(4)[:, 8 * p:9 * p], [PSB[4]], [bf("kTn")])
        MM([(ps[:, 7, b * 8:(b + 1) * 8], KTs[:, b, :], Qblk[:, b, :], True, True) for b in range(p)]
           + [(ps[0:p, 7, 128:256], kTn[:], Qblk[:].rearrange("q b h -> q (b h)"), True, True)],
           [bf("KTs"), bf("Qblk"), bf("kTn")], [PSB[7]])
        ACT(PTs[:], ps[:, 7, 0:128], AF.Exp, [PSB[7]], [bf("PTs")], scale=0.125)
        ACT(Pself_f, ps[0:p, 7, 128:256], AF.Exp, [PSB[7]], [bf("tmp")], scale=0.125)
        TT(Pself[:], Pself_f, dmask[:], ALU.mult, [bf("tmp"), bf("dmask")], [bf("Pself")])
        pvb = (5, 1)
        for kv in range(2):
            lst = [(ps[:, pvb[kv], 0:64].rearrange("q (b i) -> q b i", i=4), vdn[:, kv, :],
                    Pself[:].rearrange("q (b h) -> q b h", h=8)[:, :, kv * 4:(kv + 1) * 4], True, False)]
            for b in range(p):
                vsrc = Vda[:, b, kv, :] if b < 8 else Vdb[:, b - 8, kv, :]
                lst.append((ps[:, pvb[kv], b * 4:b * 4 + 4], vsrc, PTs[:, b * 8 + kv * 4:b * 8 + kv * 4 + 4], False, b == p - 1))
            MM(lst, [bf("Vda"), bf("Bg0"), bf("Bg1")] + [bf("Vda%d" % i) for i in range(4)] + [bf("Vdb%d" % i) for i in range(4)]
               + [bf("PTs"), bf("Pself"), bf("vdn")], [PSB[pvb[kv]]])
        MM([(ps[:, 6, 0:128], onesb[:], PTs[:], True, False), (ps[:, 6, 0:128], onesb[0:p, :], Pself[:], False, True)],
           [bf("onesb"), bf("PTs"), bf("Pself")], [PSB[6]])
        TT(rden_s.rearrange("q (b h) -> q b h", h=8), ps[:, 6, 0:128].rearrange("q (b h) -> q b h", h=8),
           esink_h[:].unsqueeze(1).broadcast_to([128, p, 8]), ALU.add, [PSB[6], bf("esink_h")], [bf("tmp")])
        S.op("dve", lambda e: e.reciprocal(out=rden_s, in_=rden_s), [bf("tmp")], [bf("tmp")])
        for kv in range(2):
            TT(Rn.rearrange("q (b h) -> q b h", h=8)[:, :, kv * 4:(kv + 1) * 4],
               ps[:, pvb[kv], 0:64].rearrange("q (b i) -> q b i", i=4),
               rden_s.rearrange("q (b h) -> q b h", h=8)[:, :, kv * 4:(kv + 1) * 4], ALU.mult,
               [PSB[pvb[kv]], bf("tmp")], [bf("tmp")])
        TR([(ps[:, 7, 256 + c * p:256 + (c + 1) * p], sz[0:p, 512 + c * 128:512 + (c + 1) * 128], identf[0:p, 0:p]) for c in range(4)],
           [bf("szc"), IDF], [PSB[7]])
        for j in range(2):
            STT(Bg[j * 64:(j + 1) * 64, G, 4:8, 0:p],
                Rn.rearrange("q (b c j) -> q c b j", c=4, j=2)[j * 64:(j + 1) * 64, :, :, j], 0.5,
                ps[j * 64:(j + 1) * 64, 7, 256:256 + 4 * p].rearrange("q (c b) -> q c b", c=4), ALU.mult, ALU.mult,
                [bf("tmp"), PSB[7]], [bf("Bg%d" % G)])
        TR([(psb(4)[:, c * p:(c + 1) * p], B_tm[0:p, c * 128:(c + 1) * 128], identb[0:p, 0:p]) for c in range(4)],
           [bf("Btm_a"), bf("Btm_b"), IDB], [PSB[4]])
        ACT(Bg[:, G, 0:4, 0:p], psb(4)[:, 0:4 * p].rearrange("q (c t) -> q c t", c=4), AF.Copy, [PSB[4]], [bf("Bg%d" % G)],
            scale=0.5)

    S.dma("sp", xs_res[:], xs[:, :], bf("xs_res"), writes=[bf("xs_res")])
    for gi in range(dbg.get("ng", NG)):
        for t in range(G):
            ti = gi * G + t
            if gi == 0:
                S.dma("sp", x_res[:, t, :], xp[ti * 128:(ti + 1) * 128, :], bf("xres%d" % t), writes=[bf("xres%d" % t)])
        for l in range(dbg.get("depth", DEPTH)):
            nxt = (gi, l + 1) if l + 1 < DEPTH else ((gi + 1, 0) if gi + 1 < NG else None)
            w2h = {b: (lambda b=b, l=l: w2_block(l, b)) for b in range(5)}
            w1h = None if nxt is None else {b: (lambda b=b, ln=nxt[1]: w1_block(ln, b)) for b in range(5)}
            smp = gi == 0 and dbg.get("sample", 1)
            if dbg.get("stage", 9) >= 1:
                if smp and l == 0:
                    for f in sample_loads_v_steps(l):
                        f()
                if smp:
                    sample_loads_k(l)
                if gi == 0 and l == 0:
                    load_w1(l)
            if dbg.get("stage", 9) >= 2:
                if gi == 0 and l == 0:
                    consts_prefetch(l)
                    consts_compute_p1(l)
                consts_late(l)
            post_ln = None
            if smp and l + 1 < DEPTH:
                nxt_steps = sample_loads_v_steps(l + 1)
                post_ln = {}
                for k_, f in enumerate(nxt_steps):
                    post_ln.setdefault(3 + k_ // 2, []).append(f)
            if dbg.get("stage", 9) >= 3:
                if gi == 0 and dbg.get("sample", 1):
                    phase_barrier()
                    sample_p1(l)
                phase_barrier()
                p1_phase(l, gi, dbg.get("tiles", G), w2h, (lambda i, l=l: w2_upper(l, i)))
            if dbg.get("stage", 9) >= 5:
                phase_barrier()
                consts_p2(l)
                if nxt is not None:
                    consts_prefetch(nxt[1])
                p2_phase(l, gi, dbg.get("tiles", G), gi == 0 and dbg.get("sample", 1), w1h, post_ln)
                if nxt is not None:
                    consts_compute_p1(nxt[1])
    S.emit()
    if S.maxops is not None:
        print("TRACE last ops:", S.trace[-3:], "total", S.nops)
    return nc, stack


_CACHE = {}


def _consts():
    half = 32
    inv = (10000.0 ** (-np.arange(half, dtype=np.float32) / half)).astype(np.float32)
    pos = np.arange(SEQ, dtype=np.float32)
    ang = (pos[:, None] * inv[None, :]).astype(np.float32)
    c_cs = np.concatenate([np.cos(ang), np.sin(ang)], axis=1).astype(np.float32)
    angs = (np.float32(PAST_LEN) * inv).astype(np.float32)
    c_cs_s = np.tile(np.concatenate([np.cos(angs), np.sin(angs)])[None, :], (NS, 1)).astype(np.float32)
    P = np.zeros((3, 4, 128, 128), np.float32)
    for g, w in enumerate(POOL_WINDOWS):
        for t in range(128):
            for s in range(max(0, t - w + 1), t + 1):
                P[0, g, s, t] += 1.0 / min(t + 1, w)
                P[1, g, s, t] += 1.0 / w
            P[0, g, t, t] -= 1.0
            P[1, g, t, t] -= 1.0
            for sp in range(128 + t - w + 1, 128):
                if sp >= 0:
                    P[2, g, sp, t] += 1.0 / w
    s_idx = np.arange(128)[:, None]
    t_idx = np.arange(128)[None, :]
    mask = np.stack([(s_idx <= t_idx), (s_idx >= t_idx)]).astype(np.float32)
    tril = (np.arange(128)[None, :] <= np.arange(128)[:, None]).astype(np.float32)
    sel = np.zeros((4, 2, 120, 16), np.float32)
    for g, w in enumerate(POOL_WINDOWS):
        for c in range(2):
            for bl in range(8):
                for row in range(15):
                    if row >= 15 - (w - 1):
                        sel[g, c, bl * 15 + row, c * 8 + bl] = 1.0 / w
    dm = np.zeros((NS, NS * 8), np.float32)
    for b in range(NS):
        dm[b, b * 8:(b + 1) * 8] = 1.0
    return dict(c_cs=c_cs, c_cs_s=c_cs_s, c_poolP=P, c_mask=mask, c_tril=tril,
                c_ident=np.eye(128, dtype=np.float32), c_sel=sel, c_dmask=dm)


def kernel(x_prompt, x_sample, state_pool, cache_k_win, cache_v_win, w_in, b_gate, pool_w, pool_scale,
           sgu_ln_g, sgu_ln_b, sgu_w, sgu_b, attn_sinks, w_proj_a, w_proj_b, w_proj_c, w_out, ln_g, ln_b):
    f = lambda a: np.ascontiguousarray(np.asarray(a, dtype=np.float32))
    if "nc" not in _CACHE:
        _CACHE["nc"] = build_program()
    nc, _stack = _CACHE["nc"]
    consts = _consts()
    shared = dict(w_in=f(w_in), b_gate=f(b_gate).reshape(DEPTH, 3 * D), pool_w=f(pool_w), pool_scale=f(pool_scale),
                  sgu_ln_g=f(sgu_ln_g), sgu_ln_b=f(sgu_ln_b), sgu_w=f(sgu_w), sgu_b=f(sgu_b), sinks=f(attn_sinks),
                  w_pa=f(w_proj_a), w_pb=f(w_proj_b), w_pc=f(w_proj_c), w_out=f(w_out), ln_g=f(ln_g), ln_b=f(ln_b))
    shared.update(consts)
    xpn, xsn = f(x_prompt), f(x_sample)
    spn, ckn, cvn = f(state_pool), f(cache_k_win), f(cache_v_win)
    in_maps = []
    for c in range(NCORES):
        sl = slice(c * NS, (c + 1) * NS)
        m = dict(shared)
        m["xp"] = xpn[c]
        m["xs"] = xsn[sl, 0, :]
        m["spool"] = np.ascontiguousarray(spn[:, sl].reshape(DEPTH, NS * 15, 256))
        m["ck"] = np.ascontiguousarray(ckn[:, sl].reshape(DEPTH, NS, 128, 128))
        m["cv"] = np.ascontiguousarray(cvn[:, sl].reshape(DEPTH, NS, 128, 128))
        in_maps.append(m)
    res = run_bass_kernel_spmd(nc, in_maps, core_ids=list(range(NCORES)))
    R = res.results
    y_p = np.stack([R[c]["y_p"] for c in range(NCORES)], 0)
    y_s = np.concatenate([R[c]["y_s"] for c in range(NCORES)], 0).reshape(128, 1, D)
    pool_p = np.stack([R[c]["o_pool_p"] for c in range(NCORES)], 1)
    k_p = np.stack([R[c]["o_k_p"] for c in range(NCORES)], 1).reshape(DEPTH, 8, 128, 2, 64)
    v_p = np.stack([R[c]["o_v_p"] for c in range(NCORES)], 1).reshape(DEPTH, 8, 128, 2, 64)
    pool_s = np.concatenate([R[c]["o_pool_s"] for c in range(NCORES)], 1)
    k_s = np.concatenate([R[c]["o_k_s"] for c in range(NCORES)], 1).reshape(DEPTH, 128, 128, 2, 64)
    v_s = np.concatenate([R[c]["o_v_s"] for c in range(NCORES)], 1).reshape(DEPTH, 128, 128, 2, 64)
    cv_s = np.concatenate([R[c]["o_cv_s"] for c in range(NCORES)], 1).reshape(DEPTH, 128, 1, 256)
    return (y_p, y_s, pool_p, k_p, v_p, pool_s, k_s, v_s, cv_s)


if __name__ == "__main__":
    import time
    t0 = time.time()
    nc, _ = build_program()
    print("built in", time.time() - t0)
```

```python
from contextlib import ExitStack
import numpy as np
import concourse.bass as bass
import concourse.mybir as mybir
from concourse.bass_utils import run_bass_kernel_spmd

F32 = mybir.dt.float32
BF16 = mybir.dt.bfloat16
AF = mybir.ActivationFunctionType
ALU = mybir.AluOpType
AX = mybir.AxisListType

NCORES = 8
D = 1024
SEQ = 2048
NT = SEQ // 128
G = 8
NG = NT // G
NS = 16
DEPTH = 2
D_IN = 5632
ALPHA = (2.0 * DEPTH) ** 0.25
LN_EPS = 1e-5
PAST_LEN = 8192
POOL_WINDOWS = (2, 4, 8, 16)


class Buf:
    __slots__ = ("name", "last_w", "readers", "dsem", "dcount", "exclusive")

    def __init__(self, name, exclusive=False):
        self.name = name
        self.exclusive = exclusive
        self.last_w = None
        self.readers = []
        self.dsem = None
        self.dcount = 0


class Sched:
    ENGS = ("pe", "act", "dve", "pool", "sp")

    def __init__(self, nc, stack):
        self.nc = nc
        self.stack = stack
        self.sems = {}
        self.ops = {e: [] for e in self.ENGS}
        self.count = {e: 0 for e in self.ENGS}
        self.waited = {e: {} for e in self.ENGS}
        self.nsem = 0
        for e in ("pe", "act", "dve", "pool"):
            self.sems[e] = self._sem("eng_" + e)
        self.final_events = []
        self.dma_keys = []
        self.nops = 0
        self.maxops = None
        self.trace = []

    def _sem(self, name):
        self.nsem += 1
        return self.stack.enter_context(self.nc.semaphore(name))

    def buf(self, name):
        return Buf(name)

    def _deps(self, eng, reads, writes):
        waits = {}

        def need(ev):
            if ev is None:
                return
            sid, val, weng = ev
            if weng == eng and eng == "pe":
                return
            if self.waited[eng].get(sid, 0) >= val:
                return
            if waits.get(sid, (0,))[0] < val:
                waits[sid] = (val,)

        for b in reads:
            need(b.last_w)
            if b.exclusive:
                for ev in b.readers:
                    if ev[2] != eng:
                        need(ev)
        for b in writes:
            need(b.last_w)
            for ev in b.readers:
                need(ev)
        out = []
        for sid, (val,) in waits.items():
            self.waited[eng][sid] = val
            out.append((sid, val))
        return out

    def _commit(self, ev, reads, writes):
        for b in writes:
            b.last_w = ev
            b.readers = []
        for b in reads:
            if b not in writes:
                b.readers.append(ev)

    def _skip(self):
        self.nops += 1
        if self.maxops is not None and self.nops > self.maxops:
            return True
        if self.maxops is not None:
            import inspect
            fr = inspect.stack()
            self.trace.append((self.nops, [f.lineno for f in fr[2:5]]))
        return False

    def op(self, eng, fn, reads=(), writes=()):
        if self._skip():
            return None
        waits = self._deps(eng, reads, writes)
        self.count[eng] += 1
        ev = (eng, self.count[eng], eng)
        self.waited[eng][eng] = max(self.waited[eng].get(eng, 0), 0)
        self.ops[eng].append((waits, fn, (eng, 1)))
        self._commit(ev, reads, writes)
        return ev

    def dma(self, q, out_ap, in_ap, key, reads=(), writes=(), final=False, slow=False):
        if self._skip():
            return None
        if key.dsem is None:
            key.dsem = "dma_" + key.name
            self.dma_keys.append(key)
            self.sems[key.dsem] = self._sem(key.dsem)
        waits = self._deps(q, reads, writes)
        key.dcount += 16
        ev = (key.dsem, key.dcount, "dma")

        def fn(e, out_ap=out_ap, in_ap=in_ap, slow=slow):
            if slow:
                return e.dma_start(out=out_ap, in_=in_ap, allow_slow_non_contiguous=True)
            return e.dma_start(out=out_ap, in_=in_ap)

        self.ops[q].append((waits, fn, (key.dsem, 16)))
        self._commit(ev, reads, writes)
        if final:
            self.final_events.append(ev)
        return ev

    def emit(self):
        nc = self.nc
        fin = {}
        for key in self.dma_keys:
            fin[key.dsem] = key.dcount
        handles = {"pe": "tensor", "act": "scalar", "dve": "vector", "pool": "gpsimd", "sp": "sync"}
        with nc.Block() as block:
            for eng in self.ENGS:
                ops = self.ops[eng]
                if not ops and eng != "sp":
                    continue

                def body(e, ops=ops, eng=eng):
                    for waits, fn, inc in ops:
                        for sid, val in waits:
                            e.wait_ge(self.sems[sid], val)
                        ins = fn(e)
                        ins.then_inc(self.sems[inc[0]], inc[1])
                    if eng == "sp":
                        for sid, val in fin.items():
                            e.wait_ge(self.sems[sid], val)

                getattr(block, handles[eng])(body)


def build_program(dbg=None):
    dbg = dbg or {}
    nc = bass.Bass("TRN2", target_bir_lowering=False)
    stack = ExitStack()
    S = Sched(nc, stack)
    S.maxops = dbg.get("maxops")

    def din(name, shape):
        return nc.dram_tensor(name, list(shape), F32, kind="ExternalInput").ap()

    def dout(name, shape):
        return nc.dram_tensor(name, list(shape), F32, kind="ExternalOutput").ap()

    xp = din("xp", [SEQ, D])
    xs = din("xs", [NS, D])
    spool = din("spool", [DEPTH, NS * 15, 256])
    ck = din("ck", [DEPTH, NS, 128, 128])
    cv = din("cv", [DEPTH, NS, 128, 128])
    w_in = din("w_in", [DEPTH, D, D_IN])
    b_gate = din("b_gate", [DEPTH, 3 * D])
    pool_w = din("pool_w", [DEPTH, 4, 64, 64])
    pool_scale = din("pool_scale", [DEPTH, 256])
    sgu_ln_g = din("sgu_ln_g", [DEPTH, 256])
    sgu_ln_b = din("sgu_ln_b", [DEPTH, 256])
    sgu_w = din("sgu_w", [DEPTH, 4, 128, 128])
    sgu_b = din("sgu_b", [DEPTH, 4, 128])
    sinks = din("sinks", [DEPTH, 8])
    w_pa = din("w_pa", [DEPTH, 256, D])
    w_pb = din("w_pb", [DEPTH, 256, D])
    w_pc = din("w_pc", [DEPTH, 512, D])
    w_out = din("w_out", [DEPTH, D, D])
    ln_g = din("ln_g", [DEPTH, D])
    ln_b = din("ln_b", [DEPTH, D])
    c_cs = din("c_cs", [SEQ, 64])
    c_cs_s = din("c_cs_s", [NS, 64])
    c_poolP = din("c_poolP", [3, 4, 128, 128])
    c_mask = din("c_mask", [2, 128, 128])
    c_tril = din("c_tril", [128, 128])
    c_ident = din("c_ident", [128, 128])
    c_sel = din("c_sel", [4, 2, 120, 16])
    c_dmask = din("c_dmask", [NS, NS * 8])

    y_p = dout("y_p", [SEQ, D])
    y_s = dout("y_s", [NS, D])
    o_pool_p = dout("o_pool_p", [DEPTH, 15, 256])
    o_k_p = dout("o_k_p", [DEPTH, 128, 128])
    o_v_p = dout("o_v_p", [DEPTH, 128, 128])
    o_pool_s = dout("o_pool_s", [DEPTH, NS, 15, 256])
    o_k_s = dout("o_k_s", [DEPTH, NS, 128, 128])
    o_v_s = dout("o_v_s", [DEPTH, NS, 128, 128])
    o_cv_s = dout("o_cv_s", [DEPTH, NS, 256])

    def sb(name, shape, dt=F32):
        return stack.enter_context(nc.sbuf_tensor(name, list(shape), dt))

    Wbuf = sb("Wbuf", [128, 8, 5120], BF16)
    W1 = Wbuf[:, :, 0:2560]
    Wg = Wbuf[:, :, 0:3072]
    Wp = Wbuf[:, :, 3072:4096]
    Wo = Wbuf[:, :, 4096:5120]
    Bg = sb("Bg", [128, G + 1, 8, 128], BF16)
    x_res = sb("x_res", [128, G, D])
    xs_res = sb("xs_res", [NS, D])
    identf = sb("identf", [128, 128])
    identb = sb("identb", [128, 128], BF16)
    poolP = sb("poolP", [128, 12, 128], BF16)
    maskb = sb("maskb", [128, 2, 128], BF16)
    cs_t = sb("cs_t", [128, 64])
    cs_s = sb("cs_s", [NS, 64])
    ones2 = sb("ones2", [2, 128], BF16)
    onesb = sb("onesb", [128, 128], BF16)
    chalf = sb("chalf", [128, 1])
    selb = sb("selb", [120, 8, 16], BF16)
    dmask = sb("dmask", [NS, NS * 8])
    sguWT = sb("sguWT", [128, 4, 128], BF16)
    sgub = sb("sgub", [128, 256])
    slng = sb("slng", [128, 256])
    slnb = sb("slnb", [128, 256])
    bdw = sb("bdw", [128, 2, 128], BF16)
    bg24 = sb("bg24", [24, 128])
    bgh = sb("bgh", [128, 24])
    lng = sb("lng", [128, D])
    lnb = sb("lnb", [128, D])
    esink = sb("esink", [128, 8])
    esink_h = sb("esink_h", [128, 8])
    xT2 = sb("xT2", [128, 2, 8, 128], BF16)
    xT = xT2[:, 0, :, :]
    sz = sb("sz", [128, 1024])
    pooled_bf = sb("pooled_bf", [128, 256], BF16)
    pooledT = sb("pooledT", [128, 2, 128], BF16)
    B_tm = sb("B_tm", [128, 1024], BF16)
    st6 = sb("st6", [128, 12])
    mv = sb("mv", [128, 8])
    vn_bf = sb("vn_bf", [128, 256], BF16)
    q_bf = sb("q_bf", [128, 512], BF16)
    kdup = sb("kdup", [128, 2, 128], BF16)
    qT = sb("qT", [128, 4, 128], BF16)
    den = sb("den", [128, 8])
    rden = sb("rden", [128, 8])
    gsb = sb("gsb", [128, 1024])
    acc = sb("acc", [128, 1024])
    tmp = sb("tmp", [128, 1024])
    PTm = sb("PTm", [128, 2, 1024], BF16)
    PT = PTm
    m_bf = PTm[:, 0, :]
    mT = PTm[:, 1, :].rearrange("p (k t) -> p k t", k=8)
    qk = acc[:, 0:640]
    vtmp = acc[:, 640:896]
    rb = gsb[:, 0:640]
    uz = gsb[:, 640:896]
    yc = tmp[:, 512:1024]
    xa_f = tmp[:, 0:256]
    vr_f = tmp[:, 256:384]
    sguW = sb("sguW_s", [128, 4, 128])
    tril = sb("tril_s", [128, 128])
    sgb4 = sb("sgb4_s", [4, 128])
    bdwf = sb("bdwf_s", [128, 2, 128])
    pscale = sb("pscale_s", [128, 256])

    KTs = sb("KTs", [128, NS, 128], BF16)
    Vda = sb("Vda", [128, NS // 2, 2, 128], BF16)
    Vdb = Bg[:, 0:2, :, :].rearrange("p a c f -> p (a c) f").rearrange("p (b kv) f -> p b kv f", kv=2)
    Ks = PTm[:, :, :].rearrange("p a (b f) -> p (a b) f", f=128)
    hist = sb("hist", [120, 2, 256], BF16)
    Qexp = sb("Qexp", [NS, 8, 128], BF16)
    Qblk = sb("Qblk", [128, NS, 8], BF16)
    PTs = sb("PTs", [128, 128], BF16)
    Pself = sb("Pself", [NS, 128], BF16)
    Pself_f = tmp[0:NS, 256:384]
    vdn = sb("vdn", [NS, 2, 128], BF16)
    kTn = sb("kTn", [128, NS], BF16)
    sw00 = sb("sw00", [NS, 4])
    sb0 = sb("sb0", [NS, 4])
    rden_s = tmp[:, 0:128]
    Rn = tmp[:, 128:256]

    ps = stack.enter_context(nc.psum_tensor("ps", [128, 8, 512], F32))

    def psb(bank):
        return ps[:, bank, :].bitcast(BF16)

    B = {}

    def bf(name):
        if name not in B:
            B[name] = S.buf(name)
        return B[name]

    PSB = [bf("ps%d" % i) for i in range(8)]
    for _b in PSB:
        _b.exclusive = True

    S.dma("sp", identf[:], c_ident[:, :], bf("identf"), writes=[bf("identf")])
    S.op("dve", lambda e: e.tensor_copy(out=identb[:], in_=identf[:]), reads=[bf("identf")], writes=[bf("identb")])
    S.dma("pool", poolP[:], c_poolP.rearrange("v g s t -> s (v g) t"), bf("poolP"), writes=[bf("poolP")])
    S.dma("pool", maskb[:], c_mask.rearrange("v s t -> s v t"), bf("maskb"), writes=[bf("maskb")])
    S.dma("sp", cs_s[:], c_cs_s[:, :], bf("cs_s"), writes=[bf("cs_s")])
    S.dma("sp", tril[:], c_tril[:, :], bf("stg_tril"), writes=[bf("stg_tril")])
    S.dma("pool", selb[:], c_sel.rearrange("g c r b -> r (g c) b"), bf("selb"), writes=[bf("selb")])
    S.dma("sp", dmask[:], c_dmask[:, :], bf("dmask"), writes=[bf("dmask")])
    S.op("dve", lambda e: e.memset(ones2[:], 1.0), writes=[bf("ones2")])
    S.op("dve", lambda e: e.memset(onesb[:], 1.0), writes=[bf("onesb")])
    S.op("dve", lambda e: e.memset(chalf[:], -0.5), writes=[bf("chalf")])
    S.op("dve", lambda e: e.memset(Qexp[:], 0.0), writes=[bf("Qexp")])

    def WB(*idx):
        return [bf("WB%d" % i) for i in idx]

    def w1_block(l, b):
        wv = w_in[l].rearrange("(k p) n -> p k n", p=128)
        if b == 0:
            S.dma("pool", Wbuf[:, :, 0:512], wv[:, :, 1280:1792], bf("WB0"), writes=WB(0))
        elif b == 1:
            S.dma("pool", Wbuf[:, :, 512:768], wv[:, :, 1792:2048], bf("WB1"), writes=WB(1))
            S.dma("pool", Wbuf[:, :, 768:1024], wv[:, :, 1024:1280], bf("WB1"), writes=WB(1))
        elif b == 2:
            S.dma("pool", Wbuf[:, :, 1024:1536], wv[:, :, 0:512], bf("WB2"), writes=WB(2))
        elif b == 3:
            S.dma("pool", Wbuf[:, :, 1536:2048], wv[:, :, 512:1024], bf("WB3"), writes=WB(3))
        else:
            S.dma("pool", Wbuf[:, :, 2048:2560], wv[:, :, 2048:2560], bf("WB4"), writes=WB(4))

    def w2_block(l, i):
        wv = w_in[l].rearrange("(k p) n -> p k n", p=128)
        S.dma("pool", Wbuf[:, :, i * 512:(i + 1) * 512], wv[:, :, 2560 + i * 512:2560 + (i + 1) * 512],
              bf("WB%d" % i), writes=WB(i))

    def load_w1(l):
        for b in range(5):
            w1_block(l, b)
        return
        wv = w_in[l].rearrange("(k p) n -> p k n", p=128)
        S.dma("pool", Wbuf[:, :, 0:512], wv[:, :, 1280:1792], bf("WB0"), writes=WB(0))
        S.dma("pool", Wbuf[:, :, 512:768], wv[:, :, 1792:2048], bf("WB1"), writes=WB(1))
        S.dma("pool", Wbuf[:, :, 768:1024], wv[:, :, 1024:1280], bf("WB1"), writes=WB(1))
        S.dma("pool", Wbuf[:, :, 1024:1536], wv[:, :, 0:512], bf("WB2"), writes=WB(2))
        S.dma("pool", Wbuf[:, :, 1536:2048], wv[:, :, 512:1024], bf("WB3"), writes=WB(3))
        S.dma("pool", Wbuf[:, :, 2048:2560], wv[:, :, 2048:2560], bf("WB4"), writes=WB(4))

    def w2_upper(l, i):
        wv = w_in[l].rearrange("(k p) n -> p k n", p=128)
        if i == 0:
            S.dma("pool", Wbuf[:, :, 2560:3072], wv[:, :, 2560 + 2560:2560 + 3072], bf("WB5"), writes=WB(5))
        elif i == 1:
            S.dma("pool", Wp[:, 0:2, :], w_pa[l].rearrange("(k p) n -> p k n", p=128), bf("WB6"), writes=WB(6, 7))
        elif i == 2:
            S.dma("pool", Wp[:, 2:4, :], w_pb[l].rearrange("(k p) n -> p k n", p=128), bf("WB6"), writes=WB(6, 7))
        elif i == 3:
            S.dma("pool", Wp[:, 4:8, :], w_pc[l].rearrange("(k p) n -> p k n", p=128), bf("WB6"), writes=WB(6, 7))
        else:
            S.dma("pool", Wo, w_out[l].rearrange("(k p) n -> p k n", p=128), bf("WB8"), writes=WB(8, 9))

    def load_w2(l, part):
        wv = w_in[l].rearrange("(k p) n -> p k n", p=128)
        if part == 0:
            S.dma("pool", Wbuf[:, :, 2560:3072], wv[:, :, 2560 + 2560:2560 + 3072], bf("WB5"), writes=WB(5))
            S.dma("pool", Wp[:, 0:2, :], w_pa[l].rearrange("(k p) n -> p k n", p=128), bf("WB6"), writes=WB(6, 7))
            S.dma("pool", Wp[:, 2:4, :], w_pb[l].rearrange("(k p) n -> p k n", p=128), bf("WB6"), writes=WB(6, 7))
            S.dma("pool", Wp[:, 4:8, :], w_pc[l].rearrange("(k p) n -> p k n", p=128), bf("WB6"), writes=WB(6, 7))
            S.dma("pool", Wo, w_out[l].rearrange("(k p) n -> p k n", p=128), bf("WB8"), writes=WB(8, 9))
            return
        for i in range(5):
            S.dma("pool", Wbuf[:, :, i * 512:(i + 1) * 512], wv[:, :, 2560 + i * 512:2560 + (i + 1) * 512],
                  bf("WB%d" % i), writes=WB(i))
        return
        for i in range(6):
            S.dma("pool", Wbuf[:, :, i * 512:(i + 1) * 512], wv[:, :, 2560 + i * 512:2560 + (i + 1) * 512],
                  bf("WB%d" % i), writes=WB(i))
            if i == 1:
                S.dma("pool", Wp[:, 0:2, :], w_pa[l].rearrange("(k p) n -> p k n", p=128), bf("WB6"), writes=WB(6, 7))
            elif i == 3:
                S.dma("pool", Wp[:, 2:4, :], w_pb[l].rearrange("(k p) n -> p k n", p=128), bf("WB6"), writes=WB(6, 7))
            elif i == 5:
                S.dma("pool", Wp[:, 4:8, :], w_pc[l].rearrange("(k p) n -> p k n", p=128), bf("WB6"), writes=WB(6, 7))
        S.dma("pool", Wo, w_out[l].rearrange("(k p) n -> p k n", p=128), bf("WB8"), writes=WB(8, 9))

    SW, ST, SP_, SG = bf("stg_sguW"), bf("stg_tril"), bf("stg_pscale"), bf("stg_sgb4")
    SBD = [bf("stg_bdw%d" % g) for g in range(4)]

    def consts_prefetch(l):
        S.dma("sp", sguW[:], sgu_w[l].rearrange("g t s -> t g s"), SW, writes=[SW])
        S.dma("sp", sgb4[:], sgu_b[l], SG, writes=[SG])
        S.dma("sp", pscale[:], pool_scale[l].partition_broadcast(128), SP_, writes=[SP_])
        S.op("dve", lambda e: e.memset(bdwf[:], 0.0), writes=SBD)
        for g in range(4):
            c, j = g // 2, g % 2
            S.dma("sp", bdwf[j * 64:(j + 1) * 64, c, j * 64:(j + 1) * 64], pool_w[l, g], SBD[g], writes=[SBD[g]])
        S.dma("sp", slng[:], sgu_ln_g[l].partition_broadcast(128), bf("slng"), writes=[bf("slng")])
        S.dma("sp", slnb[:], sgu_ln_b[l].partition_broadcast(128), bf("slnb"), writes=[bf("slnb")])
        S.dma("sp", esink_h[:], sinks[l].partition_broadcast(128), bf("esink_h"), writes=[bf("esink_h")])
        S.dma("sp", bg24[:], b_gate[l].rearrange("(n p) -> n p", p=128), bf("bg24"), writes=[bf("bg24")])

    def consts_compute_p1(l):
        S.op("dve", lambda e: e.tensor_tensor(out=sguW[:], in0=sguW[:],
                                              in1=tril[:].unsqueeze(1).broadcast_to([128, 4, 128]), op=ALU.mult),
             reads=[ST, SW], writes=[SW])

        def tr(e):
            for g in range(4):
                ins = e.transpose(ps[:, 7, g * 128:(g + 1) * 128], sguW[:, g, :], identf[:])
            return ins
        S.op("pe", tr, reads=[SW, bf("identf")], writes=[PSB[7]])
        S.op("act", lambda e: e.activation(out=sguWT[:].rearrange("p g t -> p (g t)"), in_=ps[:, 7, :], func=AF.Copy),
             reads=[PSB[7]], writes=[bf("sguWT")])
        S.op("pe", lambda e: e.transpose(ps[:, 6, 0:4], sgb4[:], identf[0:4, 0:4]), reads=[SG, bf("identf")],
             writes=[PSB[6]])
        S.op("dve", lambda e: e.tensor_copy(out=sgub[:].rearrange("p (g c) -> p g c", g=4),
                                            in_=ps[:, 6, 0:4].unsqueeze(2).broadcast_to([128, 4, 64])),
             reads=[PSB[6]], writes=[bf("sgub")])
        S.op("dve", lambda e: e.tensor_tensor(out=bdw[:].rearrange("p c d -> p (c d)"),
                                              in0=bdwf[:].rearrange("p c d -> p (c d)"), in1=pscale[:], op=ALU.mult),
             reads=SBD + [SP_], writes=[bf("bdw")])
        S.op("act", lambda e: e.activation(out=esink_h[:], in_=esink_h[:], func=AF.Exp), reads=[bf("esink_h")],
             writes=[bf("esink_h")])
        S.op("dve", lambda e: e.tensor_copy(out=esink[:].rearrange("p (j kv ci) -> p j kv ci", kv=2, j=2),
                                            in_=esink_h[:].rearrange("p (kv ci j) -> p j kv ci", kv=2, ci=2)),
             reads=[bf("esink_h")], writes=[bf("esink")])

    def consts_p2(l):
        S.op("pe", lambda e: e.transpose(ps[:, 5, 0:24], bg24[:], identf[0:24, 0:24]), reads=[bf("bg24"), bf("identf")],
             writes=[PSB[5]])
        S.op("dve", lambda e: e.tensor_scalar(out=bgh[:], in0=ps[:, 5, 0:24], scalar1=0.5, scalar2=None, op0=ALU.mult),
             reads=[PSB[5]], writes=[bf("bgh")])

    def consts_late(l):
        S.dma("sp", lng[:], ln_g[l].partition_broadcast(128), bf("lng"), writes=[bf("lng")])
        S.dma("sp", lnb[:], ln_b[l].partition_broadcast(128), bf("lnb"), writes=[bf("lnb")])

    def ACT(out, in_, func, R, W, **kw):
        return S.op("act", lambda e: e.activation(out=out, in_=in_, func=func, **kw), R, W)

    def TT(out, in0, in1, op, R, W, eng="dve"):
        return S.op(eng, lambda e: e.tensor_tensor(out=out, in0=in0, in1=in1, op=op), R, W)

    def TS(out, in0, s1, s2, op0, op1, R, W):
        if op1 is None:
            return S.op("dve", lambda e: e.tensor_scalar(out=out, in0=in0, scalar1=s1, scalar2=None, op0=op0), R, W)
        return S.op("dve", lambda e: e.tensor_scalar(out=out, in0=in0, scalar1=s1, scalar2=s2, op0=op0, op1=op1), R, W)

    def STT(out, in0, scalar, in1, op0, op1, R, W):
        return S.op("dve", lambda e: e.scalar_tensor_tensor(out=out, in0=in0, scalar=scalar, in1=in1, op0=op0, op1=op1), R, W)

    def CP(out, in_, R, W):
        return S.op("dve", lambda e: e.tensor_copy(out=out, in_=in_), R, W)

    EVENG = dict(cast="act", xT="act", kdup="dve", V="act", xa="act", pooled="dve", pooledT="act", qT="act", kT="act", BT="act")
    EVENG.update(dbg.get("eveng") or {})

    def EV(key, out, in_, R, W, scale=None):
        if EVENG[key] == "act":
            if scale is None:
                return ACT(out, in_, AF.Copy, R, W)
            return ACT(out, in_, AF.Copy, R, W, scale=scale)
        if scale is None:
            return CP(out, in_, R, W)
        return TS(out, in_, scale, None, ALU.mult, None, R, W)

    def MM(lst, R, W):
        def fn(e):
            for (o, a, b, st, sp) in lst:
                ins = e.matmul(o, lhsT=a, rhs=b, start=st, stop=sp)
            return ins
        return S.op("pe", fn, R, W)

    def TR(lst, R, W):
        def fn(e):
            for (o, a, idn) in lst:
                ins = e.transpose(o, a, idn)
            return ins
        return S.op("pe", fn, R, W)

    def flat(ap3):
        return ap3.rearrange("p a b -> p (a b)")

    def layer_norm_stats(src_chunks, R):
        n = len(src_chunks)
        p = src_chunks[0].shape[0]
        for i, c in enumerate(src_chunks):
            S.op("dve", lambda e, c=c, i=i: e.bn_stats(st6[0:p, i * 6:(i + 1) * 6], c), R, [bf("st6")])
        S.op("dve", lambda e: e.bn_aggr(mv[0:p, 0:2], st6[0:p, 0:6 * n]), [bf("st6")], [bf("mv")])
        TS(mv[0:p, 2:3], mv[0:p, 1:2], LN_EPS, None, ALU.add, None, [bf("mv")], [bf("mv")])
        TT(mv[0:p, 3:4], mv[0:p, 2:3], chalf[0:p, :], ALU.pow, [bf("mv"), bf("chalf")], [bf("mv")], eng="pool")
        STT(mv[0:p, 4:5], mv[0:p, 0:1], -1.0, mv[0:p, 3:4], ALU.mult, ALU.mult, [bf("mv")], [bf("mv")])

    xa_ring = sb("xa_ring", [128, 4, 256], BF16)
    kT_ring = sb("kT_ring", [128, 4, 2, 128], BF16)
    V_ring = sb("V_ring", [128, 6, 2, 65], BF16)
    S.op("dve", lambda e: e.memset(V_ring[:, :, :, 64:65], 1.0), writes=[bf("V%d" % i) for i in range(6)])

    def p1_steps(l, gi, t):
        ti = gi * G + t
        slot = l * 2 + ti % 2
        pslot = l * 2 + 1 - ti % 2
        vslot = l * 3 + ti % 3
        vpslot = l * 3 + (ti - 1) % 3
        has_prev = ti > 0
        last = ti == NT - 1
        XR = bf("xres%d" % t)
        IDB = bf("identb")
        XA, XAP = bf("xa%d" % slot), bf("xa%d" % pslot)
        VS, VP = bf("V%d" % vslot), bf("V%d" % vpslot)
        KS, KP = bf("kT%d" % slot), bf("kT%d" % pslot)
        BGB = bf("Bg%d" % t)
        xstage = flat(Bg[:, t, :, :])
        H, T = [], []

        HB = dbg.get("hb") or [1, 2, 3, 1, 2]
        PB_ = dbg.get("poolbank", 0)
        SB_ = dbg.get("sgubank", 6)

        XSL = (ti % 2) if dbg.get("xT2slots", 1) else 0
        xTs = xT2[:, XSL, :, :]
        XBF = [bf("xTa"), bf("xTb")] if XSL == 0 else [bf("xTc")]

        def mm_group(bank, j):
            MM([(ps[:, bank, :], xTs[:, k, :], W1[:, k, j * 512:(j + 1) * 512], k == 0, k == 7) for k in range(8)],
               XBF + [bf("WB%d" % j)], [PSB[bank]])

        def h0a():
            EV("cast", xstage, x_res[:, t, :], [XR], [BGB])

        def h0():
            TR([(psb(0)[:, k * 128:(k + 1) * 128], xstage[:, k * 128:(k + 1) * 128], identb[:]) for k in range(8)],
               [BGB, IDB], [PSB[0]])
            EV("xT", xTs.rearrange("p k t -> p (k t)"), psb(0)[:, :], [PSB[0]], XBF)

        cos_q = cs_t[:, 0:32].unsqueeze(1).unsqueeze(1).broadcast_to([128, 8, 2, 32])
        sin_q = cs_t[:, 32:64].unsqueeze(1).broadcast_to([128, 8, 32])
        cos_k = cs_t[:, 0:32].unsqueeze(1).unsqueeze(1).broadcast_to([128, 2, 2, 32])
        sin_k = cs_t[:, 32:64].unsqueeze(1).broadcast_to([128, 2, 32])

        def h_q():
            S.dma("sp", cs_t[:], c_cs[ti * 128:(ti + 1) * 128, :], bf("cs_t"), writes=[bf("cs_t")])
            mm_group(HB[0], 0)
            q4 = ps[:, HB[0], :].rearrange("p (h two f) -> p h two f", h=8, two=2)
            a4 = rb[:, 0:512].rearrange("p (h two f) -> p h two f", h=8, two=2)
            qb4 = q_bf[:].rearrange("p (h two f) -> p h two f", h=8, two=2)
            t1 = tmp[:, 0:256].rearrange("p (h f) -> p h f", h=8)
            t2 = tmp[:, 256:512].rearrange("p (h f) -> p h f", h=8)
            TT(a4, q4, cos_q, ALU.mult, [PSB[HB[0]], bf("cs_t")], [bf("rb")])
            TT(t1, q4[:, :, 1, :], sin_q, ALU.mult, [PSB[HB[0]], bf("cs_t")], [bf("tmp")])
            TT(t2, q4[:, :, 0, :], sin_q, ALU.mult, [PSB[HB[0]], bf("cs_t")], [bf("tmp")])
            TT(qb4[:, :, 0, :], a4[:, :, 0, :], t1, ALU.subtract, [bf("rb"), bf("tmp")], [bf("q_bf")])
            TT(qb4[:, :, 1, :], a4[:, :, 1, :], t2, ALU.add, [bf("rb"), bf("tmp")], [bf("q_bf")])

        def h_kvz():
            mm_group(HB[1], 1)
            k4 = ps[:, HB[1], 0:128].rearrange("p (h two f) -> p h two f", h=2, two=2)
            kr4 = rb[:, 512:640].rearrange("p (h two f) -> p h two f", h=2, two=2)
            u1 = acc[:, 0:64].rearrange("p (h f) -> p h f", h=2)
            u2 = acc[:, 64:128].rearrange("p (h f) -> p h f", h=2)
            TT(kr4, k4, cos_k, ALU.mult, [PSB[HB[1]], bf("cs_t")], [bf("rbk")])
            TT(u1, k4[:, :, 1, :], sin_k, ALU.mult, [PSB[HB[1]], bf("cs_t")], [bf("acc")])
            TT(u2, k4[:, :, 0, :], sin_k, ALU.mult, [PSB[HB[1]], bf("cs_t")], [bf("acc")])
            TT(kr4[:, :, 0, :], kr4[:, :, 0, :], u1, ALU.subtract, [bf("rbk"), bf("acc")], [bf("rbk")])
            TT(kr4[:, :, 1, :], kr4[:, :, 1, :], u2, ALU.add, [bf("rbk"), bf("acc")], [bf("rbk")])
            EV("kdup", kdup[:].rearrange("p kv (u d) -> p kv u d", u=2),
               rb[:, 512:640].rearrange("p (kv d) -> p kv d", kv=2).unsqueeze(2).broadcast_to([128, 2, 2, 64]),
               [bf("rbk")], [bf("kdup")])
            if last:
                S.dma("sp", o_k_p[l], rb[:, 512:640], bf("rbout"), reads=[bf("rbk")], final=True)
            EV("V", V_ring[:, vslot, :, 0:64], ps[:, HB[1], 128:256].rearrange("p (kv d) -> p kv d", kv=2), [PSB[HB[1]]], [VS])
            if last:
                ACT(vr_f[:], ps[:, HB[1], 128:256], AF.Copy, [PSB[HB[1]]], [bf("tmp")])
                S.dma("sp", o_v_p[l], vr_f[:], bf("vr_f"), reads=[bf("tmp")], final=True)
            ACT(sz[:, 256:512], ps[:, HB[1], 256:512], AF.Tanh, [PSB[HB[1]]], [bf("szb")], scale=0.5)
            STT(sz[:, 256:512], sz[:, 256:512], 1.0, ps[:, HB[1], 256:512], ALU.add, ALU.mult, [bf("szb"), PSB[HB[1]]], [bf("szb")])

        def h_xaza():
            mm_group(HB[2], 2)
            EV("xa", xa_ring[:, slot, :], ps[:, HB[2], 0:256], [PSB[HB[2]]], [XA])
            if last:
                ACT(xa_f[:], ps[:, HB[2], 0:256], AF.Copy, [PSB[HB[2]]], [bf("tmp")])
                S.dma("sp", o_pool_p[l], xa_f[113:128, :], bf("xa_f"), reads=[bf("tmp")], final=True)
            ACT(sz[:, 0:256], ps[:, HB[2], 256:512], AF.Tanh, [PSB[HB[2]]], [bf("sza")], scale=0.5)
            STT(sz[:, 0:256], sz[:, 0:256], 1.0, ps[:, HB[2], 256:512], ALU.add, ALU.mult, [bf("sza"), PSB[HB[2]]], [bf("sza")])

        def h_uv():
            mm_group(HB[3], 3)
            layer_norm_stats([ps[:, HB[3], 256:512]], [PSB[HB[3]]])
            TS(vtmp[:], ps[:, HB[3], 256:512], mv[:, 0:1], mv[:, 3:4], ALU.subtract, ALU.mult, [PSB[HB[3]], bf("mv")], [bf("vt")])
            TT(vtmp[:], vtmp[:], slng[:], ALU.mult, [bf("vt"), bf("slng")], [bf("vt")])
            TT(vn_bf[:], vtmp[:], slnb[:], ALU.add, [bf("vt"), bf("slnb")], [bf("vn_bf")])
            TT(uz[:], ps[:, HB[3], 0:256], sz[:, 256:512], ALU.mult, [PSB[HB[3]], bf("szb")], [bf("uz")])

        def h_zc():
            mm_group(HB[4], 4)
            ACT(sz[:, 512:1024], ps[:, HB[4], :], AF.Tanh, [PSB[HB[4]]], [bf("szc")], scale=0.5)
            STT(sz[:, 512:1024], sz[:, 512:1024], 1.0, ps[:, HB[4], :], ALU.add, ALU.mult, [bf("szc"), PSB[HB[4]]], [bf("szc")])

        H.extend([h0, h_q, h_kvz, h_xaza, h_uv, h_zc, h0a])

        def t_pool1():
            lst = []
            for g in range(4):
                o = ps[:, PB_, g * 64:(g + 1) * 64]
                lst.append((o, poolP[:, (0 if ti == 0 else 4) + g, :], xa_ring[:, slot, g * 64:(g + 1) * 64], True, not has_prev))
                if has_prev:
                    lst.append((o, poolP[:, 8 + g, :], xa_ring[:, pslot, g * 64:(g + 1) * 64], False, True))
            MM(lst, [XA, bf("poolP")] + ([XAP] if has_prev else []), [PSB[PB_]])
            EV("pooled", pooled_bf[:], ps[:, PB_, 0:256], [PSB[PB_]], [bf("pooled_bf")])

        def t_pool2():
            TR([(psb(5)[:, c * 128:(c + 1) * 128], pooled_bf[:, c * 128:(c + 1) * 128], identb[:]) for c in range(2)],
               [bf("pooled_bf"), IDB], [PSB[5]])
            EV("pooledT", flat(pooledT[:]), psb(5)[:, 0:256], [PSB[5]], [bf("pooledT")])

        def t_pool3():
            MM([(ps[:, PB_, 256 + c * 128:256 + (c + 1) * 128], pooledT[:, c, :], bdw[:, c, :], True, True) for c in range(2)],
               [bf("pooledT"), bf("bdw")], [PSB[PB_]])
            TT(B_tm[:, 0:256], ps[:, PB_, 256:512], sz[:, 0:256], ALU.mult, [PSB[PB_], bf("sza")], [bf("Btm_a")])

        def t_sgu():
            MM([(ps[:, SB_, g * 64:(g + 1) * 64], sguWT[:, g, :], vn_bf[:, g * 64:(g + 1) * 64], True, True) for g in range(4)],
               [bf("sguWT"), bf("vn_bf")], [PSB[SB_]])
            TT(vtmp[:], ps[:, SB_, 0:256], sgub[:], ALU.add, [PSB[SB_], bf("sgub")], [bf("vt")])
            TT(B_tm[:, 256:512], vtmp[:], uz[:], ALU.mult, [bf("vt"), bf("uz")], [bf("Btm_b")])

        def t_qkT():
            TR([(psb(7)[:, c * 128:(c + 1) * 128], q_bf[:, c * 128:(c + 1) * 128], identb[:]) for c in range(4)],
               [bf("q_bf"), IDB], [PSB[7]])
            TR([(psb(5)[:, 256 + kv * 128:256 + (kv + 1) * 128], kdup[:, kv, :], identb[:]) for kv in range(2)],
               [bf("kdup"), IDB], [PSB[5]])
            EV("qT", flat(qT[:]), psb(7)[:, 0:512], [PSB[7]], [bf("qT")])
            EV("kT", flat(kT_ring[:, slot, :, :]), psb(5)[:, 256:512], [PSB[5]], [KS])

        positions = [(0, slot, vslot, KS, VS, 6)] + ([(1, pslot, vpslot, KP, VP, 4)] if has_prev else [])

        def t_scores(pos):
            pi, sl, vs_, KB, VB, base = pos

            def fn():
                lst = []
                for kv in range(2):
                    for j in range(2):
                        lst.append((ps[:, base + j, kv * 256:(kv + 1) * 256], kT_ring[j * 64:(j + 1) * 64, sl, kv, :],
                                    qT[j * 64:(j + 1) * 64, 2 * kv:2 * kv + 2, :], True, True))
                MM(lst, [KB, bf("qT")], [PSB[base], PSB[base + 1]])
                ACT(PT[:, pi, :], ps[:, base:base + 2, :].rearrange("p b c -> p (b c)"), AF.Exp,
                    [PSB[base], PSB[base + 1]], [bf("PT%d" % pi)], scale=0.125)
                TT(PT[:, pi, :].rearrange("p (r t) -> p r t", r=8), PT[:, pi, :].rearrange("p (r t) -> p r t", r=8),
                   maskb[:, pi, :].unsqueeze(1).broadcast_to([128, 8, 128]), ALU.mult,
                   [bf("PT%d" % pi), bf("maskb")], [bf("PT%d" % pi)])
            return fn

        def t_pv():
            lst = []
            npos = len(positions)
            for r in range(8):
                kv = (r // 2) % 2
                o = ps[:, 6 + r // 4, (r % 4) * 65:(r % 4) * 65 + 65]
                for (pi, sl, vs_, KB, VB, base) in positions:
                    lst.append((o, PT[:, pi, r * 128:(r + 1) * 128], V_ring[:, vs_, kv, :], pi == 0, pi == npos - 1))
            MM(lst, [bf("PT0")] + ([bf("PT1")] if has_prev else []) + [p_[4] for p_ in positions], [PSB[6], PSB[7]])
            for b in range(2):
                pv = ps[:, 6 + b, 0:260].rearrange("p (r c) -> p r c", c=65)
                TT(den[:, b * 4:(b + 1) * 4], pv[:, :, 64], esink[:, b * 4:(b + 1) * 4], ALU.add,
                   [PSB[6 + b], bf("esink")], [bf("den")])
            S.op("dve", lambda e: e.reciprocal(out=rden[:], in_=den[:]), [bf("den")], [bf("rden")])
            for b in range(2):
                pv = ps[:, 6 + b, 0:260].rearrange("p (r c) -> p r c", c=65)
                TT(yc.rearrange("p (kv ci j d) -> p j kv ci d", kv=2, ci=2, j=2)[:, b],
                   pv[:, :, 0:64].rearrange("p (kv ci) d -> p kv ci d", kv=2),
                   rden[:, b * 4:(b + 1) * 4].rearrange("p (kv ci) -> p kv ci", kv=2).unsqueeze(3).broadcast_to([128, 2, 2, 64]),
                   ALU.mult, [PSB[6 + b], bf("rden")], [bf("tmp")])
            TT(B_tm[:, 512:1024], yc, sz[:, 512:1024], ALU.mult, [bf("tmp"), bf("szc")], [bf("Btm_c")])

        def t_BT():
            TR([(psb(4)[:, c * 128:(c + 1) * 128], B_tm[:, c * 128:(c + 1) * 128], identb[:]) for c in range(8)],
               [bf("Btm_a"), bf("Btm_b"), bf("Btm_c"), IDB], [PSB[4]])
            EV("BT", flat(Bg[:, t, :, :]), psb(4)[:, :], [PSB[4]], [BGB], scale=0.5)

        T.extend([t_pool1, t_pool2, t_pool3, t_sgu, t_qkT] + [t_scores(p_) for p_ in positions] + [t_pv, t_BT])
        return H, T

    bar_t = sb("bar_t", [128, 1])
    ALIASED = ["acc", "gsb", "tmp", "vt", "uz", "rb", "rbk", "th0", "th1", "ac0", "ac1",
               "sza", "szb", "szc", "stgA", "stgB", "stgC", "stgD"]

    def phase_barrier():
        bl = [bf(n) for n in ALIASED]
        S.op("dve", lambda e: e.memset(bar_t[:], 0.0), bl, bl + [bf("bar_t")])

    P1_ORDER = dbg.get("p1order") or ["qkT", "sgu", "pool1", "h_q", "pool2", "sc0", "sc1", "pool3", "cast", "h_kvz", "h_xaza",
                                      "xT", "pv", "h_uv", "h_zc", "BT"]

    def p1_phase(l, gi, ntiles, hooks=None, upper=None):
        Hs, Ts = zip(*[p1_steps(l, gi, t) for t in range(ntiles)])
        def hooked(h_steps):
            h0, h_q, h_kvz, h_xaza, h_uv, h_zc, h0a = h_steps
            if hooks is None:
                return h_steps
            def h_zc_h():
                h_zc()
                for b in range(5):
                    hooks[b]()
            return (h0, h_q, h_kvz, h_xaza, h_uv, h_zc_h, h0a)

        Hs = list(Hs)
        Hs[ntiles - 1] = hooked(Hs[ntiles - 1])
        if upper is not None:
            for i in range(min(5, ntiles)):
                h = list(Hs[i])
                h[4] = (lambda f=h[4], i=i: (f(), upper(i)))
                Hs[i] = tuple(h)
            for i in range(ntiles, 5):
                pass
        Hs[0][6]()
        for f in Hs[0][:6]:
            f()
        if ntiles > 1:
            Hs[1][6]()
            Hs[1][0]()
        for t in range(ntiles):
            T = list(Ts[t])
            if t + 1 < ntiles:
                h0, h_q, h_kvz, h_xaza, h_uv, h_zc, h0a = Hs[t + 1]
                cast_next = Hs[t + 2][6] if t + 2 < ntiles else (lambda: None)
                xT_next = Hs[t + 2][0] if t + 2 < ntiles else (lambda: None)
                names = ["pool1", "pool2", "pool3", "sgu", "qkT"] + ["sc%d" % i for i in range(len(T) - 7)] + ["pv", "BT"]
                tm = dict(zip(names, T))
                tm.update(h_q=h_q, h_kvz=h_kvz, h_xaza=h_xaza, h_uv=h_uv, h_zc=h_zc, cast=cast_next, xT=xT_next)
                for nme in P1_ORDER:
                    if nme in tm:
                        tm[nme]()
            else:
                names = ["pool1", "pool2", "pool3", "sgu", "qkT"] + ["sc%d" % i for i in range(len(T) - 7)] + ["pv", "BT"]
                tm = dict(zip(names, T))
                if dbg.get("tailorder", 1) == 0:
                    order = [tm["qkT"], tm["sc0"]] + ([tm["sc1"]] if "sc1" in tm else [])
                    order += [tm["pool1"], tm["pool2"], tm["pool3"], tm["sgu"], tm["pv"], tm["BT"]]
                else:
                    order = [tm["qkT"], tm["sgu"], tm["pool1"], tm["pool2"], tm["sc0"]] + ([tm["sc1"]] if "sc1" in tm else [])
                    order += [tm["pool3"], tm["pv"], tm["BT"]]
                for f in order:
                    f()

    def resid_ln(l, p, xr, XR, ob):
        STT(xr, xr, ALPHA, ps[0:p, ob:ob + 2, :].rearrange("p b c -> p (b c)"), ALU.mult, ALU.add,
            [XR, PSB[ob], PSB[ob + 1]], [XR])
        layer_norm_stats([xr[:, 0:512], xr[:, 512:1024]], [XR])
        ACT(xr, xr, AF.Identity, [XR, bf("mv")], [XR], scale=mv[0:p, 3:4], bias=mv[0:p, 4:5])
        TT(xr, xr, lng[0:p, :], ALU.mult, [XR, bf("lng")], [XR])
        TT(xr, xr, lnb[0:p, :], ALU.add, [XR, bf("lnb")], [XR])

    pair_ctr = [0]

    def pair():
        b = 2 * (pair_ctr[0] % 4)
        pair_ctr[0] += 1
        return b

    CH = {0: [0, 1], 1: [2, 3], 2: [4, 5, 6, 7]}

    def p2_phase(l, gi, ntiles, with_sample, hooks=None, post_ln=None):
        szb16 = sz[:].bitcast(BF16)
        tmpb16 = tmp[:].bitcast(BF16)
        xa = xT2[:].rearrange("p a k t -> p (a k t)").rearrange("p (k t) -> p k t", k=4)
        xb = PTm[:].rearrange("p a c -> p (a c)").rearrange("p (k t) -> p k t", k=4)
        XA = [bf("xTa"), bf("xTb"), bf("xTc")]
        XB = [bf("PT0"), bf("PT1")]
        BT3 = [bf("Btm_a"), bf("Btm_b"), bf("Btm_c")]
        TH = [bf("th0"), bf("th1")]
        AC = [bf("ac0"), bf("ac1")]
        SZ3 = [bf("sza"), bf("szb"), bf("szc")]

        def make_batch(kind, tiles):
            if kind == "t":
                nt = len(tiles)
                N = 128 * nt
                t0 = tiles[0]
                d = dict(N=N, tiles=tiles,
                         xk=lambda k: (xa if k < 4 else xb)[:, k % 4, 0:N], XBUFS=XA + XB,
                         brhs=lambda f: Bg[:, t0:t0 + nt, f, :], BBUFS=[bf("Bg%d" % t) for t in tiles],
                         mch=lambda c: (szb16 if c < 4 else tmpb16)[:, (c % 4) * 512:(c % 4) * 512 + N],
                         MB=lambda c: (SZ3 if c < 4 else [bf("tmp")]))
            else:
                N = NS
                d = dict(N=N, tiles=None,
                         xk=lambda k: xT2[:, 0, k, 0:N], XBUFS=[bf("xTa"), bf("xTb")],
                         brhs=lambda f: Bg[:, G, f, 0:N], BBUFS=[bf("Bg%d" % G)],
                         mch=lambda c: mT[:, c, 0:N], MB=lambda c: [bf("PT1")])
            return d

        def gen_xT(bt, dead=None):
            if bt["tiles"] is not None and dead is not None and len(dead) >= len(bt["tiles"]) and dbg.get("deadstage", 1):
                tl_ = bt["tiles"]
                if dead == "first":
                    stg = [(szb16[:, 0:1024], bf("stgA")), (szb16[:, 1024:2048], bf("stgB")),
                           (tmpb16[:, 0:1024], bf("stgC")), (tmpb16[:, 1024:2048], bf("stgD"))][:len(tl_)]
                else:
                    stg = [(flat(Bg[:, dead[j], :, :]), bf("Bg%d" % dead[j])) for j in range(len(tl_))]
                for j, t in enumerate(tl_):
                    ACT(stg[j][0], x_res[:, t, :], AF.Copy, [bf("xres%d" % t)], [stg[j][1]])
                bs = []
                for j, t in enumerate(tl_):
                    b = pair()
                    bs.append(b)
                    TR([(psb(b + k // 4)[:, (k % 4) * 128:(k % 4 + 1) * 128], stg[j][0][:, k * 128:(k + 1) * 128], identb[:])
                        for k in range(8)], [stg[j][1], bf("identb")], [PSB[b], PSB[b + 1]])
                    ACT(xa[:, :, j * 128:(j + 1) * 128], psb(b)[:, 0:512].rearrange("q (k t) -> q k t", k=4), AF.Copy, [PSB[b]], XA)
                    CP(xb[:, :, j * 128:(j + 1) * 128], psb(b + 1)[:, 0:512].rearrange("q (k t) -> q k t", k=4), [PSB[b + 1]], XB)
                return
            if bt["tiles"] is None:
                p = NS
                ACT(B_tm[0:p, :], xs_res[0:p, :], AF.Copy, [bf("xs_res")], BT3)
                b = pair()
                TR([(psb(b)[:, k * p:(k + 1) * p], B_tm[0:p, k * 128:(k + 1) * 128], identb[0:p, 0:p]) for k in range(8)],
                   BT3 + [bf("identb")], [PSB[b]])
                ACT(xT2[:, 0, :, 0:p], psb(b)[:, 0:8 * p].rearrange("q (k t) -> q k t", k=8), AF.Copy, [PSB[b]],
                    [bf("xTa"), bf("xTb")])
                return
            for j, t in enumerate(bt["tiles"]):
                XR = bf("xres%d" % t)
                ACT(B_tm[:, :], x_res[:, t, :], AF.Copy, [XR], BT3)
                b = pair()
                TR([(psb(b + k // 4)[:, (k % 4) * 128:(k % 4 + 1) * 128], B_tm[:, k * 128:(k + 1) * 128], identb[:])
                    for k in range(8)], BT3 + [bf("identb")], [PSB[b], PSB[b + 1]])
                ACT(xa[:, :, j * 128:(j + 1) * 128], psb(b)[:, 0:512].rearrange("q (k t) -> q k t", k=4), AF.Copy, [PSB[b]], XA)
                CP(xb[:, :, j * 128:(j + 1) * 128], psb(b + 1)[:, 0:512].rearrange("q (k t) -> q k t", k=4), [PSB[b + 1]], XB)

        cnt = [0]

        def c_loop(bt):
            N = bt["N"]
            for c in range(8):
                for br in range(3):
                    i = cnt[0]
                    cnt[0] += 1
                    gb = pair()
                    ob = gb + 1
                    col = br * 1024 + c * 128
                    MM([(ps[:, gb, 0:N], Wg[:, k, col:col + 128], bt["xk"](k), k == 0, k == 7) for k in range(8)],
                       bt["XBUFS"] + [bf("WB%d" % (col // 512))], [PSB[gb]])
                    cl = CH[br]
                    MM([(ps[:, ob, 0:N], Wp[:, f, c * 128:(c + 1) * 128], bt["brhs"](f), f == cl[0], f == cl[-1]) for f in cl],
                       bt["BBUFS"] + [bf("WB6"), bf("WB7")], [PSB[ob]])
                    thv = gsb[:, (i % 2) * 512:(i % 2) * 512 + N]
                    ACT(thv, ps[:, gb, 0:N], AF.Tanh, [PSB[gb], bf("bgh")], [TH[i % 2]], scale=0.5,
                        bias=bgh[:, br * 8 + c:br * 8 + c + 1])
                    a0, a1 = acc[:, 0:N], acc[:, 512:512 + N]
                    if br == 0:
                        STT(a0, thv, 1.0, ps[:, ob, 0:N], ALU.add, ALU.mult, [TH[i % 2], PSB[ob]], [AC[0]])
                    else:
                        STT(a1, thv, 1.0, ps[:, ob, 0:N], ALU.add, ALU.mult, [TH[i % 2], PSB[ob]], [AC[1]])
                        TT(a0, a0, a1, ALU.add, AC, [AC[0]])
                        if br == 2:
                            ACT(bt["mch"](c), a0, AF.Copy, [AC[0]], bt["MB"](c), scale=0.5)

        def out_ln(bt, post):
            if bt["tiles"] is None:
                rows = [(NS, 0, xs_res[0:NS, :], bf("xs_res"), None)]
            else:
                rows = [(128, j, x_res[:, t, :], bf("xres%d" % t), t) for j, t in enumerate(bt["tiles"])]
            for (p, j, xr, XR, t) in rows:
                ob = pair()
                for half in range(2):
                    MM([(ps[0:p, ob + half, :], bt["mch"](k)[:, j * 128:j * 128 + p], Wo[:, k, half * 512:(half + 1) * 512],
                         k == 0, k == 7) for k in range(8)],
                       [b_ for k in range(8) for b_ in bt["MB"](k)] + [bf("WB8"), bf("WB9")], [PSB[ob + half]])
                resid_ln(l, p, xr, XR, ob)
                if l == DEPTH - 1:
                    if t is not None:
                        ti = gi * G + t
                        S.dma("sp", y_p[ti * 128:(ti + 1) * 128, :], xr, XR, reads=[XR], final=True)
                        if gi + 1 < NG:
                            tn = ti + G
                            S.dma("sp", xr, xp[tn * 128:(tn + 1) * 128, :], XR, writes=[XR])
                    else:
                        S.dma("sp", y_s[:, :], xr, XR, reads=[XR], final=True)
                post(t)

        tl = list(range(ntiles))
        batches = [make_batch("t", tl[i:i + 4]) for i in range(0, ntiles, 4)]
        if with_sample:
            batches.append(make_batch("s", None))
        nb = len(batches)
        fired = [False, False]

        def post(bi):
            def f(t):
                if t is not None:
                    for g_ in (post_ln or {}).get(t, []):
                        g_()
                if hooks is not None and bi == nb - 1 and not fired[1]:
                    fired[1] = True
                    for b in (2, 3, 4):
                        hooks[b]()
            return f

        gen_xT(batches[0], "first" if dbg.get("firststage", 1) else None)
        for bi, bt in enumerate(batches):
            c_loop(bt)
            if hooks is not None and bi == nb - 1:
                hooks[0]()
                hooks[1]()
            if bi + 1 < nb:
                gen_xT(batches[bi + 1], bt["tiles"])
            out_ln(bt, post(bi))

    spool_v = spool.rearrange("l (b r) f -> l b r f", r=15)

    def sample_loads_v_steps(l):
        steps = [lambda: S.dma("pool", hist[:], spool[l].rearrange("(c r) f -> r c f", c=2), bf("hist"), writes=[bf("hist")])]
        for j in range(2):
            for kv in range(2):
                i = j * 2 + kv
                steps.append(lambda i=i, j=j, kv=kv: S.dma(
                    "pool", Vda[:, :, kv, j * 64:(j + 1) * 64],
                    cv[l, 0:8, :, kv * 64:(kv + 1) * 64].rearrange("b s d -> s b d"), bf("Vda%d" % i),
                    writes=[bf("Vda%d" % i)] + ([bf("Vda")] if i == 0 else [])))
                steps.append(lambda i=i, j=j, kv=kv: S.dma(
                    "pool", Vdb[:, :, kv, j * 64:(j + 1) * 64],
                    cv[l, 8:16, :, kv * 64:(kv + 1) * 64].rearrange("b s d -> s b d"), bf("Vdb%d" % i),
                    writes=[bf("Vdb%d" % i)] + ([bf("Bg0"), bf("Bg1")] if i == 0 else [])))
        return steps

    def sample_loads_k(l):
        S.dma("sp", o_pool_s[l, :, 0:14, :], spool_v[l, :, 1:15, :], bf("sh_pool"), final=True)
        S.dma("sp", o_k_s[l, :, 0:127, :], ck[l, :, 1:128, :], bf("sh_k"), final=True)
        S.dma("sp", o_v_s[l, :, 0:127, :], cv[l, :, 1:128, :], bf("sh_v"), final=True)
        S.dma("pool", Ks, ck[l].rearrange("b s f -> s b f"), bf("PT0"), writes=[bf("PT0"), bf("PT1")])
        S.dma("sp", sw00[:], sgu_w[l, :, 0, 0].partition_broadcast(NS), bf("sw00"), writes=[bf("sw00")], slow=True)
        S.dma("sp", sb0[:], sgu_b[l, :, 0].partition_broadcast(NS), bf("sb0"), writes=[bf("sb0")], slow=True)

    def sample_p1(l):
        p = NS
        XR = bf("xs_res")
        IDF, IDB = bf("identf"), bf("identb")
        TR([(ps[:, 0, k * p:(k + 1) * p], xs_res[0:p, k * 128:(k + 1) * 128], identf[0:p, 0:p]) for k in range(8)],
           [XR, IDF], [PSB[0]])
        ACT(xT[:, :, 0:p], ps[:, 0, 0:8 * p].rearrange("q (k t) -> q k t", k=8), AF.Copy, [PSB[0]], [bf("xTa"), bf("xTb")])
        for pb, j in [(4, 0), (5, 1), (2, 2), (3, 3), (6, 4)]:
            MM([(ps[0:p, pb, :], xT[:, k, 0:p], W1[:, k, j * 512:(j + 1) * 512], k == 0, k == 7) for k in range(8)],
               [bf("xTa"), bf("xTb"), bf("WB%d" % j)], [PSB[pb]])
        xa_s = tmp[0:p, 0:256]
        ACT(xa_s, ps[0:p, 2, 0:256], AF.Copy, [PSB[2]], [bf("tmp")])
        S.dma("sp", o_pool_s[l, :, 14, :], xa_s, bf("o_xa_s"), reads=[bf("tmp")], final=True)
        ACT(sz[0:p, 0:256], ps[0:p, 2, 256:512], AF.Tanh, [PSB[2]], [bf("sza")], scale=0.5)
        STT(sz[0:p, 0:256], sz[0:p, 0:256], 1.0, ps[0:p, 2, 256:512], ALU.add, ALU.mult, [bf("sza"), PSB[2]], [bf("sza")])
        layer_norm_stats([ps[0:p, 3, 256:512]], [PSB[3]])
        TS(vtmp[0:p, :], ps[0:p, 3, 256:512], mv[0:p, 0:1], mv[0:p, 3:4], ALU.subtract, ALU.mult, [PSB[3], bf("mv")], [bf("acc")])
        TT(vtmp[0:p, :], vtmp[0:p, :], slng[0:p, :], ALU.mult, [bf("acc"), bf("slng")], [bf("acc")])
        TT(vtmp[0:p, :], vtmp[0:p, :], slnb[0:p, :], ALU.add, [bf("acc"), bf("slnb")], [bf("acc")])
        S.dma("sp", o_cv_s[l], vtmp[0:p, :], bf("o_cv_s"), reads=[bf("acc")], final=True)
        ACT(sz[0:p, 256:512], ps[0:p, 5, 256:512], AF.Tanh, [PSB[5]], [bf("szb")], scale=0.5)
        STT(sz[0:p, 256:512], sz[0:p, 256:512], 1.0, ps[0:p, 5, 256:512], ALU.add, ALU.mult, [bf("szb"), PSB[5]], [bf("szb")])
        TT(uz[0:p, :], ps[0:p, 3, 0:256], sz[0:p, 256:512], ALU.mult, [PSB[3], bf("szb")], [bf("gsb")])
        ACT(qk[0:p, 0:512], ps[0:p, 4, :], AF.Copy, [PSB[4]], [bf("acc")])
        ACT(qk[0:p, 512:640], ps[0:p, 5, 0:128], AF.Copy, [PSB[5]], [bf("acc")])
        v_new = tmp[0:p, 256:384]
        ACT(v_new, ps[0:p, 5, 128:256], AF.Copy, [PSB[5]], [bf("tmp")])
        S.dma("sp", o_v_s[l, :, 127, :], v_new, bf("o_vn_s"), reads=[bf("tmp")], final=True)
        CP(vdn[:].rearrange("p kv (j d) -> p kv j d", j=2),
           v_new.rearrange("p (kv d) -> p kv d", kv=2).unsqueeze(2).broadcast_to([p, 2, 2, 64]), [bf("tmp")], [bf("vdn")])
        ACT(sz[0:p, 512:1024], ps[0:p, 6, :], AF.Tanh, [PSB[6]], [bf("szc")], scale=0.5)
        STT(sz[0:p, 512:1024], sz[0:p, 512:1024], 1.0, ps[0:p, 6, :], ALU.add, ALU.mult, [bf("szc"), PSB[6]], [bf("szc")])
        qk4 = qk[0:p, :].rearrange("p (h two f) -> p h two f", h=10, two=2)
        rb4 = rb[0:p, :].rearrange("p (h two f) -> p h two f", h=10, two=2)
        cos_b = cs_s[:, 0:32].unsqueeze(1).unsqueeze(1).broadcast_to([p, 10, 2, 32])
        sin_b = cs_s[:, 32:64].unsqueeze(1).broadcast_to([p, 10, 32])
        t1 = tmp[0:p, 384:704].rearrange("p (h f) -> p h f", h=10)
        t2 = tmp[0:p, 704:1024].rearrange("p (h f) -> p h f", h=10)
        TT(rb4, qk4, cos_b, ALU.mult, [bf("acc"), bf("cs_s")], [bf("gsb")])
        TT(t1, qk4[:, :, 1, :], sin_b, ALU.mult, [bf("acc"), bf("cs_s")], [bf("tmp")])
        TT(t2, qk4[:, :, 0, :], sin_b, ALU.mult, [bf("acc"), bf("cs_s")], [bf("tmp")])
        TT(rb4[:, :, 0, :], rb4[:, :, 0, :], t1, ALU.subtract, [bf("gsb"), bf("tmp")], [bf("gsb")])
        TT(rb4[:, :, 1, :], rb4[:, :, 1, :], t2, ALU.add, [bf("gsb"), bf("tmp")], [bf("gsb")])
        S.dma("sp", o_k_s[l, :, 127, :], rb[0:p, 512:640], bf("o_kn_s"), reads=[bf("gsb")], final=True)
        lst = []
        for g in range(4):
            for c in range(2):
                lst.append((ps[0:p, 0, g * 64:(g + 1) * 64], selb[:, g * 2 + c, :], hist[:, c, g * 64:(g + 1) * 64], c == 0, c == 1))
        MM(lst, [bf("selb"), bf("hist")], [PSB[0]])
        for g, w in enumerate(POOL_WINDOWS):
            STT(pooled_bf[0:p, g * 64:(g + 1) * 64], xa_s[:, g * 64:(g + 1) * 64], 1.0 / w - 1.0,
                ps[0:p, 0, g * 64:(g + 1) * 64], ALU.mult, ALU.add, [bf("tmp"), PSB[0]], [bf("pooled_bf")])
        TR([(psb(1)[:, c * p:(c + 1) * p], pooled_bf[0:p, c * 128:(c + 1) * 128], identb[0:p, 0:p]) for c in range(2)],
           [bf("pooled_bf"), IDB], [PSB[1]])
        ACT(pooledT[:, :, 0:p], psb(1)[:, 0:2 * p].rearrange("q (c t) -> q c t", c=2), AF.Copy, [PSB[1]], [bf("pooledT")])
        MM([(ps[0:p, 0, 256 + c * 128:256 + (c + 1) * 128], pooledT[:, c, 0:p], bdw[:, c, :], True, True) for c in range(2)],
           [bf("pooledT"), bf("bdw")], [PSB[0]])
        TT(B_tm[0:p, 0:256], ps[0:p, 0, 256:512], sz[0:p, 0:256], ALU.mult, [PSB[0], bf("sza")], [bf("Btm_a")])
        vt3 = vtmp[0:p, :].rearrange("p (g c) -> p g c", g=4)
        t3 = tmp[0:p, 384:640].rearrange("p (g c) -> p g c", g=4)
        TT(t3, vt3, sw00[:].unsqueeze(2).broadcast_to([p, 4, 64]), ALU.mult, [bf("acc"), bf("sw00")], [bf("tmp")])
        TT(t3, t3, sb0[:].unsqueeze(2).broadcast_to([p, 4, 64]), ALU.add, [bf("tmp"), bf("sb0")], [bf("tmp")])
        TT(B_tm[0:p, 256:512], tmp[0:p, 384:640], uz[0:p, :], ALU.mult, [bf("tmp"), bf("gsb")], [bf("Btm_b")])
        for hb in range(2):
            TR([(psb(2 + hb)[:, i * 128:(i + 1) * 128], Ks[:, hb * 8 + i, :], identb[:]) for i in range(8)],
               [bf("PT0"), bf("PT1"), IDB], [PSB[2 + hb]])
            if hb == 0:
                ACT(flat(KTs[:, 0:8, :]), psb(2)[:, :], AF.Copy, [PSB[2]], [bf("KTs")])
            else:
                CP(flat(KTs[:, 8:16, :]), psb(3)[:, :], [PSB[3]], [bf("KTs")])
        for kv in range(2):
            CP(Qexp[:, kv * 4:(kv + 1) * 4, kv * 64:(kv + 1) * 64],
               rb[0:p, kv * 256:(kv + 1) * 256].rearrange("p (h d) -> p h d", h=4), [bf("gsb")], [bf("Qexp")])
        CP(kdup[0:p, 0, :], rb[0:p, 512:640], [bf("gsb")], [bf("kdup")])
        TR([(psb(4)[:, h * p:(h + 1) * p], Qexp[:, h, :], identb[0:p, 0:p]) for h in range(8)]
           + [(psb(4)[:, 8 * p:9 * p], kdup[0:p, 0, :], identb[0:p, 0:p])],
           [bf("Qexp"), bf("kdup"), IDB], [PSB[4]])
        ACT(Qblk[:].rearrange("q b h -> q h b"), psb(4)[:, 0:8 * p].rearrange("q (h b) -> q h b", h=8), AF.Copy,
            [PSB[4]], [bf("Qblk")])
        CP(kTn[:], psb(4)[:, 8 * p:9 * p], [PSB[4]], [bf("kTn")])
        MM([(ps[:, 7, b * 8:(b + 1) * 8], KTs[:, b, :], Qblk[:, b, :], True, True) for b in range(p)]
           + [(ps[0:p, 7, 128:256], kTn[:], Qblk[:].rearrange("q b h -> q (b h)"), True, True)],
           [bf("KTs"), bf("Qblk"), bf("kTn")], [PSB[7]])
        ACT(PTs[:], ps[:, 7, 0:128], AF.Exp, [PSB[7]], [bf("PTs")], scale=0.125)
        ACT(Pself_f, ps[0:p, 7, 128:256], AF.Exp, [PSB[7]], [bf("tmp")], scale=0.125)
        TT(Pself[:], Pself_f, dmask[:], ALU.mult, [bf("tmp"), bf("dmask")], [bf("Pself")])
        pvb = (5, 1)
        for kv in range(2):
            lst = [(ps[:, pvb[kv], 0:64].rearrange("q (b i) -> q b i", i=4), vdn[:, kv, :],
                    Pself[:].rearrange("q (b h) -> q b h", h=8)[:, :, kv * 4:(kv + 1) * 4], True, False)]
            for b in range(p):
                vsrc = Vda[:, b, kv, :] if b < 8 else Vdb[:, b - 8, kv, :]
                lst.append((ps[:, pvb[kv], b * 4:b * 4 + 4], vsrc, PTs[:, b * 8 + kv * 4:b * 8 + kv * 4 + 4], False, b == p - 1))
            MM(lst, [bf("Vda"), bf("Bg0"), bf("Bg1")] + [bf("Vda%d" % i) for i in range(4)] + [bf("Vdb%d" % i) for i in range(4)]
               + [bf("PTs"), bf("Pself"), bf("vdn")], [PSB[pvb[kv]]])
        MM([(ps[:, 6, 0:128], onesb[:], PTs[:], True, False), (ps[:, 6, 0:128], onesb[0:p, :], Pself[:], False, True)],
           [bf("onesb"), bf("PTs"), bf("Pself")], [PSB[6]])
        TT(rden_s.rearrange("q (b h) -> q b h", h=8), ps[:, 6, 0:128].rearrange("q (b h) -> q b h", h=8),
           esink_h[:].unsqueeze(1).broadcast_to([128, p, 8]), ALU.add, [PSB[6], bf("esink_h")], [bf("tmp")])
        S.op("dve", lambda e: e.reciprocal(out=rden_s, in_=rden_s), [bf("tmp")], [bf("tmp")])
        for kv in range(2):
            TT(Rn.rearrange("q (b h) -> q b h", h=8)[:, :, kv * 4:(kv + 1) * 4],
               ps[:, pvb[kv], 0:64].rearrange("q (b i) -> q b i", i=4),
               rden_s.rearrange("q (b h) -> q b h", h=8)[:, :, kv * 4:(kv + 1) * 4], ALU.mult,
               [PSB[pvb[kv]], bf("tmp")], [bf("tmp")])
        TR([(ps[:, 7, 256 + c * p:256 + (c + 1) * p], sz[0:p, 512 + c * 128:512 + (c + 1) * 128], identf[0:p, 0:p]) for c in range(4)],
           [bf("szc"), IDF], [PSB[7]])
        for j in range(2):
            STT(Bg[j * 64:(j + 1) * 64, G, 4:8, 0:p],
                Rn.rearrange("q (b c j) -> q c b j", c=4, j=2)[j * 64:(j + 1) * 64, :, :, j], 0.5,
                ps[j * 64:(j + 1) * 64, 7, 256:256 + 4 * p].rearrange("q (c b) -> q c b", c=4), ALU.mult, ALU.mult,
                [bf("tmp"), PSB[7]], [bf("Bg%d" % G)])
        TR([(psb(4)[:, c * p:(c + 1) * p], B_tm[0:p, c * 128:(c + 1) * 128], identb[0:p, 0:p]) for c in range(4)],
           [bf("Btm_a"), bf("Btm_b"), IDB], [PSB[4]])
        ACT(Bg[:, G, 0:4, 0:p], psb(4)[:, 0:4 * p].rearrange("q (c t) -> q c t", c=4), AF.Copy, [PSB[4]], [bf("Bg%d" % G)],
            scale=0.5)

    S.dma("sp", xs_res[:], xs[:, :], bf("xs_res"), writes=[bf("xs_res")])
    for gi in range(dbg.get("ng", NG)):
        for t in range(G):
            ti = gi * G + t
            if gi == 0:
                S.dma("sp", x_res[:, t, :], xp[ti * 128:(ti + 1) * 128, :], bf("xres%d" % t), writes=[bf("xres%d" % t)])
        for l in range(dbg.get("depth", DEPTH)):
            nxt = (gi, l + 1) if l + 1 < DEPTH else ((gi + 1, 0) if gi + 1 < NG else None)
            w2h = {b: (lambda b=b, l=l: w2_block(l, b)) for b in range(5)}
            w1h = None if nxt is None else {b: (lambda b=b, ln=nxt[1]: w1_block(ln, b)) for b in range(5)}
            smp = gi == 0 and dbg.get("sample", 1)
            if dbg.get("stage", 9) >= 1:
                if gi == 0 and l == 0:
                    load_w1(l)
                if smp and l == 0:
                    for f in sample_loads_v_steps(l):
                        f()
                if smp:
                    sample_loads_k(l)
            if dbg.get("stage", 9) >= 2:
                if gi == 0 and l == 0:
                    consts_prefetch(l)
                    consts_compute_p1(l)
                consts_late(l)
            post_ln = None
            if smp and l + 1 < DEPTH:
                nxt_steps = sample_loads_v_steps(l + 1)
                post_ln = {}
                for k_, f in enumerate(nxt_steps):
                    post_ln.setdefault(3 + k_ // 2, []).append(f)
            if dbg.get("stage", 9) >= 3:
                if gi == 0 and dbg.get("sample", 1):
                    phase_barrier()
                    sample_p1(l)
                phase_barrier()
                p1_phase(l, gi, dbg.get("tiles", G), w2h, (lambda i, l=l: w2_upper(l, i)))
            if dbg.get("stage", 9) >= 5:
                phase_barrier()
                consts_p2(l)
                if nxt is not None:
                    consts_prefetch(nxt[1])
                p2_phase(l, gi, dbg.get("tiles", G), gi == 0 and dbg.get("sample", 1), w1h, post_ln)
                if nxt is not None:
                    consts_compute_p1(nxt[1])
    S.emit()
    if S.maxops is not None:
        print("TRACE last ops:", S.trace[-3:], "total", S.nops)
    return nc, stack


_CACHE = {}


def _consts():
    half = 32
    inv = (10000.0 ** (-np.arange(half, dtype=np.float32) / half)).astype(np.float32)
    pos = np.arange(SEQ, dtype=np.float32)
    ang = (pos[:, None] * inv[None, :]).astype(np.float32)
    c_cs = np.concatenate([np.cos(ang), np.sin(ang)], axis=1).astype(np.float32)
    angs = (np.float32(PAST_LEN) * inv).astype(np.float32)
    c_cs_s = np.tile(np.concatenate([np.cos(angs), np.sin(angs)])[None, :], (NS, 1)).astype(np.float32)
    P = np.zeros((3, 4, 128, 128), np.float32)
    for g, w in enumerate(POOL_WINDOWS):
        for t in range(128):
            for s in range(max(0, t - w + 1), t + 1):
                P[0, g, s, t] += 1.0 / min(t + 1, w)
                P[1, g, s, t] += 1.0 / w
            P[0, g, t, t] -= 1.0
            P[1, g, t, t] -= 1.0
            for sp in range(128 + t - w + 1, 128):
                if sp >= 0:
                    P[2, g, sp, t] += 1.0 / w
    s_idx = np.arange(128)[:, None]
    t_idx = np.arange(128)[None, :]
    mask = np.stack([(s_idx <= t_idx), (s_idx >= t_idx)]).astype(np.float32)
    tril = (np.arange(128)[None, :] <= np.arange(128)[:, None]).astype(np.float32)
    sel = np.zeros((4, 2, 120, 16), np.float32)
    for g, w in enumerate(POOL_WINDOWS):
        for c in range(2):
            for bl in range(8):
                for row in range(15):
                    if row >= 15 - (w - 1):
                        sel[g, c, bl * 15 + row, c * 8 + bl] = 1.0 / w
    dm = np.zeros((NS, NS * 8), np.float32)
    for b in range(NS):
        dm[b, b * 8:(b + 1) * 8] = 1.0
    return dict(c_cs=c_cs, c_cs_s=c_cs_s, c_poolP=P, c_mask=mask, c_tril=tril,
                c_ident=np.eye(128, dtype=np.float32), c_sel=sel, c_dmask=dm)


def kernel(x_prompt, x_sample, state_pool, cache_k_win, cache_v_win, w_in, b_gate, pool_w, pool_scale,
           sgu_ln_g, sgu_ln_b, sgu_w, sgu_b, attn_sinks, w_proj_a, w_proj_b, w_proj_c, w_out, ln_g, ln_b):
    f = lambda a: np.ascontiguousarray(np.asarray(a, dtype=np.float32))
    if "nc" not in _CACHE:
        _CACHE["nc"] = build_program()
    nc, _stack = _CACHE["nc"]
    consts = _consts()
    shared = dict(w_in=f(w_in), b_gate=f(b_gate).reshape(DEPTH, 3 * D), pool_w=f(pool_w), pool_scale=f(pool_scale),
                  sgu_ln_g=f(sgu_ln_g), sgu_ln_b=f(sgu_ln_b), sgu_w=f(sgu_w), sgu_b=f(sgu_b), sinks=f(attn_sinks),
                  w_pa=f(w_proj_a), w_pb=f(w_proj_b), w_pc=f(w_proj_c), w_out=f(w_out), ln_g=f(ln_g), ln_b=f(ln_b))
    shared.update(consts)
    xpn, xsn = f(x_prompt), f(x_sample)
    spn, ckn, cvn = f(state_pool), f(cache_k_win), f(cache_v_win)
    in_maps = []
    for c in range(NCORES):
        sl = slice(c * NS, (c + 1) * NS)
        m = dict(shared)
        m["xp"] = xpn[c]
        m["xs"] = xsn[sl, 0, :]
        m["spool"] = np.ascontiguousarray(spn[:, sl].reshape(DEPTH, NS * 15, 256))
        m["ck"] = np.ascontiguousarray(ckn[:, sl].reshape(DEPTH, NS, 128, 128))
        m["cv"] = np.ascontiguousarray(cvn[:, sl].reshape(DEPTH, NS, 128, 128))
        in_maps.append(m)
    res = run_bass_kernel_spmd(nc, in_maps, core_ids=list(range(NCORES)))
    R = res.results
    y_p = np.stack([R[c]["y_p"] for c in range(NCORES)], 0)
    y_s = np.concatenate([R[c]["y_s"] for c in range(NCORES)], 0).reshape(128, 1, D)
    pool_p = np.stack([R[c]["o_pool_p"] for c in range(NCORES)], 1)
    k_p = np.stack([R[c]["o_k_p"] for c in range(NCORES)], 1).reshape(DEPTH, 8, 128, 2, 64)
    v_p = np.stack([R[c]["o_v_p"] for c in range(NCORES)], 1).reshape(DEPTH, 8, 128, 2, 64)
    pool_s = np.concatenate([R[c]["o_pool_s"] for c in range(NCORES)], 1)
    k_s = np.concatenate([R[c]["o_k_s"] for c in range(NCORES)], 1).reshape(DEPTH, 128, 128, 2, 64)
    v_s = np.concatenate([R[c]["o_v_s"] for c in range(NCORES)], 1).reshape(DEPTH, 128, 128, 2, 64)
    cv_s = np.concatenate([R[c]["o_cv_s"] for c in range(NCORES)], 1).reshape(DEPTH, 128, 1, 256)
    return (y_p, y_s, pool_p, k_p, v_p, pool_s, k_s, v_s, cv_s)


if __name__ == "__main__":
    import time
    t0 = time.time()
    nc, _ = build_program()
    print("built in", time.time() - t0)
```

```python
from contextlib import ExitStack
import numpy as np
import concourse.bass as bass
import concourse.mybir as mybir
from concourse.bass_utils import run_bass_kernel_spmd

F32 = mybir.dt.float32
BF16 = mybir.dt.bfloat16
AF = mybir.ActivationFunctionType
ALU = mybir.AluOpType
AX = mybir.AxisListType

NCORES = 8
D = 1024
SEQ = 2048
NT = SEQ // 128
G = 8
NG = NT // G
NS = 16
DEPTH = 2
D_IN = 5632
ALPHA = (2.0 * DEPTH) ** 0.25
LN_EPS = 1e-5
PAST_LEN = 8192
POOL_WINDOWS = (2, 4, 8, 16)


class Buf:
    __slots__ = ("name", "last_w", "readers", "dsem", "dcount", "exclusive")

    def __init__(self, name, exclusive=False):
        self.name = name
        self.exclusive = exclusive
        self.last_w = None
        self.readers = []
        self.dsem = None
        self.dcount = 0


class Sched:
    ENGS = ("pe", "act", "dve", "pool", "sp")

    def __init__(self, nc, stack):
        self.nc = nc
        self.stack = stack
        self.sems = {}
        self.ops = {e: [] for e in self.ENGS}
        self.count = {e: 0 for e in self.ENGS}
        self.waited = {e: {} for e in self.ENGS}
        self.nsem = 0
        for e in ("pe", "act", "dve", "pool"):
            self.sems[e] = self._sem("eng_" + e)
        self.final_events = []
        self.dma_keys = []
        self.nops = 0
        self.maxops = None
        self.trace = []

    def _sem(self, name):
        self.nsem += 1
        return self.stack.enter_context(self.nc.semaphore(name))

    def buf(self, name):
        return Buf(name)

    def _deps(self, eng, reads, writes):
        waits = {}

        def need(ev):
            if ev is None:
                return
            sid, val, weng = ev
            if weng == eng and eng == "pe":
                return
            if self.waited[eng].get(sid, 0) >= val:
                return
            if waits.get(sid, (0,))[0] < val:
                waits[sid] = (val,)

        for b in reads:
            need(b.last_w)
            if b.exclusive:
                for ev in b.readers:
                    if ev[2] != eng:
                        need(ev)
        for b in writes:
            need(b.last_w)
            for ev in b.readers:
                need(ev)
        out = []
        for sid, (val,) in waits.items():
            self.waited[eng][sid] = val
            out.append((sid, val))
        return out

    def _commit(self, ev, reads, writes):
        for b in writes:
            b.last_w = ev
            b.readers = []
        for b in reads:
            if b not in writes:
                b.readers.append(ev)

    def _skip(self):
        self.nops += 1
        if self.maxops is not None and self.nops > self.maxops:
            return True
        if self.maxops is not None:
            import inspect
            fr = inspect.stack()
            self.trace.append((self.nops, [f.lineno for f in fr[2:5]]))
        return False

    def op(self, eng, fn, reads=(), writes=()):
        if self._skip():
            return None
        waits = self._deps(eng, reads, writes)
        self.count[eng] += 1
        ev = (eng, self.count[eng], eng)
        self.waited[eng][eng] = max(self.waited[eng].get(eng, 0), 0)
        self.ops[eng].append((waits, fn, (eng, 1)))
        self._commit(ev, reads, writes)
        return ev

    def dma(self, q, out_ap, in_ap, key, reads=(), writes=(), final=False, slow=False):
        if self._skip():
            return None
        if key.dsem is None:
            key.dsem = "dma_" + key.name
            self.dma_keys.append(key)
            self.sems[key.dsem] = self._sem(key.dsem)
        waits = self._deps(q, reads, writes)
        key.dcount += 16
        ev = (key.dsem, key.dcount, "dma")

        def fn(e, out_ap=out_ap, in_ap=in_ap, slow=slow):
            if slow:
                return e.dma_start(out=out_ap, in_=in_ap, allow_slow_non_contiguous=True)
            return e.dma_start(out=out_ap, in_=in_ap)

        self.ops[q].append((waits, fn, (key.dsem, 16)))
        self._commit(ev, reads, writes)
        if final:
            self.final_events.append(ev)
        return ev

    def emit(self):
        nc = self.nc
        fin = {}
        for key in self.dma_keys:
            fin[key.dsem] = key.dcount
        handles = {"pe": "tensor", "act": "scalar", "dve": "vector", "pool": "gpsimd", "sp": "sync"}
        with nc.Block() as block:
            for eng in self.ENGS:
                ops = self.ops[eng]
                if not ops and eng != "sp":
                    continue

                def body(e, ops=ops, eng=eng):
                    for waits, fn, inc in ops:
                        for sid, val in waits:
                            e.wait_ge(self.sems[sid], val)
                        ins = fn(e)
                        ins.then_inc(self.sems[inc[0]], inc[1])
                    if eng == "sp":
                        for sid, val in fin.items():
                            e.wait_ge(self.sems[sid], val)

                getattr(block, handles[eng])(body)


def build_program(dbg=None):
    dbg = dbg or {}
    nc = bass.Bass("TRN2", target_bir_lowering=False)
    stack = ExitStack()
    S = Sched(nc, stack)
    S.maxops = dbg.get("maxops")

    def din(name, shape):
        return nc.dram_tensor(name, list(shape), F32, kind="ExternalInput").ap()

    def dout(name, shape):
        return nc.dram_tensor(name, list(shape), F32, kind="ExternalOutput").ap()

    xp = din("xp", [SEQ, D])
    xs = din("xs", [NS, D])
    spool = din("spool", [DEPTH, NS * 15, 256])
    ck = din("ck", [DEPTH, NS, 128, 128])
    cv = din("cv", [DEPTH, NS, 128, 128])
    w_in = din("w_in", [DEPTH, D, D_IN])
    b_gate = din("b_gate", [DEPTH, 3 * D])
    pool_w = din("pool_w", [DEPTH, 4, 64, 64])
    pool_scale = din("pool_scale", [DEPTH, 256])
    sgu_ln_g = din("sgu_ln_g", [DEPTH, 256])
    sgu_ln_b = din("sgu_ln_b", [DEPTH, 256])
    sgu_w = din("sgu_w", [DEPTH, 4, 128, 128])
    sgu_b = din("sgu_b", [DEPTH, 4, 128])
    sinks = din("sinks", [DEPTH, 8])
    w_pa = din("w_pa", [DEPTH, 256, D])
    w_pb = din("w_pb", [DEPTH, 256, D])
    w_pc = din("w_pc", [DEPTH, 512, D])
    w_out = din("w_out", [DEPTH, D, D])
    ln_g = din("ln_g", [DEPTH, D])
    ln_b = din("ln_b", [DEPTH, D])
    c_cs = din("c_cs", [SEQ, 64])
    c_cs_s = din("c_cs_s", [NS, 64])
    c_poolP = din("c_poolP", [3, 4, 128, 128])
    c_mask = din("c_mask", [2, 128, 128])
    c_tril = din("c_tril", [128, 128])
    c_ident = din("c_ident", [128, 128])
    c_sel = din("c_sel", [4, 2, 120, 16])
    c_dmask = din("c_dmask", [NS, NS * 8])

    y_p = dout("y_p", [SEQ, D])
    y_s = dout("y_s", [NS, D])
    o_pool_p = dout("o_pool_p", [DEPTH, 15, 256])
    o_k_p = dout("o_k_p", [DEPTH, 128, 128])
    o_v_p = dout("o_v_p", [DEPTH, 128, 128])
    o_pool_s = dout("o_pool_s", [DEPTH, NS, 15, 256])
    o_k_s = dout("o_k_s", [DEPTH, NS, 128, 128])
    o_v_s = dout("o_v_s", [DEPTH, NS, 128, 128])
    o_cv_s = dout("o_cv_s", [DEPTH, NS, 256])

    def sb(name, shape, dt=F32):
        return stack.enter_context(nc.sbuf_tensor(name, list(shape), dt))

    Wbuf = sb("Wbuf", [128, 8, 5120], BF16)
    W1 = Wbuf[:, :, 0:2560]
    Wg = Wbuf[:, :, 0:3072]
    Wp = Wbuf[:, :, 3072:4096]
    Wo = Wbuf[:, :, 4096:5120]
    Bg = sb("Bg", [128, G + 1, 8, 128], BF16)
    x_res = sb("x_res", [128, G, D])
    xs_res = sb("xs_res", [NS, D])
    identf = sb("identf", [128, 128])
    identb = sb("identb", [128, 128], BF16)
    poolP = sb("poolP", [128, 12, 128], BF16)
    maskb = sb("maskb", [128, 2, 128], BF16)
    cs_t2 = sb("cs_t2", [128, 2, 64])
    cs_s = sb("cs_s", [NS, 64])
    ones2 = sb("ones2", [2, 128], BF16)
    onesb = sb("onesb", [128, 128], BF16)
    chalf = sb("chalf", [128, 1])
    selb = sb("selb", [120, 8, 16], BF16)
    dmask = sb("dmask", [NS, NS * 8])
    sguWT = sb("sguWT", [128, 4, 128], BF16)
    sgub = sb("sgub", [128, 256])
    slng = sb("slng", [128, 256])
    slnb = sb("slnb", [128, 256])
    bdw = sb("bdw", [128, 2, 128], BF16)
    bg24 = sb("bg24", [24, 128])
    bgh = sb("bgh", [128, 24])
    lng = sb("lng", [128, D])
    lnb = sb("lnb", [128, D])
    esink = sb("esink", [128, 8])
    esink_h = sb("esink_h", [128, 8])
    xT2 = sb("xT2", [128, 2, 8, 128], BF16)
    xT = xT2[:, 0, :, :]
    sz = sb("sz", [128, 1024])
    pooled_bf = sb("pooled_bf", [128, 256], BF16)
    pooledT = sb("pooledT", [128, 2, 128], BF16)
    B_tm = sb("B_tm", [128, 1024], BF16)
    st6 = sb("st6", [128, 12])
    mv = sb("mv", [128, 8])
    vn_bf = sb("vn_bf", [128, 256], BF16)
    q_bf = sb("q_bf", [128, 512], BF16)
    kdup = sb("kdup", [128, 2, 128], BF16)
    qT = sb("qT", [128, 4, 128], BF16)
    den = sb("den", [128, 8])
    rden = sb("rden", [128, 8])
    gsb = sb("gsb", [128, 1024])
    acc = sb("acc", [128, 1024])
    tmp = sb("tmp", [128, 1024])
    PTm = sb("PTm", [128, 2, 1024], BF16)
    PT = PTm
    m_bf = PTm[:, 0, :]
    mT = PTm[:, 1, :].rearrange("p (k t) -> p k t", k=8)
    qk = acc[:, 0:640]
    vtmp = acc[:, 640:896]
    rb = gsb[:, 0:640]
    uz = gsb[:, 640:896]
    yc = tmp[:, 512:1024]
    xa_f = tmp[:, 0:256]
    vr_f = tmp[:, 256:384]
    sguW = sb("sguW_s", [128, 4, 128])
    tril = sb("tril_s", [128, 128])
    sgb4 = sb("sgb4_s", [4, 128])
    bdwf = sb("bdwf_s", [128, 2, 128])
    pscale = sb("pscale_s", [128, 256])

    KTs = sb("KTs", [128, NS, 128], BF16)
    Vda = sb("Vda", [128, NS // 2, 2, 128], BF16)
    Vdb = Bg[:, 0:2, :, :].rearrange("p a c f -> p (a c) f").rearrange("p (b kv) f -> p b kv f", kv=2)
    Ks = PTm[:, :, :].rearrange("p a (b f) -> p (a b) f", f=128)
    hist = sb("hist", [120, 2, 256], BF16)
    Qexp = sb("Qexp", [NS, 8, 128], BF16)
    Qblk = sb("Qblk", [128, NS, 8], BF16)
    PTs = sb("PTs", [128, 128], BF16)
    Pself = sb("Pself", [NS, 128], BF16)
    Pself_f = tmp[0:NS, 256:384]
    vdn = sb("vdn", [NS, 2, 128], BF16)
    kTn = sb("kTn", [128, NS], BF16)
    sw00 = sb("sw00", [NS, 4])
    sb0 = sb("sb0", [NS, 4])
    rden_s = tmp[:, 0:128]
    Rn = tmp[:, 128:256]

    ps = stack.enter_context(nc.psum_tensor("ps", [128, 8, 512], F32))

    def psb(bank):
        return ps[:, bank, :].bitcast(BF16)

    B = {}

    def bf(name):
        if name not in B:
            B[name] = S.buf(name)
        return B[name]

    PSB = [bf("ps%d" % i) for i in range(8)]
    for _b in PSB:
        _b.exclusive = True

    S.dma("sp", identf[:], c_ident[:, :], bf("identf"), writes=[bf("identf")])
    S.op("dve", lambda e: e.tensor_copy(out=identb[:], in_=identf[:]), reads=[bf("identf")], writes=[bf("identb")])
    S.dma("pool", poolP[:], c_poolP.rearrange("v g s t -> s (v g) t"), bf("poolP"), writes=[bf("poolP")])
    S.dma("pool", maskb[:], c_mask.rearrange("v s t -> s v t"), bf("maskb"), writes=[bf("maskb")])
    S.dma("sp", cs_s[:], c_cs_s[:, :], bf("cs_s"), writes=[bf("cs_s")])
    S.dma("sp", tril[:], c_tril[:, :], bf("stg_tril"), writes=[bf("stg_tril")])
    S.dma("pool", selb[:], c_sel.rearrange("g c r b -> r (g c) b"), bf("selb"), writes=[bf("selb")])
    S.dma("sp", dmask[:], c_dmask[:, :], bf("dmask"), writes=[bf("dmask")])
    S.op("dve", lambda e: e.memset(ones2[:], 1.0), writes=[bf("ones2")])
    S.op("dve", lambda e: e.memset(onesb[:], 1.0), writes=[bf("onesb")])
    S.op("dve", lambda e: e.memset(chalf[:], -0.5), writes=[bf("chalf")])
    S.op("dve", lambda e: e.memset(Qexp[:], 0.0), writes=[bf("Qexp")])

    def WB(*idx):
        return [bf("WB%d" % i) for i in idx]

    def w1_block(l, b):
        wv = w_in[l].rearrange("(k p) n -> p k n", p=128)
        if b == 0:
            S.dma("pool", Wbuf[:, :, 0:512], wv[:, :, 1280:1792], bf("WB0"), writes=WB(0))
        elif b == 1:
            S.dma("pool", Wbuf[:, :, 512:768], wv[:, :, 1792:2048], bf("WB1"), writes=WB(1))
            S.dma("pool", Wbuf[:, :, 768:1024], wv[:, :, 1024:1280], bf("WB1"), writes=WB(1))
        elif b == 2:
            S.dma("pool", Wbuf[:, :, 1024:1536], wv[:, :, 0:512], bf("WB2"), writes=WB(2))
        elif b == 3:
            S.dma("pool", Wbuf[:, :, 1536:2048], wv[:, :, 512:1024], bf("WB3"), writes=WB(3))
        else:
            S.dma("pool", Wbuf[:, :, 2048:2560], wv[:, :, 2048:2560], bf("WB4"), writes=WB(4))

    def w2_block(l, i):
        wv = w_in[l].rearrange("(k p) n -> p k n", p=128)
        S.dma("pool", Wbuf[:, :, i * 512:(i + 1) * 512], wv[:, :, 2560 + i * 512:2560 + (i + 1) * 512],
              bf("WB%d" % i), writes=WB(i))

    def load_w1(l):
        for b in range(5):
            w1_block(l, b)
        return
        wv = w_in[l].rearrange("(k p) n -> p k n", p=128)
        S.dma("pool", Wbuf[:, :, 0:512], wv[:, :, 1280:1792], bf("WB0"), writes=WB(0))
        S.dma("pool", Wbuf[:, :, 512:768], wv[:, :, 1792:2048], bf("WB1"), writes=WB(1))
        S.dma("pool", Wbuf[:, :, 768:1024], wv[:, :, 1024:1280], bf("WB1"), writes=WB(1))
        S.dma("pool", Wbuf[:, :, 1024:1536], wv[:, :, 0:512], bf("WB2"), writes=WB(2))
        S.dma("pool", Wbuf[:, :, 1536:2048], wv[:, :, 512:1024], bf("WB3"), writes=WB(3))
        S.dma("pool", Wbuf[:, :, 2048:2560], wv[:, :, 2048:2560], bf("WB4"), writes=WB(4))

    def w2_upper(l, i):
        wv = w_in[l].rearrange("(k p) n -> p k n", p=128)
        if i == 0:
            S.dma("pool", Wbuf[:, :, 2560:3072], wv[:, :, 2560 + 2560:2560 + 3072], bf("WB5"), writes=WB(5))
        elif i == 1:
            S.dma("pool", Wp[:, 0:2, :], w_pa[l].rearrange("(k p) n -> p k n", p=128), bf("WB6"), writes=WB(6, 7))
        elif i == 2:
            S.dma("pool", Wp[:, 2:4, :], w_pb[l].rearrange("(k p) n -> p k n", p=128), bf("WB6"), writes=WB(6, 7))
        elif i == 3:
            S.dma("pool", Wp[:, 4:8, :], w_pc[l].rearrange("(k p) n -> p k n", p=128), bf("WB6"), writes=WB(6, 7))
        else:
            S.dma("pool", Wo, w_out[l].rearrange("(k p) n -> p k n", p=128), bf("WB8"), writes=WB(8, 9))

    def load_w2(l, part):
        wv = w_in[l].rearrange("(k p) n -> p k n", p=128)
        if part == 0:
            S.dma("pool", Wbuf[:, :, 2560:3072], wv[:, :, 2560 + 2560:2560 + 3072], bf("WB5"), writes=WB(5))
            S.dma("pool", Wp[:, 0:2, :], w_pa[l].rearrange("(k p) n -> p k n", p=128), bf("WB6"), writes=WB(6, 7))
            S.dma("pool", Wp[:, 2:4, :], w_pb[l].rearrange("(k p) n -> p k n", p=128), bf("WB6"), writes=WB(6, 7))
            S.dma("pool", Wp[:, 4:8, :], w_pc[l].rearrange("(k p) n -> p k n", p=128), bf("WB6"), writes=WB(6, 7))
            S.dma("pool", Wo, w_out[l].rearrange("(k p) n -> p k n", p=128), bf("WB8"), writes=WB(8, 9))
            return
        for i in range(5):
            S.dma("pool", Wbuf[:, :, i * 512:(i + 1) * 512], wv[:, :, 2560 + i * 512:2560 + (i + 1) * 512],
                  bf("WB%d" % i), writes=WB(i))
        return
        for i in range(6):
            S.dma("pool", Wbuf[:, :, i * 512:(i + 1) * 512], wv[:, :, 2560 + i * 512:2560 + (i + 1) * 512],
                  bf("WB%d" % i), writes=WB(i))
            if i == 1:
                S.dma("pool", Wp[:, 0:2, :], w_pa[l].rearrange("(k p) n -> p k n", p=128), bf("WB6"), writes=WB(6, 7))
            elif i == 3:
                S.dma("pool", Wp[:, 2:4, :], w_pb[l].rearrange("(k p) n -> p k n", p=128), bf("WB6"), writes=WB(6, 7))
            elif i == 5:
                S.dma("pool", Wp[:, 4:8, :], w_pc[l].rearrange("(k p) n -> p k n", p=128), bf("WB6"), writes=WB(6, 7))
        S.dma("pool", Wo, w_out[l].rearrange("(k p) n -> p k n", p=128), bf("WB8"), writes=WB(8, 9))

    SW, ST, SP_, SG = bf("stg_sguW"), bf("stg_tril"), bf("stg_pscale"), bf("stg_sgb4")
    SBD = [bf("stg_bdw%d" % g) for g in range(4)]

    def consts_prefetch(l):
        S.dma("sp", sguW[:], sgu_w[l].rearrange("g t s -> t g s"), SW, writes=[SW])
        S.dma("sp", sgb4[:], sgu_b[l], SG, writes=[SG])
        S.dma("sp", pscale[:], pool_scale[l].partition_broadcast(128), SP_, writes=[SP_])
        S.op("dve", lambda e: e.memset(bdwf[:], 0.0), writes=SBD)
        for g in range(4):
            c, j = g // 2, g % 2
            S.dma("sp", bdwf[j * 64:(j + 1) * 64, c, j * 64:(j + 1) * 64], pool_w[l, g], SBD[g], writes=[SBD[g]])
        S.dma("sp", slng[:], sgu_ln_g[l].partition_broadcast(128), bf("slng"), writes=[bf("slng")])
        S.dma("sp", slnb[:], sgu_ln_b[l].partition_broadcast(128), bf("slnb"), writes=[bf("slnb")])
        S.dma("sp", esink_h[:], sinks[l].partition_broadcast(128), bf("esink_h"), writes=[bf("esink_h")])
        S.dma("sp", bg24[:], b_gate[l].rearrange("(n p) -> n p", p=128), bf("bg24"), writes=[bf("bg24")])

    def consts_compute_p1(l):
        S.op("dve", lambda e: e.tensor_tensor(out=sguW[:], in0=sguW[:],
                                              in1=tril[:].unsqueeze(1).broadcast_to([128, 4, 128]), op=ALU.mult),
             reads=[ST, SW], writes=[SW])

        def tr(e):
            for g in range(4):
                ins = e.transpose(ps[:, 7, g * 128:(g + 1) * 128], sguW[:, g, :], identf[:])
            return ins
        S.op("pe", tr, reads=[SW, bf("identf")], writes=[PSB[7]])
        S.op("act", lambda e: e.activation(out=sguWT[:].rearrange("p g t -> p (g t)"), in_=ps[:, 7, :], func=AF.Copy),
             reads=[PSB[7]], writes=[bf("sguWT")])
        S.op("pe", lambda e: e.transpose(ps[:, 6, 0:4], sgb4[:], identf[0:4, 0:4]), reads=[SG, bf("identf")],
             writes=[PSB[6]])
        S.op("dve", lambda e: e.tensor_copy(out=sgub[:].rearrange("p (g c) -> p g c", g=4),
                                            in_=ps[:, 6, 0:4].unsqueeze(2).broadcast_to([128, 4, 64])),
             reads=[PSB[6]], writes=[bf("sgub")])
        S.op("dve", lambda e: e.tensor_tensor(out=bdw[:].rearrange("p c d -> p (c d)"),
                                              in0=bdwf[:].rearrange("p c d -> p (c d)"), in1=pscale[:], op=ALU.mult),
             reads=SBD + [SP_], writes=[bf("bdw")])
        S.op("act", lambda e: e.activation(out=esink_h[:], in_=esink_h[:], func=AF.Exp), reads=[bf("esink_h")],
             writes=[bf("esink_h")])
        S.op("dve", lambda e: e.tensor_copy(out=esink[:].rearrange("p (j kv ci) -> p j kv ci", kv=2, j=2),
                                            in_=esink_h[:].rearrange("p (kv ci j) -> p j kv ci", kv=2, ci=2)),
             reads=[bf("esink_h")], writes=[bf("esink")])

    def consts_p2(l):
        S.op("pe", lambda e: e.transpose(ps[:, 5, 0:24], bg24[:], identf[0:24, 0:24]), reads=[bf("bg24"), bf("identf")],
             writes=[PSB[5]])
        S.op("dve", lambda e: e.tensor_scalar(out=bgh[:], in0=ps[:, 5, 0:24], scalar1=0.5, scalar2=None, op0=ALU.mult),
             reads=[PSB[5]], writes=[bf("bgh")])

    def consts_late(l):
        S.dma("sp", lng[:], ln_g[l].partition_broadcast(128), bf("lng"), writes=[bf("lng")])
        S.dma("sp", lnb[:], ln_b[l].partition_broadcast(128), bf("lnb"), writes=[bf("lnb")])

    def ACT(out, in_, func, R, W, **kw):
        return S.op("act", lambda e: e.activation(out=out, in_=in_, func=func, **kw), R, W)

    def TT(out, in0, in1, op, R, W, eng="dve"):
        return S.op(eng, lambda e: e.tensor_tensor(out=out, in0=in0, in1=in1, op=op), R, W)

    def TS(out, in0, s1, s2, op0, op1, R, W):
        if op1 is None:
            return S.op("dve", lambda e: e.tensor_scalar(out=out, in0=in0, scalar1=s1, scalar2=None, op0=op0), R, W)
        return S.op("dve", lambda e: e.tensor_scalar(out=out, in0=in0, scalar1=s1, scalar2=s2, op0=op0, op1=op1), R, W)

    def STT(out, in0, scalar, in1, op0, op1, R, W):
        return S.op("dve", lambda e: e.scalar_tensor_tensor(out=out, in0=in0, scalar=scalar, in1=in1, op0=op0, op1=op1), R, W)

    def CP(out, in_, R, W):
        return S.op("dve", lambda e: e.tensor_copy(out=out, in_=in_), R, W)

    EVENG = dict(cast="act", xT="act", kdup="dve", V="act", xa="act", pooled="dve", pooledT="act", qT="act", kT="act", BT="act")
    EVENG.update(dbg.get("eveng") or {})

    def EV(key, out, in_, R, W, scale=None):
        if EVENG[key] == "act":
            if scale is None:
                return ACT(out, in_, AF.Copy, R, W)
            return ACT(out, in_, AF.Copy, R, W, scale=scale)
        if scale is None:
            return CP(out, in_, R, W)
        return TS(out, in_, scale, None, ALU.mult, None, R, W)

    def MM(lst, R, W):
        def fn(e):
            for (o, a, b, st, sp) in lst:
                ins = e.matmul(o, lhsT=a, rhs=b, start=st, stop=sp)
            return ins
        return S.op("pe", fn, R, W)

    def TR(lst, R, W):
        def fn(e):
            for (o, a, idn) in lst:
                ins = e.transpose(o, a, idn)
            return ins
        return S.op("pe", fn, R, W)

    def flat(ap3):
        return ap3.rearrange("p a b -> p (a b)")

    def layer_norm_stats(src_chunks, R):
        n = len(src_chunks)
        p = src_chunks[0].shape[0]
        for i, c in enumerate(src_chunks):
            S.op("dve", lambda e, c=c, i=i: e.bn_stats(st6[0:p, i * 6:(i + 1) * 6], c), R, [bf("st6")])
        S.op("dve", lambda e: e.bn_aggr(mv[0:p, 0:2], st6[0:p, 0:6 * n]), [bf("st6")], [bf("mv")])
        TS(mv[0:p, 2:3], mv[0:p, 1:2], LN_EPS, None, ALU.add, None, [bf("mv")], [bf("mv")])
        TT(mv[0:p, 3:4], mv[0:p, 2:3], chalf[0:p, :], ALU.pow, [bf("mv"), bf("chalf")], [bf("mv")], eng="pool")
        STT(mv[0:p, 4:5], mv[0:p, 0:1], -1.0, mv[0:p, 3:4], ALU.mult, ALU.mult, [bf("mv")], [bf("mv")])

    xa_ring = sb("xa_ring", [128, 4, 256], BF16)
    kT_ring = sb("kT_ring", [128, 4, 2, 128], BF16)
    V_ring = sb("V_ring", [128, 6, 2, 65], BF16)
    S.op("dve", lambda e: e.memset(V_ring[:, :, :, 64:65], 1.0), writes=[bf("V%d" % i) for i in range(6)])

    def p1_steps(l, gi, t):
        ti = gi * G + t
        slot = l * 2 + ti % 2
        pslot = l * 2 + 1 - ti % 2
        vslot = l * 3 + ti % 3
        vpslot = l * 3 + (ti - 1) % 3
        has_prev = ti > 0
        last = ti == NT - 1
        XR = bf("xres%d" % t)
        IDB = bf("identb")
        XA, XAP = bf("xa%d" % slot), bf("xa%d" % pslot)
        VS, VP = bf("V%d" % vslot), bf("V%d" % vpslot)
        KS, KP = bf("kT%d" % slot), bf("kT%d" % pslot)
        BGB = bf("Bg%d" % t)
        xstage = flat(Bg[:, t, :, :])
        H, T = [], []

        HB = dbg.get("hb") or [1, 2, 3, 1, 2]
        PB_ = dbg.get("poolbank", 0)
        SB_ = dbg.get("sgubank", 6)

        XSL = (ti % 2) if dbg.get("xT2slots", 1) else 0
        xTs = xT2[:, XSL, :, :]
        XBF = [bf("xTa"), bf("xTb")] if XSL == 0 else [bf("xTc")]

        def mm_group(bank, j):
            MM([(ps[:, bank, :], xTs[:, k, :], W1[:, k, j * 512:(j + 1) * 512], k == 0, k == 7) for k in range(8)],
               XBF + [bf("WB%d" % j)], [PSB[bank]])

        def h0a():
            S.dma("sp", cs_t2[:, ti % 2, :], c_cs[ti * 128:(ti + 1) * 128, :], bf("cs_t%d" % (ti % 2)),
                  writes=[bf("cs_t%d" % (ti % 2))])
            EV("cast", xstage, x_res[:, t, :], [XR], [BGB])

        def h0():
            TR([(psb(0)[:, k * 128:(k + 1) * 128], xstage[:, k * 128:(k + 1) * 128], identb[:]) for k in range(8)],
               [BGB, IDB], [PSB[0]])
            EV("xT", xTs.rearrange("p k t -> p (k t)"), psb(0)[:, :], [PSB[0]], XBF)

        cs_t = cs_t2[:, ti % 2, :]
        CSB = bf("cs_t%d" % (ti % 2))
        cos_q = cs_t[:, 0:32].unsqueeze(1).unsqueeze(1).broadcast_to([128, 8, 2, 32])
        sin_q = cs_t[:, 32:64].unsqueeze(1).broadcast_to([128, 8, 32])
        cos_k = cs_t[:, 0:32].unsqueeze(1).unsqueeze(1).broadcast_to([128, 2, 2, 32])
        sin_k = cs_t[:, 32:64].unsqueeze(1).broadcast_to([128, 2, 32])

        def h_q():
            mm_group(HB[0], 0)
            q4 = ps[:, HB[0], :].rearrange("p (h two f) -> p h two f", h=8, two=2)
            a4 = rb[:, 0:512].rearrange("p (h two f) -> p h two f", h=8, two=2)
            qb4 = q_bf[:].rearrange("p (h two f) -> p h two f", h=8, two=2)
            t1 = tmp[:, 0:256].rearrange("p (h f) -> p h f", h=8)
            t2 = tmp[:, 256:512].rearrange("p (h f) -> p h f", h=8)
            TT(a4, q4, cos_q, ALU.mult, [PSB[HB[0]], CSB], [bf("rb")])
            TT(t1, q4[:, :, 1, :], sin_q, ALU.mult, [PSB[HB[0]], CSB], [bf("tmp")])
            TT(t2, q4[:, :, 0, :], sin_q, ALU.mult, [PSB[HB[0]], CSB], [bf("tmp")])
            TT(qb4[:, :, 0, :], a4[:, :, 0, :], t1, ALU.subtract, [bf("rb"), bf("tmp")], [bf("q_bf")])
            TT(qb4[:, :, 1, :], a4[:, :, 1, :], t2, ALU.add, [bf("rb"), bf("tmp")], [bf("q_bf")])

        def h_kvz():
            mm_group(HB[1], 1)
            k4 = ps[:, HB[1], 0:128].rearrange("p (h two f) -> p h two f", h=2, two=2)
            kr4 = rb[:, 512:640].rearrange("p (h two f) -> p h two f", h=2, two=2)
            u1 = acc[:, 0:64].rearrange("p (h f) -> p h f", h=2)
            u2 = acc[:, 64:128].rearrange("p (h f) -> p h f", h=2)
            TT(kr4, k4, cos_k, ALU.mult, [PSB[HB[1]], CSB], [bf("rbk")])
            TT(u1, k4[:, :, 1, :], sin_k, ALU.mult, [PSB[HB[1]], CSB], [bf("acc")])
            TT(u2, k4[:, :, 0, :], sin_k, ALU.mult, [PSB[HB[1]], CSB], [bf("acc")])
            TT(kr4[:, :, 0, :], kr4[:, :, 0, :], u1, ALU.subtract, [bf("rbk"), bf("acc")], [bf("rbk")])
            TT(kr4[:, :, 1, :], kr4[:, :, 1, :], u2, ALU.add, [bf("rbk"), bf("acc")], [bf("rbk")])
            EV("kdup", kdup[:].rearrange("p kv (u d) -> p kv u d", u=2),
               rb[:, 512:640].rearrange("p (kv d) -> p kv d", kv=2).unsqueeze(2).broadcast_to([128, 2, 2, 64]),
               [bf("rbk")], [bf("kdup")])
            if last:
                S.dma("sp", o_k_p[l], rb[:, 512:640], bf("rbout"), reads=[bf("rbk")], final=True)
            EV("V", V_ring[:, vslot, :, 0:64], ps[:, HB[1], 128:256].rearrange("p (kv d) -> p kv d", kv=2), [PSB[HB[1]]], [VS])
            if last:
                ACT(vr_f[:], ps[:, HB[1], 128:256], AF.Copy, [PSB[HB[1]]], [bf("tmp")])
                S.dma("sp", o_v_p[l], vr_f[:], bf("vr_f"), reads=[bf("tmp")], final=True)
            ACT(sz[:, 256:512], ps[:, HB[1], 256:512], AF.Tanh, [PSB[HB[1]]], [bf("szb")], scale=0.5)
            STT(sz[:, 256:512], sz[:, 256:512], 1.0, ps[:, HB[1], 256:512], ALU.add, ALU.mult, [bf("szb"), PSB[HB[1]]], [bf("szb")])

        def h_xaza():
            mm_group(HB[2], 2)
            EV("xa", xa_ring[:, slot, :], ps[:, HB[2], 0:256], [PSB[HB[2]]], [XA])
            if last:
                ACT(xa_f[:], ps[:, HB[2], 0:256], AF.Copy, [PSB[HB[2]]], [bf("tmp")])
                S.dma("sp", o_pool_p[l], xa_f[113:128, :], bf("xa_f"), reads=[bf("tmp")], final=True)
            ACT(sz[:, 0:256], ps[:, HB[2], 256:512], AF.Tanh, [PSB[HB[2]]], [bf("sza")], scale=0.5)
            STT(sz[:, 0:256], sz[:, 0:256], 1.0, ps[:, HB[2], 256:512], ALU.add, ALU.mult, [bf("sza"), PSB[HB[2]]], [bf("sza")])

        def h_uv():
            mm_group(HB[3], 3)
            layer_norm_stats([ps[:, HB[3], 256:512]], [PSB[HB[3]]])
            TS(vtmp[:], ps[:, HB[3], 256:512], mv[:, 0:1], mv[:, 3:4], ALU.subtract, ALU.mult, [PSB[HB[3]], bf("mv")], [bf("vt")])
            TT(vtmp[:], vtmp[:], slng[:], ALU.mult, [bf("vt"), bf("slng")], [bf("vt")])
            TT(vn_bf[:], vtmp[:], slnb[:], ALU.add, [bf("vt"), bf("slnb")], [bf("vn_bf")])
            TT(uz[:], ps[:, HB[3], 0:256], sz[:, 256:512], ALU.mult, [PSB[HB[3]], bf("szb")], [bf("uz")])

        def h_zc():
            mm_group(HB[4], 4)
            ACT(sz[:, 512:1024], ps[:, HB[4], :], AF.Tanh, [PSB[HB[4]]], [bf("szc")], scale=0.5)
            STT(sz[:, 512:1024], sz[:, 512:1024], 1.0, ps[:, HB[4], :], ALU.add, ALU.mult, [bf("szc"), PSB[HB[4]]], [bf("szc")])

        H.extend([h0, h_q, h_kvz, h_xaza, h_uv, h_zc, h0a])

        def t_pool1():
            lst = []
            for g in range(4):
                o = ps[:, PB_, g * 64:(g + 1) * 64]
                lst.append((o, poolP[:, (0 if ti == 0 else 4) + g, :], xa_ring[:, slot, g * 64:(g + 1) * 64], True, not has_prev))
                if has_prev:
                    lst.append((o, poolP[:, 8 + g, :], xa_ring[:, pslot, g * 64:(g + 1) * 64], False, True))
            MM(lst, [XA, bf("poolP")] + ([XAP] if has_prev else []), [PSB[PB_]])
            EV("pooled", pooled_bf[:], ps[:, PB_, 0:256], [PSB[PB_]], [bf("pooled_bf")])

        def t_pool2():
            TR([(psb(5)[:, c * 128:(c + 1) * 128], pooled_bf[:, c * 128:(c + 1) * 128], identb[:]) for c in range(2)],
               [bf("pooled_bf"), IDB], [PSB[5]])
            EV("pooledT", flat(pooledT[:]), psb(5)[:, 0:256], [PSB[5]], [bf("pooledT")])

        def t_pool3():
            MM([(ps[:, PB_, 256 + c * 128:256 + (c + 1) * 128], pooledT[:, c, :], bdw[:, c, :], True, True) for c in range(2)],
               [bf("pooledT"), bf("bdw")], [PSB[PB_]])
            TT(B_tm[:, 0:256], ps[:, PB_, 256:512], sz[:, 0:256], ALU.mult, [PSB[PB_], bf("sza")], [bf("Btm_a")])

        def t_sgu():
            MM([(ps[:, SB_, g * 64:(g + 1) * 64], sguWT[:, g, :], vn_bf[:, g * 64:(g + 1) * 64], True, True) for g in range(4)],
               [bf("sguWT"), bf("vn_bf")], [PSB[SB_]])
            TT(vtmp[:], ps[:, SB_, 0:256], sgub[:], ALU.add, [PSB[SB_], bf("sgub")], [bf("vt")])
            TT(B_tm[:, 256:512], vtmp[:], uz[:], ALU.mult, [bf("vt"), bf("uz")], [bf("Btm_b")])

        def t_qkT():
            TR([(psb(7)[:, c * 128:(c + 1) * 128], q_bf[:, c * 128:(c + 1) * 128], identb[:]) for c in range(4)],
               [bf("q_bf"), IDB], [PSB[7]])
            TR([(psb(5)[:, 256 + kv * 128:256 + (kv + 1) * 128], kdup[:, kv, :], identb[:]) for kv in range(2)],
               [bf("kdup"), IDB], [PSB[5]])
            EV("qT", flat(qT[:]), psb(7)[:, 0:512], [PSB[7]], [bf("qT")])
            EV("kT", flat(kT_ring[:, slot, :, :]), psb(5)[:, 256:512], [PSB[5]], [KS])

        positions = [(0, slot, vslot, KS, VS, 6)] + ([(1, pslot, vpslot, KP, VP, 4)] if has_prev else [])

        def t_scores(pos):
            pi, sl, vs_, KB, VB, base = pos

            def fn():
                lst = []
                for kv in range(2):
                    for j in range(2):
                        lst.append((ps[:, base + j, kv * 256:(kv + 1) * 256], kT_ring[j * 64:(j + 1) * 64, sl, kv, :],
                                    qT[j * 64:(j + 1) * 64, 2 * kv:2 * kv + 2, :], True, True))
                MM(lst, [KB, bf("qT")], [PSB[base], PSB[base + 1]])
                ACT(PT[:, pi, :], ps[:, base:base + 2, :].rearrange("p b c -> p (b c)"), AF.Exp,
                    [PSB[base], PSB[base + 1]], [bf("PT%d" % pi)], scale=0.125)
                TT(PT[:, pi, :].rearrange("p (r t) -> p r t", r=8), PT[:, pi, :].rearrange("p (r t) -> p r t", r=8),
                   maskb[:, pi, :].unsqueeze(1).broadcast_to([128, 8, 128]), ALU.mult,
                   [bf("PT%d" % pi), bf("maskb")], [bf("PT%d" % pi)])
            return fn

        def t_pv():
            lst = []
            npos = len(positions)
            for r in range(8):
                kv = (r // 2) % 2
                o = ps[:, 6 + r // 4, (r % 4) * 65:(r % 4) * 65 + 65]
                for (pi, sl, vs_, KB, VB, base) in positions:
                    lst.append((o, PT[:, pi, r * 128:(r + 1) * 128], V_ring[:, vs_, kv, :], pi == 0, pi == npos - 1))
            MM(lst, [bf("PT0")] + ([bf("PT1")] if has_prev else []) + [p_[4] for p_ in positions], [PSB[6], PSB[7]])
            for b in range(2):
                pv = ps[:, 6 + b, 0:260].rearrange("p (r c) -> p r c", c=65)
                TT(den[:, b * 4:(b + 1) * 4], pv[:, :, 64], esink[:, b * 4:(b + 1) * 4], ALU.add,
                   [PSB[6 + b], bf("esink")], [bf("den")])
            S.op("dve", lambda e: e.reciprocal(out=rden[:], in_=den[:]), [bf("den")], [bf("rden")])
            for b in range(2):
                pv = ps[:, 6 + b, 0:260].rearrange("p (r c) -> p r c", c=65)
                TT(yc.rearrange("p (kv ci j d) -> p j kv ci d", kv=2, ci=2, j=2)[:, b],
                   pv[:, :, 0:64].rearrange("p (kv ci) d -> p kv ci d", kv=2),
                   rden[:, b * 4:(b + 1) * 4].rearrange("p (kv ci) -> p kv ci", kv=2).unsqueeze(3).broadcast_to([128, 2, 2, 64]),
                   ALU.mult, [PSB[6 + b], bf("rden")], [bf("tmp")])
            TT(B_tm[:, 512:1024], yc, sz[:, 512:1024], ALU.mult, [bf("tmp"), bf("szc")], [bf("Btm_c")])

        def t_BT():
            TR([(psb(4)[:, c * 128:(c + 1) * 128], B_tm[:, c * 128:(c + 1) * 128], identb[:]) for c in range(8)],
               [bf("Btm_a"), bf("Btm_b"), bf("Btm_c"), IDB], [PSB[4]])
            EV("BT", flat(Bg[:, t, :, :]), psb(4)[:, :], [PSB[4]], [BGB], scale=0.5)

        T.extend([t_pool1, t_pool2, t_pool3, t_sgu, t_qkT] + [t_scores(p_) for p_ in positions] + [t_pv, t_BT])
        return H, T

    bar_t = sb("bar_t", [128, 1])
    ALIASED = ["acc", "gsb", "tmp", "vt", "uz", "rb", "rbk", "th0", "th1", "ac0", "ac1",
               "sza", "szb", "szc", "stgA", "stgB", "stgC", "stgD"]

    def phase_barrier():
        bl = [bf(n) for n in ALIASED]
        S.op("dve", lambda e: e.memset(bar_t[:], 0.0), bl, bl + [bf("bar_t")])

    P1_ORDER = dbg.get("p1order") or ["qkT", "sgu", "pool1", "h_q", "pool2", "sc0", "sc1", "pool3", "cast", "h_kvz", "h_xaza",
                                      "xT", "pv", "h_uv", "h_zc", "BT"]

    def p1_phase(l, gi, ntiles, hooks=None, upper=None):
        Hs, Ts = zip(*[p1_steps(l, gi, t) for t in range(ntiles)])
        def hooked(h_steps):
            h0, h_q, h_kvz, h_xaza, h_uv, h_zc, h0a = h_steps
            if hooks is None:
                return h_steps
            def h_zc_h():
                h_zc()
                for b in range(5):
                    hooks[b]()
            return (h0, h_q, h_kvz, h_xaza, h_uv, h_zc_h, h0a)

        Hs = list(Hs)
        Hs[ntiles - 1] = hooked(Hs[ntiles - 1])
        if upper is not None:
            for i in range(min(5, ntiles)):
                h = list(Hs[i])
                h[4] = (lambda f=h[4], i=i: (f(), upper(i)))
                Hs[i] = tuple(h)
            for i in range(ntiles, 5):
                pass
        Hs[0][6]()
        for f in Hs[0][:6]:
            f()
        if ntiles > 1:
            Hs[1][6]()
            Hs[1][0]()
        for t in range(ntiles):
            T = list(Ts[t])
            if t + 1 < ntiles:
                h0, h_q, h_kvz, h_xaza, h_uv, h_zc, h0a = Hs[t + 1]
                cast_next = Hs[t + 2][6] if t + 2 < ntiles else (lambda: None)
                xT_next = Hs[t + 2][0] if t + 2 < ntiles else (lambda: None)
                names = ["pool1", "pool2", "pool3", "sgu", "qkT"] + ["sc%d" % i for i in range(len(T) - 7)] + ["pv", "BT"]
                tm = dict(zip(names, T))
                tm.update(h_q=h_q, h_kvz=h_kvz, h_xaza=h_xaza, h_uv=h_uv, h_zc=h_zc, cast=cast_next, xT=xT_next)
                for nme in P1_ORDER:
                    if nme in tm:
                        tm[nme]()
            else:
                names = ["pool1", "pool2", "pool3", "sgu", "qkT"] + ["sc%d" % i for i in range(len(T) - 7)] + ["pv", "BT"]
                tm = dict(zip(names, T))
                if dbg.get("tailorder", 1) == 0:
                    order = [tm["qkT"], tm["sc0"]] + ([tm["sc1"]] if "sc1" in tm else [])
                    order += [tm["pool1"], tm["pool2"], tm["pool3"], tm["sgu"], tm["pv"], tm["BT"]]
                else:
                    order = [tm["qkT"], tm["sgu"], tm["pool1"], tm["pool2"], tm["sc0"]] + ([tm["sc1"]] if "sc1" in tm else [])
                    order += [tm["pool3"], tm["pv"], tm["BT"]]
                for f in order:
                    f()

    def resid_ln(l, p, xr, XR, ob):
        STT(xr, xr, ALPHA, ps[0:p, ob:ob + 2, :].rearrange("p b c -> p (b c)"), ALU.mult, ALU.add,
            [XR, PSB[ob], PSB[ob + 1]], [XR])
        layer_norm_stats([xr[:, 0:512], xr[:, 512:1024]], [XR])
        ACT(xr, xr, AF.Identity, [XR, bf("mv")], [XR], scale=mv[0:p, 3:4], bias=mv[0:p, 4:5])
        TT(xr, xr, lng[0:p, :], ALU.mult, [XR, bf("lng")], [XR])
        TT(xr, xr, lnb[0:p, :], ALU.add, [XR, bf("lnb")], [XR])

    pair_ctr = [0]

    def pair():
        b = 2 * (pair_ctr[0] % 4)
        pair_ctr[0] += 1
        return b

    CH = {0: [0, 1], 1: [2, 3], 2: [4, 5, 6, 7]}

    def p2_phase(l, gi, ntiles, with_sample, hooks=None, post_ln=None):
        szb16 = sz[:].bitcast(BF16)
        tmpb16 = tmp[:].bitcast(BF16)
        xa = xT2[:].rearrange("p a k t -> p (a k t)").rearrange("p (k t) -> p k t", k=4)
        xb = PTm[:].rearrange("p a c -> p (a c)").rearrange("p (k t) -> p k t", k=4)
        XA = [bf("xTa"), bf("xTb"), bf("xTc")]
        XB = [bf("PT0"), bf("PT1")]
        BT3 = [bf("Btm_a"), bf("Btm_b"), bf("Btm_c")]
        TH = [bf("th0"), bf("th1")]
        AC = [bf("ac0"), bf("ac1")]
        SZ3 = [bf("sza"), bf("szb"), bf("szc")]

        def make_batch(kind, tiles):
            if kind == "t":
                nt = len(tiles)
                N = 128 * nt
                t0 = tiles[0]
                d = dict(N=N, tiles=tiles,
                         xk=lambda k: (xa if k < 4 else xb)[:, k % 4, 0:N], XBUFS=XA + XB,
                         brhs=lambda f: Bg[:, t0:t0 + nt, f, :], BBUFS=[bf("Bg%d" % t) for t in tiles],
                         mch=lambda c: (szb16 if c < 4 else tmpb16)[:, (c % 4) * 512:(c % 4) * 512 + N],
                         MB=lambda c: (SZ3 if c < 4 else [bf("tmp")]))
            else:
                N = NS
                d = dict(N=N, tiles=None,
                         xk=lambda k: xT2[:, 0, k, 0:N], XBUFS=[bf("xTa"), bf("xTb")],
                         brhs=lambda f: Bg[:, G, f, 0:N], BBUFS=[bf("Bg%d" % G)],
                         mch=lambda c: mT[:, c, 0:N], MB=lambda c: [bf("PT1")])
            return d

        def gen_xT(bt, dead=None):
            if bt["tiles"] is not None and dead is not None and len(dead) >= len(bt["tiles"]) and dbg.get("deadstage", 1):
                tl_ = bt["tiles"]
                if dead == "first":
                    stg = [(szb16[:, 0:1024], bf("stgA")), (szb16[:, 1024:2048], bf("stgB")),
                           (tmpb16[:, 0:1024], bf("stgC")), (tmpb16[:, 1024:2048], bf("stgD"))][:len(tl_)]
                else:
                    stg = [(flat(Bg[:, dead[j], :, :]), bf("Bg%d" % dead[j])) for j in range(len(tl_))]
                for j, t in enumerate(tl_):
                    ACT(stg[j][0], x_res[:, t, :], AF.Copy, [bf("xres%d" % t)], [stg[j][1]])
                bs = []
                for j, t in enumerate(tl_):
                    b = pair()
                    bs.append(b)
                    TR([(psb(b + k // 4)[:, (k % 4) * 128:(k % 4 + 1) * 128], stg[j][0][:, k * 128:(k + 1) * 128], identb[:])
                        for k in range(8)], [stg[j][1], bf("identb")], [PSB[b], PSB[b + 1]])
                    ACT(xa[:, :, j * 128:(j + 1) * 128], psb(b)[:, 0:512].rearrange("q (k t) -> q k t", k=4), AF.Copy, [PSB[b]], XA)
                    CP(xb[:, :, j * 128:(j + 1) * 128], psb(b + 1)[:, 0:512].rearrange("q (k t) -> q k t", k=4), [PSB[b + 1]], XB)
                return
            if bt["tiles"] is None:
                p = NS
                ACT(B_tm[0:p, :], xs_res[0:p, :], AF.Copy, [bf("xs_res")], BT3)
                b = pair()
                TR([(psb(b)[:, k * p:(k + 1) * p], B_tm[0:p, k * 128:(k + 1) * 128], identb[0:p, 0:p]) for k in range(8)],
                   BT3 + [bf("identb")], [PSB[b]])
                ACT(xT2[:, 0, :, 0:p], psb(b)[:, 0:8 * p].rearrange("q (k t) -> q k t", k=8), AF.Copy, [PSB[b]],
                    [bf("xTa"), bf("xTb")])
                return
            for j, t in enumerate(bt["tiles"]):
                XR = bf("xres%d" % t)
                ACT(B_tm[:, :], x_res[:, t, :], AF.Copy, [XR], BT3)
                b = pair()
                TR([(psb(b + k // 4)[:, (k % 4) * 128:(k % 4 + 1) * 128], B_tm[:, k * 128:(k + 1) * 128], identb[:])
                    for k in range(8)], BT3 + [bf("identb")], [PSB[b], PSB[b + 1]])
                ACT(xa[:, :, j * 128:(j + 1) * 128], psb(b)[:, 0:512].rearrange("q (k t) -> q k t", k=4), AF.Copy, [PSB[b]], XA)
                CP(xb[:, :, j * 128:(j + 1) * 128], psb(b + 1)[:, 0:512].rearrange("q (k t) -> q k t", k=4), [PSB[b + 1]], XB)

        cnt = [0]

        def c_loop(bt):
            N = bt["N"]
            for c in range(8):
                for br in range(3):
                    i = cnt[0]
                    cnt[0] += 1
                    gb = pair()
                    ob = gb + 1
                    col = br * 1024 + c * 128
                    MM([(ps[:, gb, 0:N], Wg[:, k, col:col + 128], bt["xk"](k), k == 0, k == 7) for k in range(8)],
                       bt["XBUFS"] + [bf("WB%d" % (col // 512))], [PSB[gb]])
                    cl = CH[br]
                    MM([(ps[:, ob, 0:N], Wp[:, f, c * 128:(c + 1) * 128], bt["brhs"](f), f == cl[0], f == cl[-1]) for f in cl],
                       bt["BBUFS"] + [bf("WB6"), bf("WB7")], [PSB[ob]])
                    thv = gsb[:, (i % 2) * 512:(i % 2) * 512 + N]
                    ACT(thv, ps[:, gb, 0:N], AF.Tanh, [PSB[gb], bf("bgh")], [TH[i % 2]], scale=0.5,
                        bias=bgh[:, br * 8 + c:br * 8 + c + 1])
                    a0, a1 = acc[:, 0:N], acc[:, 512:512 + N]
                    if br == 0:
                        STT(a0, thv, 1.0, ps[:, ob, 0:N], ALU.add, ALU.mult, [TH[i % 2], PSB[ob]], [AC[0]])
                    else:
                        STT(a1, thv, 1.0, ps[:, ob, 0:N], ALU.add, ALU.mult, [TH[i % 2], PSB[ob]], [AC[1]])
                        TT(a0, a0, a1, ALU.add, AC, [AC[0]])
                        if br == 2:
                            ACT(bt["mch"](c), a0, AF.Copy, [AC[0]], bt["MB"](c), scale=0.5)

        def out_ln(bt, post):
            if bt["tiles"] is None:
                rows = [(NS, 0, xs_res[0:NS, :], bf("xs_res"), None)]
            else:
                rows = [(128, j, x_res[:, t, :], bf("xres%d" % t), t) for j, t in enumerate(bt["tiles"])]
            for (p, j, xr, XR, t) in rows:
                ob = pair()
                for half in range(2):
                    MM([(ps[0:p, ob + half, :], bt["mch"](k)[:, j * 128:j * 128 + p], Wo[:, k, half * 512:(half + 1) * 512],
                         k == 0, k == 7) for k in range(8)],
                       [b_ for k in range(8) for b_ in bt["MB"](k)] + [bf("WB8"), bf("WB9")], [PSB[ob + half]])
                resid_ln(l, p, xr, XR, ob)
                if l == DEPTH - 1:
                    if t is not None:
                        ti = gi * G + t
                        S.dma("sp", y_p[ti * 128:(ti + 1) * 128, :], xr, XR, reads=[XR], final=True)
                        if gi + 1 < NG:
                            tn = ti + G
                            S.dma("sp", xr, xp[tn * 128:(tn + 1) * 128, :], XR, writes=[XR])
                    else:
                        S.dma("sp", y_s[:, :], xr, XR, reads=[XR], final=True)
                post(t)

        tl = list(range(ntiles))
        batches = [make_batch("t", tl[i:i + 4]) for i in range(0, ntiles, 4)]
        if with_sample:
            batches.append(make_batch("s", None))
        nb = len(batches)
        fired = [False, False]

        def post(bi):
            def f(t):
                if t is not None:
                    for g_ in (post_ln or {}).get(t, []):
                        g_()
                if hooks is not None and bi == nb - 1 and not fired[1]:
                    fired[1] = True
                    for b in (2, 3, 4):
                        hooks[b]()
            return f

        gen_xT(batches[0], "first" if dbg.get("firststage", 1) else None)
        for bi, bt in enumerate(batches):
            c_loop(bt)
            if hooks is not None and bi == nb - 1:
                hooks[0]()
                hooks[1]()
            if bi + 1 < nb:
                gen_xT(batches[bi + 1], bt["tiles"])
            out_ln(bt, post(bi))

    spool_v = spool.rearrange("l (b r) f -> l b r f", r=15)

    def sample_loads_v_steps(l):
        steps = [lambda: S.dma("pool", hist[:], spool[l].rearrange("(c r) f -> r c f", c=2), bf("hist"), writes=[bf("hist")])]
        for j in range(2):
            for kv in range(2):
                i = j * 2 + kv
                steps.append(lambda i=i, j=j, kv=kv: S.dma(
                    "pool", Vda[:, :, kv, j * 64:(j + 1) * 64],
                    cv[l, 0:8, :, kv * 64:(kv + 1) * 64].rearrange("b s d -> s b d"), bf("Vda%d" % i),
                    writes=[bf("Vda%d" % i)] + ([bf("Vda")] if i == 0 else [])))
                steps.append(lambda i=i, j=j, kv=kv: S.dma(
                    "pool", Vdb[:, :, kv, j * 64:(j + 1) * 64],
                    cv[l, 8:16, :, kv * 64:(kv + 1) * 64].rearrange("b s d -> s b d"), bf("Vdb%d" % i),
                    writes=[bf("Vdb%d" % i)] + ([bf("Bg0"), bf("Bg1")] if i == 0 else [])))
        return steps

    def sample_loads_k(l):
        S.dma("sp", o_pool_s[l, :, 0:14, :], spool_v[l, :, 1:15, :], bf("sh_pool"), final=True)
        S.dma("sp", o_k_s[l, :, 0:127, :], ck[l, :, 1:128, :], bf("sh_k"), final=True)
        S.dma("sp", o_v_s[l, :, 0:127, :], cv[l, :, 1:128, :], bf("sh_v"), final=True)
        S.dma("pool", Ks, ck[l].rearrange("b s f -> s b f"), bf("PT0"), writes=[bf("PT0"), bf("PT1")])
        S.dma("sp", sw00[:], sgu_w[l, :, 0, 0].partition_broadcast(NS), bf("sw00"), writes=[bf("sw00")], slow=True)
        S.dma("sp", sb0[:], sgu_b[l, :, 0].partition_broadcast(NS), bf("sb0"), writes=[bf("sb0")], slow=True)

    def sample_p1(l):
        p = NS
        XR = bf("xs_res")
        IDF, IDB = bf("identf"), bf("identb")
        TR([(ps[:, 0, k * p:(k + 1) * p], xs_res[0:p, k * 128:(k + 1) * 128], identf[0:p, 0:p]) for k in range(8)],
           [XR, IDF], [PSB[0]])
        ACT(xT[:, :, 0:p], ps[:, 0, 0:8 * p].rearrange("q (k t) -> q k t", k=8), AF.Copy, [PSB[0]], [bf("xTa"), bf("xTb")])
        for pb, j in [(4, 0), (5, 1), (2, 2), (3, 3), (6, 4)]:
            MM([(ps[0:p, pb, :], xT[:, k, 0:p], W1[:, k, j * 512:(j + 1) * 512], k == 0, k == 7) for k in range(8)],
               [bf("xTa"), bf("xTb"), bf("WB%d" % j)], [PSB[pb]])
        xa_s = tmp[0:p, 0:256]
        ACT(xa_s, ps[0:p, 2, 0:256], AF.Copy, [PSB[2]], [bf("tmp")])
        S.dma("sp", o_pool_s[l, :, 14, :], xa_s, bf("o_xa_s"), reads=[bf("tmp")], final=True)
        ACT(sz[0:p, 0:256], ps[0:p, 2, 256:512], AF.Tanh, [PSB[2]], [bf("sza")], scale=0.5)
        STT(sz[0:p, 0:256], sz[0:p, 0:256], 1.0, ps[0:p, 2, 256:512], ALU.add, ALU.mult, [bf("sza"), PSB[2]], [bf("sza")])
        layer_norm_stats([ps[0:p, 3, 256:512]], [PSB[3]])
        TS(vtmp[0:p, :], ps[0:p, 3, 256:512], mv[0:p, 0:1], mv[0:p, 3:4], ALU.subtract, ALU.mult, [PSB[3], bf("mv")], [bf("acc")])
        TT(vtmp[0:p, :], vtmp[0:p, :], slng[0:p, :], ALU.mult, [bf("acc"), bf("slng")], [bf("acc")])
        TT(vtmp[0:p, :], vtmp[0:p, :], slnb[0:p, :], ALU.add, [bf("acc"), bf("slnb")], [bf("acc")])
        S.dma("sp", o_cv_s[l], vtmp[0:p, :], bf("o_cv_s"), reads=[bf("acc")], final=True)
        ACT(sz[0:p, 256:512], ps[0:p, 5, 256:512], AF.Tanh, [PSB[5]], [bf("szb")], scale=0.5)
        STT(sz[0:p, 256:512], sz[0:p, 256:512], 1.0, ps[0:p, 5, 256:512], ALU.add, ALU.mult, [bf("szb"), PSB[5]], [bf("szb")])
        TT(uz[0:p, :], ps[0:p, 3, 0:256], sz[0:p, 256:512], ALU.mult, [PSB[3], bf("szb")], [bf("gsb")])
        ACT(qk[0:p, 0:512], ps[0:p, 4, :], AF.Copy, [PSB[4]], [bf("acc")])
        ACT(qk[0:p, 512:640], ps[0:p, 5, 0:128], AF.Copy, [PSB[5]], [bf("acc")])
        v_new = tmp[0:p, 256:384]
        ACT(v_new, ps[0:p, 5, 128:256], AF.Copy, [PSB[5]], [bf("tmp")])
        S.dma("sp", o_v_s[l, :, 127, :], v_new, bf("o_vn_s"), reads=[bf("tmp")], final=True)
        CP(vdn[:].rearrange("p kv (j d) -> p kv j d", j=2),
           v_new.rearrange("p (kv d) -> p kv d", kv=2).unsqueeze(2).broadcast_to([p, 2, 2, 64]), [bf("tmp")], [bf("vdn")])
        ACT(sz[0:p, 512:1024], ps[0:p, 6, :], AF.Tanh, [PSB[6]], [bf("szc")], scale=0.5)
        STT(sz[0:p, 512:1024], sz[0:p, 512:1024], 1.0, ps[0:p, 6, :], ALU.add, ALU.mult, [bf("szc"), PSB[6]], [bf("szc")])
        qk4 = qk[0:p, :].rearrange("p (h two f) -> p h two f", h=10, two=2)
        rb4 = rb[0:p, :].rearrange("p (h two f) -> p h two f", h=10, two=2)
        cos_b = cs_s[:, 0:32].unsqueeze(1).unsqueeze(1).broadcast_to([p, 10, 2, 32])
        sin_b = cs_s[:, 32:64].unsqueeze(1).broadcast_to([p, 10, 32])
        t1 = tmp[0:p, 384:704].rearrange("p (h f) -> p h f", h=10)
        t2 = tmp[0:p, 704:1024].rearrange("p (h f) -> p h f", h=10)
        TT(rb4, qk4, cos_b, ALU.mult, [bf("acc"), bf("cs_s")], [bf("gsb")])
        TT(t1, qk4[:, :, 1, :], sin_b, ALU.mult, [bf("acc"), bf("cs_s")], [bf("tmp")])
        TT(t2, qk4[:, :, 0, :], sin_b, ALU.mult, [bf("acc"), bf("cs_s")], [bf("tmp")])
        TT(rb4[:, :, 0, :], rb4[:, :, 0, :], t1, ALU.subtract, [bf("gsb"), bf("tmp")], [bf("gsb")])
        TT(rb4[:, :, 1, :], rb4[:, :, 1, :], t2, ALU.add, [bf("gsb"), bf("tmp")], [bf("gsb")])
        S.dma("sp", o_k_s[l, :, 127, :], rb[0:p, 512:640], bf("o_kn_s"), reads=[bf("gsb")], final=True)
        lst = []
        for g in range(4):
            for c in range(2):
                lst.append((ps[0:p, 0, g * 64:(g + 1) * 64], selb[:, g * 2 + c, :], hist[:, c, g * 64:(g + 1) * 64], c == 0, c == 1))
        MM(lst, [bf("selb"), bf("hist")], [PSB[0]])
        for g, w in enumerate(POOL_WINDOWS):
            STT(pooled_bf[0:p, g * 64:(g + 1) * 64], xa_s[:, g * 64:(g + 1) * 64], 1.0 / w - 1.0,
                ps[0:p, 0, g * 64:(g + 1) * 64], ALU.mult, ALU.add, [bf("tmp"), PSB[0]], [bf("pooled_bf")])
        TR([(psb(1)[:, c * p:(c + 1) * p], pooled_bf[0:p, c * 128:(c + 1) * 128], identb[0:p, 0:p]) for c in range(2)],
           [bf("pooled_bf"), IDB], [PSB[1]])
        ACT(pooledT[:, :, 0:p], psb(1)[:, 0:2 * p].rearrange("q (c t) -> q c t", c=2), AF.Copy, [PSB[1]], [bf("pooledT")])
        MM([(ps[0:p, 0, 256 + c * 128:256 + (c + 1) * 128], pooledT[:, c, 0:p], bdw[:, c, :], True, True) for c in range(2)],
           [bf("pooledT"), bf("bdw")], [PSB[0]])
        TT(B_tm[0:p, 0:256], ps[0:p, 0, 256:512], sz[0:p, 0:256], ALU.mult, [PSB[0], bf("sza")], [bf("Btm_a")])
        vt3 = vtmp[0:p, :].rearrange("p (g c) -> p g c", g=4)
        t3 = tmp[0:p, 384:640].rearrange("p (g c) -> p g c", g=4)
        TT(t3, vt3, sw00[:].unsqueeze(2).broadcast_to([p, 4, 64]), ALU.mult, [bf("acc"), bf("sw00")], [bf("tmp")])
        TT(t3, t3, sb0[:].unsqueeze(2).broadcast_to([p, 4, 64]), ALU.add, [bf("tmp"), bf("sb0")], [bf("tmp")])
        TT(B_tm[0:p, 256:512], tmp[0:p, 384:640], uz[0:p, :], ALU.mult, [bf("tmp"), bf("gsb")], [bf("Btm_b")])
        for hb in range(2):
            TR([(psb(2 + hb)[:, i * 128:(i + 1) * 128], Ks[:, hb * 8 + i, :], identb[:]) for i in range(8)],
               [bf("PT0"), bf("PT1"), IDB], [PSB[2 + hb]])
            if hb == 0:
                ACT(flat(KTs[:, 0:8, :]), psb(2)[:, :], AF.Copy, [PSB[2]], [bf("KTs")])
            else:
                CP(flat(KTs[:, 8:16, :]), psb(3)[:, :], [PSB[3]], [bf("KTs")])
        for kv in range(2):
            CP(Qexp[:, kv * 4:(kv + 1) * 4, kv * 64:(kv + 1) * 64],
               rb[0:p, kv * 256:(kv + 1) * 256].rearrange("p (h d) -> p h d", h=4), [bf("gsb")], [bf("Qexp")])
        CP(kdup[0:p, 0, :], rb[0:p, 512:640], [bf("gsb")], [bf("kdup")])
        TR([(psb(4)[:, h * p:(h + 1) * p], Qexp[:, h, :], identb[0:p, 0:p]) for h in range(8)]
           + [(psb(4)[:, 8 * p:9 * p], kdup[0:p, 0, :], identb[0:p, 0:p])],
           [bf("Qexp"), bf("kdup"), IDB], [PSB[4]])
        ACT(Qblk[:].rearrange("q b h -> q h b"), psb(4)[:, 0:8 * p].rearrange("q (h b) -> q h b", h=8), AF.Copy,
            [PSB[4]], [bf("Qblk")])
        CP(kTn[:], psb(4)[:, 8 * p:9 * p], [PSB[4]], [bf("kTn")])
        MM([(ps[:, 7, b * 8:(b + 1) * 8], KTs[:, b, :], Qblk[:, b, :], True, True) for b in range(p)]
           + [(ps[0:p, 7, 128:256], kTn[:], Qblk[:].rearrange("q b h -> q (b h)"), True, True)],
           [bf("KTs"), bf("Qblk"), bf("kTn")], [PSB[7]])
        ACT(PTs[:], ps[:, 7, 0:128], AF.Exp, [PSB[7]], [bf("PTs")], scale=0.125)
        ACT(Pself_f, ps[0:p, 7, 128:256], AF.Exp, [PSB[7]], [bf("tmp")], scale=0.125)
        TT(Pself[:], Pself_f, dmask[:], ALU.mult, [bf("tmp"), bf("dmask")], [bf("Pself")])
        pvb = (5, 1)
        for kv in range(2):
            lst = [(ps[:, pvb[kv], 0:64].rearrange("q (b i) -> q b i", i=4), vdn[:, kv, :],
                    Pself[:].rearrange("q (b h) -> q b h", h=8)[:, :, kv * 4:(kv + 1) * 4], True, False)]
            for b in range(p):
                vsrc = Vda[:, b, kv, :] if b < 8 else Vdb[:, b - 8, kv, :]
                lst.append((ps[:, pvb[kv], b * 4:b * 4 + 4], vsrc, PTs[:, b * 8 + kv * 4:b * 8 + kv * 4 + 4], False, b == p - 1))
            MM(lst, [bf("Vda"), bf("Bg0"), bf("Bg1")] + [bf("Vda%d" % i) for i in range(4)] + [bf("Vdb%d" % i) for i in range(4)]
               + [bf("PTs"), bf("Pself"), bf("vdn")], [PSB[pvb[kv]]])
        MM([(ps[:, 6, 0:128], onesb[:], PTs[:], True, False), (ps[:, 6, 0:128], onesb[0:p, :], Pself[:], False, True)],
           [bf("onesb"), bf("PTs"), bf("Pself")], [PSB[6]])
        TT(rden_s.rearrange("q (b h) -> q b h", h=8), ps[:, 6, 0:128].rearrange("q (b h) -> q b h", h=8),
           esink_h[:].unsqueeze(1).broadcast_to([128, p, 8]), ALU.add, [PSB[6], bf("esink_h")], [bf("tmp")])
        S.op("dve", lambda e: e.reciprocal(out=rden_s, in_=rden_s), [bf("tmp")], [bf("tmp")])
        for kv in range(2):
            TT(Rn.rearrange("q (b h) -> q b h", h=8)[:, :, kv * 4:(kv + 1) * 4],
               ps[:, pvb[kv], 0:64].rearrange("q (b i) -> q b i", i=4),
               rden_s.rearrange("q (b h) -> q b h", h=8)[:, :, kv * 4:(kv + 1) * 4], ALU.mult,
               [PSB[pvb[kv]], bf("tmp")], [bf("tmp")])
        TR([(ps[:, 7, 256 + c * p:256 + (c + 1) * p], sz[0:p, 512 + c * 128:512 + (c + 1) * 128], identf[0:p, 0:p]) for c in range(4)],
           [bf("szc"), IDF], [PSB[7]])
        for j in range(2):
            STT(Bg[j * 64:(j + 1) * 64, G, 4:8, 0:p],
                Rn.rearrange("q (b c j) -> q c b j", c=4, j=2)[j * 64:(j + 1) * 64, :, :, j], 0.5,
                ps[j * 64:(j + 1) * 64, 7, 256:256 + 4 * p].rearrange("q (c b) -> q c b", c=4), ALU.mult, ALU.mult,
                [bf("tmp"), PSB[7]], [bf("Bg%d" % G)])
        TR([(psb(4)[:, c * p:(c + 1) * p], B_tm[0:p, c * 128:(c + 1) * 128], identb[0:p, 0:p]) for c in range(4)],
           [bf("Btm_a"), bf("Btm_b"), IDB], [PSB[4]])
        ACT(Bg[:, G, 0:4, 0:p], psb(4)[:, 0:4 * p].rearrange("q (c t) -> q c t", c=4), AF.Copy, [PSB[4]], [bf("Bg%d" % G)],
            scale=0.5)

    S.dma("sp", xs_res[:], xs[:, :], bf("xs_res"), writes=[bf("xs_res")])
    for gi in range(dbg.get("ng", NG)):
        for t in range(G):
            ti = gi * G + t
            if gi == 0:
                S.dma("sp", x_res[:, t, :], xp[ti * 128:(ti + 1) * 128, :], bf("xres%d" % t), writes=[bf("xres%d" % t)])
        for l in range(dbg.get("depth", DEPTH)):
            nxt = (gi, l + 1) if l + 1 < DEPTH else ((gi + 1, 0) if gi + 1 < NG else None)
            w2h = {b: (lambda b=b, l=l: w2_block(l, b)) for b in range(5)}
            w1h = None if nxt is None else {b: (lambda b=b, ln=nxt[1]: w1_block(ln, b)) for b in range(5)}
            smp = gi == 0 and dbg.get("sample", 1)
            if dbg.get("stage", 9) >= 1:
                if gi == 0 and l == 0:
                    load_w1(l)
                if smp and l == 0:
                    for f in sample_loads_v_steps(l):
                        f()
                if smp:
                    sample_loads_k(l)
            if dbg.get("stage", 9) >= 2:
                if gi == 0 and l == 0:
                    consts_prefetch(l)
                    consts_compute_p1(l)
                consts_late(l)
            post_ln = None
            if smp and l + 1 < DEPTH:
                nxt_steps = sample_loads_v_steps(l + 1)
                post_ln = {}
                for k_, f in enumerate(nxt_steps):
                    post_ln.setdefault(3 + k_ // 2, []).append(f)
            if dbg.get("stage", 9) >= 3:
                if gi == 0 and dbg.get("sample", 1):
                    phase_barrier()
                    sample_p1(l)
                phase_barrier()
                p1_phase(l, gi, dbg.get("tiles", G), w2h, (lambda i, l=l: w2_upper(l, i)))
            if dbg.get("stage", 9) >= 5:
                phase_barrier()
                consts_p2(l)
                if nxt is not None:
                    consts_prefetch(nxt[1])
                p2_phase(l, gi, dbg.get("tiles", G), gi == 0 and dbg.get("sample", 1), w1h, post_ln)
                if nxt is not None:
                    consts_compute_p1(nxt[1])
    S.emit()
    if S.maxops is not None:
        print("TRACE last ops:", S.trace[-3:], "total", S.nops)
    return nc, stack


_CACHE = {}


def _consts():
    half = 32
    inv = (10000.0 ** (-np.arange(half, dtype=np.float32) / half)).astype(np.float32)
    pos = np.arange(SEQ, dtype=np.float32)
    ang = (pos[:, None] * inv[None, :]).astype(np.float32)
    c_cs = np.concatenate([np.cos(ang), np.sin(ang)], axis=1).astype(np.float32)
    angs = (np.float32(PAST_LEN) * inv).astype(np.float32)
    c_cs_s = np.tile(np.concatenate([np.cos(angs), np.sin(angs)])[None, :], (NS, 1)).astype(np.float32)
    P = np.zeros((3, 4, 128, 128), np.float32)
    for g, w in enumerate(POOL_WINDOWS):
        for t in range(128):
            for s in range(max(0, t - w + 1), t + 1):
                P[0, g, s, t] += 1.0 / min(t + 1, w)
                P[1, g, s, t] += 1.0 / w
            P[0, g, t, t] -= 1.0
            P[1, g, t, t] -= 1.0
            for sp in range(128 + t - w + 1, 128):
                if sp >= 0:
                    P[2, g, sp, t] += 1.0 / w
    s_idx = np.arange(128)[:, None]
    t_idx = np.arange(128)[None, :]
    mask = np.stack([(s_idx <= t_idx), (s_idx >= t_idx)]).astype(np.float32)
    tril = (np.arange(128)[None, :] <= np.arange(128)[:, None]).astype(np.float32)
    sel = np.zeros((4, 2, 120, 16), np.float32)
    for g, w in enumerate(POOL_WINDOWS):
        for c in range(2):
            for bl in range(8):
                for row in range(15):
                    if row >= 15 - (w - 1):
                        sel[g, c, bl * 15 + row, c * 8 + bl] = 1.0 / w
    dm = np.zeros((NS, NS * 8), np.float32)
    for b in range(NS):
        dm[b, b * 8:(b + 1) * 8] = 1.0
    return dict(c_cs=c_cs, c_cs_s=c_cs_s, c_poolP=P, c_mask=mask, c_tril=tril,
                c_ident=np.eye(128, dtype=np.float32), c_sel=sel, c_dmask=dm)


def kernel(x_prompt, x_sample, state_pool, cache_k_win, cache_v_win, w_in, b_gate, pool_w, pool_scale,
           sgu_ln_g, sgu_ln_b, sgu_w, sgu_b, attn_sinks, w_proj_a, w_proj_b, w_proj_c, w_out, ln_g, ln_b):
    f = lambda a: np.ascontiguousarray(np.asarray(a, dtype=np.float32))
    if "nc" not in _CACHE:
        _CACHE["nc"] = build_program()
    nc, _stack = _CACHE["nc"]
    consts = _consts()
    shared = dict(w_in=f(w_in), b_gate=f(b_gate).reshape(DEPTH, 3 * D), pool_w=f(pool_w), pool_scale=f(pool_scale),
                  sgu_ln_g=f(sgu_ln_g), sgu_ln_b=f(sgu_ln_b), sgu_w=f(sgu_w), sgu_b=f(sgu_b), sinks=f(attn_sinks),
                  w_pa=f(w_proj_a), w_pb=f(w_proj_b), w_pc=f(w_proj_c), w_out=f(w_out), ln_g=f(ln_g), ln_b=f(ln_b))
    shared.update(consts)
    xpn, xsn = f(x_prompt), f(x_sample)
    spn, ckn, cvn = f(state_pool), f(cache_k_win), f(cache_v_win)
    in_maps = []
    for c in range(NCORES):
        sl = slice(c * NS, (c + 1) * NS)
        m = dict(shared)
        m["xp"] = xpn[c]
        m["xs"] = xsn[sl, 0, :]
        m["spool"] = np.ascontiguousarray(spn[:, sl].reshape(DEPTH, NS * 15, 256))
        m["ck"] = np.ascontiguousarray(ckn[:, sl].reshape(DEPTH, NS, 128, 128))
        m["cv"] = np.ascontiguousarray(cvn[:, sl].reshape(DEPTH, NS, 128, 128))
        in_maps.append(m)
    res = run_bass_kernel_spmd(nc, in_maps, core_ids=list(range(NCORES)))
    R = res.results
    y_p = np.stack([R[c]["y_p"] for c in range(NCORES)], 0)
    y_s = np.concatenate([R[c]["y_s"] for c in range(NCORES)], 0).reshape(128, 1, D)
    pool_p = np.stack([R[c]["o_pool_p"] for c in range(NCORES)], 1)
    k_p = np.stack([R[c]["o_k_p"] for c in range(NCORES)], 1).reshape(DEPTH, 8, 128, 2, 64)
    v_p = np.stack([R[c]["o_v_p"] for c in range(NCORES)], 1).reshape(DEPTH, 8, 128, 2, 64)
    pool_s = np.concatenate([R[c]["o_pool_s"] for c in range(NCORES)], 1)
    k_s = np.concatenate([R[c]["o_k_s"] for c in range(NCORES)], 1).reshape(DEPTH, 128, 128, 2, 64)
    v_s = np.concatenate([R[c]["o_v_s"] for c in range(NCORES)], 1).reshape(DEPTH, 128, 128, 2, 64)
    cv_s = np.concatenate([R[c]["o_cv_s"] for c in range(NCORES)], 1).reshape(DEPTH, 128, 1, 256)
    return (y_p, y_s, pool_p, k_p, v_p, pool_s, k_s, v_s, cv_s)


if __name__ == "__main__":
    import time
    t0 = time.time()
    nc, _ = build_program()
    print("built in", time.time() - t0)
```
